# Optimizing a Trainium2 kernel written in Bass

```python
import math
import jax, jax.numpy as jnp
from jax import lax
import numpy as np

D_MODEL = 1024
BATCH = 8
SEQ = 4096
DEPTH = 2

RW_HEADS = 8
RW_HEAD_DIM = 64
RW_WIDTH = RW_HEADS * RW_HEAD_DIM
RW_DECAY_LORA = 64
RW_ICLR_LORA = 64
RW_GATE_LORA = 128
RW_COLS = 3 * RW_WIDTH + RW_DECAY_LORA + RW_ICLR_LORA + RW_GATE_LORA
RW_GN_EPS = 64e-5
DA_HEADS = 4
DA_HEAD_DIM = 64
DA_WIDTH = DA_HEADS * 2 * DA_HEAD_DIM
DA_COLS = 3 * DA_WIDTH
DA_BLOCK = 128
GD_HEADS = 4
GD_HEAD_DIM = 128
GD_WIDTH = GD_HEADS * GD_HEAD_DIM
GD_CONV = 4
GD_CHUNK = 64
GD_COLS = 3 * GD_WIDTH + 2 * GD_HEADS + GD_WIDTH
N_BRANCH = 3
BRANCH_WIDTH = 512
GATE_COLS = N_BRANCH * D_MODEL
IN_COLS = RW_COLS + DA_COLS + GD_COLS + GATE_COLS
D_FF = 2816
ALPHA = (2.0 * DEPTH) ** 0.25
BETA = (8.0 * DEPTH) ** -0.25

kernel_name = "hybrid_rwkv7_diffattn_gdn_macaron_deepnorm"

F32 = jnp.float32


def _layernorm(x, g, b, eps=1e-5):
    xf = x.astype(F32)
    mu = xf.mean(-1, keepdims=True)
    var = jnp.square(xf - mu).mean(-1, keepdims=True)
    return ((xf - mu) * lax.rsqrt(var + eps) * g + b).astype(x.dtype)


def _rmsnorm(x, g, eps):
    xf = x.astype(F32)
    return xf * lax.rsqrt(jnp.mean(jnp.square(xf), -1, keepdims=True) + eps) * g


def _l2norm(x, eps=1e-6):
    xf = x.astype(F32)
    return xf * lax.rsqrt(jnp.sum(jnp.square(xf), -1, keepdims=True) + eps)


def _swiglu(x, w_in, w_out):
    gate, up = jnp.split(x @ w_in, 2, axis=-1)
    return (jax.nn.silu(gate) * up) @ w_out


def _token_shift(z, mu):
    prev = jnp.pad(z, ((0, 0), (1, 0), (0, 0)))[:, :-1]
    return z + mu * (prev - z)


def _causal_dwconv(z, w):
    K = w.shape[0]
    L = z.shape[1]
    zp = jnp.pad(z, ((0, 0), (K - 1, 0), (0, 0)))
    out = zp[:, 0:L] * w[0]
    for j in range(1, K):
        out = out + zp[:, j:j + L] * w[j]
    return out


def _rwkv7(hr, mu, w0, w_up, a0, a_up, g_up, k_k, k_a, r_k, ln_g, ln_b):
    B, L, _ = hr.shape
    H, N, C = RW_HEADS, RW_HEAD_DIM, RW_WIDTH
    hr = _token_shift(hr, mu)
    r = hr[..., 0:C]
    k = hr[..., C:2 * C]
    v = hr[..., 2 * C:3 * C]
    wd = hr[..., 3 * C:3 * C + RW_DECAY_LORA]
    ad = hr[..., 3 * C + RW_DECAY_LORA:3 * C + RW_DECAY_LORA + RW_ICLR_LORA]
    gd = hr[..., 3 * C + RW_DECAY_LORA + RW_ICLR_LORA:]
    heads = lambda t: t.reshape(B, L, H, N)
    logw = -math.exp(-0.5) * jax.nn.sigmoid((w0 + jnp.tanh(wd) @ w_up).astype(F32))
    a = jax.nn.sigmoid((a0 + ad @ a_up).astype(F32))
    g = jax.nn.sigmoid(gd) @ g_up
    r, k, v = heads(r.astype(F32)), heads(k.astype(F32)), heads(v.astype(F32))
    a, w = heads(a), jnp.exp(heads(logw))
    kk = _l2norm(k * k_k.reshape(H, N).astype(F32))
    k = k * (1.0 + (a - 1.0) * k_a.reshape(H, N).astype(F32))

    def step(S, inp):
        r_t, w_t, k_t, v_t, kk_t, a_t = inp
        sk = jnp.einsum('bhvk,bhk->bhv', S, kk_t)
        S = (S * w_t[:, :, None, :] - sk[..., None] * (kk_t * a_t)[:, :, None, :]
             + v_t[..., None] * k_t[:, :, None, :])
        return S, jnp.einsum('bhvk,bhk->bhv', S, r_t)

    tm = lambda t: jnp.moveaxis(t, 1, 0)
    S0 = jnp.zeros((B, H, N, N), F32)
    _, y = lax.scan(step, S0, (tm(r), tm(w), tm(k), tm(v), tm(kk), tm(a)))
    y = jnp.moveaxis(y, 0, 1)
    ym = y.mean(-1, keepdims=True)
    yv = jnp.square(y - ym).mean(-1, keepdims=True)
    y = ((y - ym) * lax.rsqrt(yv + RW_GN_EPS)).reshape(B, L, C) * ln_g + ln_b
    bonus = jnp.sum(r * k * r_k.astype(F32), -1, keepdims=True) * v
    y = (y + bonus.reshape(B, L, C)) * g
    return y.astype(hr.dtype)


def _diff_attention(hd, lam_q1, lam_k1, lam_q2, lam_k2, norm_g, lam_init):
    B, L, _ = hd.shape
    H, d = DA_HEADS, DA_HEAD_DIM
    q = hd[..., 0:DA_WIDTH].astype(F32).reshape(B, L, H, 2, d).transpose(3, 0, 2, 1, 4)
    k = hd[..., DA_WIDTH:2 * DA_WIDTH].astype(F32).reshape(B, L, H, 2, d).transpose(3, 0, 2, 1, 4)
    v = hd[..., 2 * DA_WIDTH:].astype(F32).reshape(B, L, H, 2 * d).transpose(0, 2, 1, 3)
    lam = (jnp.exp(jnp.sum(lam_q1.astype(F32) * lam_k1.astype(F32)))
           - jnp.exp(jnp.sum(lam_q2.astype(F32) * lam_k2.astype(F32))) + lam_init)
    slopes = jnp.exp2(-8.0 * jnp.arange(1, H + 1, dtype=F32) / H)
    scale = d ** -0.5
    nb = L // DA_BLOCK
    qb = jnp.moveaxis(q.reshape(2, B, H, nb, DA_BLOCK, d), 3, 0)
    kpos = jnp.arange(L)

    def block(args):
        q_blk, i = args
        qpos = i * DA_BLOCK + jnp.arange(DA_BLOCK)
        dist = (qpos[:, None] - kpos[None, :]).astype(F32)
        bias = jnp.where(dist >= 0, -slopes[:, None, None] * dist, -jnp.inf)
        s = jnp.einsum('mbhqd,mbhkd->mbhqk', q_blk, k) * scale + bias
        p = jax.nn.softmax(s, axis=-1)
        p = p[0] - lam * p[1]
        return jnp.einsum('bhqk,bhkv->bhqv', p, v)

    o = lax.map(block, (qb, jnp.arange(nb)))
    o = jnp.moveaxis(o, 0, 2).reshape(B, H, L, 2 * d)
    o = _rmsnorm(o, norm_g, 1e-5) * (1.0 - lam_init)
    return o.transpose(0, 2, 1, 3).reshape(B, L, DA_WIDTH).astype(hd.dtype)


def _chunk_gated_delta(q, k, v, g, beta):
    B, H, L, dk = q.shape
    dv = v.shape[-1]
    C = GD_CHUNK
    n = L // C
    q = q.reshape(B, H, n, C, dk)
    k = k.reshape(B, H, n, C, dk)
    v = v.reshape(B, H, n, C, dv)
    g = g.reshape(B, H, n, C)
    beta = beta.reshape(B, H, n, C)
    G = jnp.cumsum(g, axis=-1)
    idx = jnp.arange(C)
    incl = idx[:, None] >= idx[None, :]
    strict = idx[:, None] > idx[None, :]
    decay = jnp.exp(jnp.where(incl, G[..., :, None] - G[..., None, :], -jnp.inf))
    kb = k * beta[..., None]
    Lm = jnp.where(strict, jnp.einsum('bhncd,bhnsd->bhncs', kb, k) * decay, 0.0)
    A = Lm + jnp.eye(C, dtype=Lm.dtype)
    U = lax.linalg.triangular_solve(A, v * beta[..., None], left_side=True, lower=True, unit_diagonal=True)
    W = lax.linalg.triangular_solve(A, kb * jnp.exp(G)[..., None], left_side=True, lower=True, unit_diagonal=True)
    intra = jnp.einsum('bhncd,bhnsd->bhncs', q, k) * decay
    G_last = G[..., -1:]
    qg = q * jnp.exp(G)[..., None]
    kd = k * jnp.exp(G_last - G)[..., None]
    gl = jnp.exp(G_last[..., 0])

    def step(S, inp):
        qg_c, kd_c, u_c, w_c, intra_c, gl_c = inp
        v_new = u_c - jnp.einsum('bhcd,bhdv->bhcv', w_c, S)
        o = jnp.einsum('bhcd,bhdv->bhcv', qg_c, S) + jnp.einsum('bhcs,bhsv->bhcv', intra_c, v_new)
        S = S * gl_c[..., None, None] + jnp.einsum('bhcd,bhcv->bhdv', kd_c, v_new)
        return S, o

    mv = lambda t: jnp.moveaxis(t, 2, 0)
    S0 = jnp.zeros((B, H, dk, dv), F32)
    _, o = lax.scan(step, S0, (mv(qg), mv(kd), mv(U), mv(W), mv(intra), mv(gl)))
    return jnp.moveaxis(o, 0, 2).reshape(B, H, L, dv)


def _gated_deltanet(hg, conv_w, a_log, dt_bias, norm_g):
    B, L, _ = hg.shape
    H, dh, Wd = GD_HEADS, GD_HEAD_DIM, GD_WIDTH
    qkv = jax.nn.silu(_causal_dwconv(hg[..., 0:3 * Wd], conv_w))
    a_lg = hg[..., 3 * Wd:3 * Wd + H].astype(F32)
    b_lg = hg[..., 3 * Wd + H:3 * Wd + 2 * H].astype(F32)
    z = hg[..., 3 * Wd + 2 * H:]
    heads = lambda t: t.reshape(B, L, H, dh).transpose(0, 2, 1, 3).astype(F32)
    q = _l2norm(heads(qkv[..., 0:Wd])) * (dh ** -0.5)
    k = _l2norm(heads(qkv[..., Wd:2 * Wd]))
    v = heads(qkv[..., 2 * Wd:])
    g = -(jnp.exp(a_log.astype(F32)) * jax.nn.softplus(a_lg + dt_bias.astype(F32))).transpose(0, 2, 1)
    beta = jax.nn.sigmoid(b_lg).transpose(0, 2, 1)
    o = _chunk_gated_delta(q, k, v, g, beta)
    o = _rmsnorm(o, norm_g, 1e-6) * jax.nn.silu(heads(z))
    return o.transpose(0, 2, 1, 3).reshape(B, L, Wd).astype(hg.dtype)


def _hybrid_mixer(x, layer, w_in, rw_shift_mu, rw_w0, rw_w_up, rw_a0, rw_a_up, rw_g_up, rw_k_k,
                  rw_k_a, rw_r_k, rw_ln_g, rw_ln_b, da_lam_q1, da_lam_k1, da_lam_q2, da_lam_k2,
                  da_norm_g, gd_conv_w, gd_a_log, gd_dt_bias, gd_norm_g, w_branch, w_out):
    B, L, _ = x.shape
    h = x @ w_in
    o1 = RW_COLS
    o2 = o1 + DA_COLS
    o3 = o2 + GD_COLS
    ya = _rwkv7(h[..., 0:o1], rw_shift_mu, rw_w0, rw_w_up, rw_a0, rw_a_up, rw_g_up,
                rw_k_k, rw_k_a, rw_r_k, rw_ln_g, rw_ln_b)
    lam_init = 0.8 - 0.6 * math.exp(-0.3 * layer)
    yb = _diff_attention(h[..., o1:o2], da_lam_q1, da_lam_k1, da_lam_q2, da_lam_k2, da_norm_g, lam_init)
    yc = _gated_deltanet(h[..., o2:o3], gd_conv_w, gd_a_log, gd_dt_bias, gd_norm_g)
    ys = jnp.stack([ya, yb, yc], axis=2)
    branches = jnp.einsum('blnc,ncd->blnd', ys, w_branch)
    gates = jax.nn.sigmoid(h[..., o3:].reshape(B, L, N_BRANCH, D_MODEL))
    merged = jnp.einsum('blnd,blnd->bld', gates, branches)
    return merged @ w_out


def setup_inputs(seed: int = 0) -> dict:
    key = jax.random.key(seed)
    ks = jax.random.split(key, 40)
    nrm = lambda i, shape, s: jax.random.normal(ks[i], shape, F32) * s
    gain = lambda i, shape: 1.0 + nrm(i, shape, 0.02)
    Dp, D = DEPTH, D_MODEL
    dt = jnp.exp(jax.random.uniform(ks[24], (Dp, GD_HEADS), F32, math.log(1e-3), math.log(1e-1)))
    return {
        "x": nrm(0, (BATCH, SEQ, D), 1.0),
        "ffn1_w_in": nrm(1, (Dp, D, 2 * D_FF), D ** -0.5),
        "ffn1_w_out": nrm(2, (Dp, D_FF, D), BETA * D_FF ** -0.5),
        "ln1_g": gain(3, (Dp, D)),
        "ln1_b": nrm(4, (Dp, D), 0.02),
        "mix_w_in": nrm(5, (Dp, D, IN_COLS), D ** -0.5),
        "rw_shift_mu": jax.random.uniform(ks[6], (Dp, RW_COLS), F32),
        "rw_w0": nrm(7, (Dp, RW_WIDTH), 0.5),
        "rw_w_up": nrm(8, (Dp, RW_DECAY_LORA, RW_WIDTH), 0.1),
        "rw_a0": nrm(9, (Dp, RW_WIDTH), 0.1),
        "rw_a_up": nrm(10, (Dp, RW_ICLR_LORA, RW_WIDTH), 0.1),
        "rw_g_up": nrm(11, (Dp, RW_GATE_LORA, RW_WIDTH), RW_GATE_LORA ** -0.5),
        "rw_k_k": 0.85 + nrm(12, (Dp, RW_WIDTH), 0.02),
        "rw_k_a": gain(13, (Dp, RW_WIDTH)),
        "rw_r_k": nrm(14, (Dp, RW_HEADS, RW_HEAD_DIM), 0.1),
        "rw_ln_g": gain(15, (Dp, RW_WIDTH)),
        "rw_ln_b": nrm(16, (Dp, RW_WIDTH), 0.02),
        "da_lam_q1": nrm(17, (Dp, DA_HEAD_DIM), 0.1),
        "da_lam_k1": nrm(18, (Dp, DA_HEAD_DIM), 0.1),
        "da_lam_q2": nrm(19, (Dp, DA_HEAD_DIM), 0.1),
        "da_lam_k2": nrm(20, (Dp, DA_HEAD_DIM), 0.1),
        "da_norm_g": gain(21, (Dp, 2 * DA_HEAD_DIM)),
        "gd_conv_w": nrm(22, (Dp, GD_CONV, 3 * GD_WIDTH), GD_CONV ** -0.5),
        "gd_a_log": jnp.log(jax.random.uniform(ks[23], (Dp, GD_HEADS), F32, 1.0, 16.0)),
        "gd_dt_bias": dt + jnp.log(-jnp.expm1(-dt)),
        "gd_norm_g": gain(25, (Dp, GD_HEAD_DIM)),
        "mix_w_branch": nrm(26, (Dp, N_BRANCH, BRANCH_WIDTH, D), BETA * BRANCH_WIDTH ** -0.5),
        "mix_w_out": nrm(27, (Dp, D, D), BETA * D ** -0.5),
        "ln2_g": gain(28, (Dp, D)),
        "ln2_b": nrm(29, (Dp, D), 0.02),
        "ffn2_w_in": nrm(30, (Dp, D, 2 * D_FF), D ** -0.5),
        "ffn2_w_out": nrm(31, (Dp, D_FF, D), BETA * D_FF ** -0.5),
        "ln3_g": gain(32, (Dp, D)),
        "ln3_b": nrm(33, (Dp, D), 0.02),
    }


def reference(x, ffn1_w_in, ffn1_w_out, ln1_g, ln1_b, mix_w_in, rw_shift_mu, rw_w0, rw_w_up,
              rw_a0, rw_a_up, rw_g_up, rw_k_k, rw_k_a, rw_r_k, rw_ln_g, rw_ln_b, da_lam_q1,
              da_lam_k1, da_lam_q2, da_lam_k2, da_norm_g, gd_conv_w, gd_a_log, gd_dt_bias,
              gd_norm_g, mix_w_branch, mix_w_out, ln2_g, ln2_b, ffn2_w_in, ffn2_w_out,
              ln3_g, ln3_b):
    for l in range(DEPTH):
        x = _layernorm(ALPHA * x + 0.5 * _swiglu(x, ffn1_w_in[l], ffn1_w_out[l]), ln1_g[l], ln1_b[l])
        mix = _hybrid_mixer(x, l, mix_w_in[l], rw_shift_mu[l], rw_w0[l], rw_w_up[l], rw_a0[l],
                            rw_a_up[l], rw_g_up[l], rw_k_k[l], rw_k_a[l], rw_r_k[l], rw_ln_g[l],
                            rw_ln_b[l], da_lam_q1[l], da_lam_k1[l], da_lam_q2[l], da_lam_k2[l],
                            da_norm_g[l], gd_conv_w[l], gd_a_log[l], gd_dt_bias[l], gd_norm_g[l],
                            mix_w_branch[l], mix_w_out[l])
        x = _layernorm(ALPHA * x + mix, ln2_g[l], ln2_b[l])
        x = _layernorm(ALPHA * x + 0.5 * _swiglu(x, ffn2_w_in[l], ffn2_w_out[l]), ln3_g[l], ln3_b[l])
    return x
```

```python
import math
import threading
from contextlib import ExitStack
import numpy as np
import concourse.bass as bass
import concourse.mybir as mybir
from concourse.bass_utils import run_bass_kernel_spmd

F32 = mybir.dt.float32
BF16 = mybir.dt.bfloat16
AF = mybir.ActivationFunctionType
ALU = mybir.AluOpType
AX = mybir.AxisListType

ENGS = ('pe', 'act', 'dve', 'pool', 'sp')

D_MODEL = 1024
SEQ = 4096
DEPTH = 2
D_FF = 2816
ALPHA = (2.0 * DEPTH) ** 0.25


class Buf:
    __slots__ = ('t', 'w', 'r', 'name')

    def __init__(self, t, name):
        self.t = t
        self.w = {}
        self.r = {}
        self.name = name

    def __getitem__(self, k):
        return self.t[k]


class _Rec:
    def __init__(self):
        self.call = None

    def __getattr__(self, name):
        def f(*a, **k):
            self.call = (name, a, k)
            return self
        return f


class Prog:
    NDMA = 40

    def __init__(self, nc):
        self.nc = nc
        self.es = ExitStack()
        self.ops = {e: [] for e in ENGS}
        self.sem = {}
        self.cnt = {}
        for e in ('pe', 'act', 'dve', 'pool'):
            self.sem[e] = self.es.enter_context(nc.semaphore('s_' + e))
            self.cnt[e] = 0
        self.known = {e: {} for e in ENGS}
        self.dsem = [[self.es.enter_context(nc.semaphore('d%d' % i)), 0] for i in range(self.NDMA)]
        self.dq = {'sp': (0, 28), 'pool': (28, 8), 'act': (36, 4)}
        self.dnext = {'sp': 0, 'pool': 0, 'act': 0}
        self.nalloc = 0
        self.ninst = 0
        self.nwait = 0
        self.cur = self.es

    _il = None

    def interleave(self, fns):
        if self._il is not None and getattr(self._il['tl'], 'idx', None) is not None:
            return self._interleave_nested(fns)
        n = len(fns)
        il = {'turn': 0, 'alive': [True] * n, 'cv': threading.Condition(), 'err': None, 'tl': threading.local()}
        self._il = il

        def advance(i):
            m = len(il['alive'])
            for d in range(1, m + 1):
                j = (i + d) % m
                if il['alive'][j]:
                    il['turn'] = j
                    return
            il['turn'] = -1

        il['advance'] = advance
        ths = [threading.Thread(target=self._il_runner, args=(i, f)) for i, f in enumerate(fns)]
        for t in ths:
            t.start()
        for t in ths:
            t.join()
        self._il = None
        if il['err'] is not None:
            raise il['err']

    def _il_runner(self, i, fn, on_exit=None):
        il = self._il
        il['tl'].idx = i
        with il['cv']:
            while il['turn'] != i:
                il['cv'].wait()
        try:
            fn()
        except BaseException as e:
            il['err'] = e
        finally:
            with il['cv']:
                il['alive'][i] = False
                if on_exit is not None:
                    on_exit()
                il['advance'](i)
                il['cv'].notify_all()

    def _interleave_nested(self, fns):
        il = self._il
        me = il['tl'].idx
        left = [len(fns)]

        def on_exit():
            left[0] -= 1
            if left[0] == 0:
                il['alive'][me] = True
        with il['cv']:
            base = len(il['alive'])
            il['alive'].extend([True] * len(fns))
            il['alive'][me] = False
            ths = [threading.Thread(target=self._il_runner, args=(base + k, f, on_exit)) for k, f in enumerate(fns)]
            il['advance'](me)
            il['cv'].notify_all()
        for t in ths:
            t.start()
        with il['cv']:
            while not (il['turn'] == me and il['alive'][me]):
                il['cv'].wait()
        for t in ths:
            t.join()
        if il['err'] is not None:
            raise il['err']

    def _yield(self):
        il = self._il
        if il is None:
            return
        i = getattr(il['tl'], 'idx', None)
        if i is None:
            return
        with il['cv']:
            il['advance'](i)
            il['cv'].notify_all()
            while il['turn'] != i:
                il['cv'].wait()

    def phase_begin(self):
        self.cur = ExitStack()

    def phase_end(self):
        self.barrier()
        self.emit_block()
        self.cur.close()
        self.cur = self.es

    def barrier(self):
        for e in ENGS:
            kn = self.known[e]
            waits = []
            for e2 in ('pe', 'act', 'dve', 'pool'):
                if e2 != e and self.cnt[e2] > kn.get(id(self.sem[e2]), 0):
                    waits.append((self.sem[e2], self.cnt[e2]))
                    kn[id(self.sem[e2])] = self.cnt[e2]
            for s, tot in self.dsem:
                if tot > kn.get(id(s), 0):
                    waits.append((s, tot))
                    kn[id(s)] = tot

            def emit(E, waits=waits):
                for s, v in waits:
                    E.wait_ge(s, v)
            self.ops[e].append(emit)

    def sb(self, name, shape, dtype=F32):
        self.nalloc += 1
        t = self.cur.enter_context(self.nc.sbuf_tensor('%s_%d' % (name, self.nalloc), list(shape), dtype))
        return Buf(t, name)

    def ps(self, name, shape, dtype=F32):
        self.nalloc += 1
        t = self.cur.enter_context(self.nc.psum_tensor('%s_%d' % (name, self.nalloc), list(shape), dtype))
        return Buf(t, name)

    def dram(self, name, shape, dtype=F32, kind='Internal'):
        t = self.nc.dram_tensor(name, list(shape), dtype, kind=kind)
        return Buf(t.ap(), name)

    def _collect(self, eng, reads, writes, keep):
        need = {}
        known = self.known[eng]

        def add(tok, skip_same):
            s, v, e = tok
            if skip_same and e == eng and eng == 'pe':
                return
            k = id(s)
            if known.get(k, 0) >= v:
                return
            if k not in need or need[k][1] < v:
                need[k] = (s, v)
        for b in reads:
            for tok in b.w.values():
                add(tok, False)
        for b in writes:
            if not keep:
                for tok in b.w.values():
                    add(tok, True)
            for tok in b.r.values():
                add(tok, True)
        waits = list(need.values())
        for s, v in waits:
            known[id(s)] = v
        self.nwait += len(waits)
        return waits

    def _update(self, key, tok, reads, writes, keep):
        for b in reads:
            b.r[key] = tok
        for b in writes:
            if keep:
                b.w[key] = tok
            else:
                b.w = {key: tok}
                b.r = {}

    def op(self, eng, fn, reads=(), writes=(), inc=True, keep=False):
        waits = self._collect(eng, reads, writes, keep)
        sem = self.sem[eng]
        if inc:
            self.cnt[eng] += 1
            tok = (sem, self.cnt[eng], eng)
        else:
            tok = (sem, self.cnt[eng] + 1, eng)

        rec = _Rec()
        fn(rec)
        cname, ca, ck = rec.call

        def emit(E, waits=waits, inc=inc, sem=sem, cname=cname, ca=ca, ck=ck):
            for s, v in waits:
                E.wait_ge(s, v)
            ins = getattr(E, cname)(*ca, **ck)
            if inc:
                ins.then_inc(sem, 1)
        self.ops[eng].append(emit)
        self.ninst += 1
        self._update(eng, tok, reads, writes, keep)
        self._yield()

    def dma(self, q, out, in_, reads=(), writes=(), keep=False, **kw):
        b0, nq = self.dq[q]
        slot = self.dsem[b0 + self.dnext[q]]
        self.dnext[q] = (self.dnext[q] + 1) % nq
        waits = self._collect(q, reads, writes, keep)
        s, tot = slot
        kn = self.known[q]
        if tot > 0 and kn.get(id(s), 0) < tot:
            waits.append((s, tot))
            kn[id(s)] = tot
        slot[1] = tot + 16
        tok = (s, tot + 16, 'dma')

        def emit(E, waits=waits, out=out, in_=in_, s=s, kw=kw):
            for ws, v in waits:
                E.wait_ge(ws, v)
            E.dma_start(out=out, in_=in_, **kw).then_inc(s, 16)
        self.ops[q].append(emit)
        self.ninst += 1
        self._update(('dma', id(s)), tok, reads, writes, keep)
        self._yield()

    DBG = False

    def dbg(self, name, buf, ap, shape, dtype=F32):
        if not self.DBG:
            return
        d = self.dram('dbg_' + name, shape, dtype, kind='ExternalOutput')
        self.dma('sp', d[tuple(slice(None) for _ in shape)], ap, reads=[buf], writes=[d])
        self.dbgs = getattr(self, 'dbgs', []) + [d]

    def finish(self, bufs, eng='sp'):
        bufs = list(bufs) + getattr(self, 'dbgs', [])
        waits = self._collect(eng, bufs, (), False)

        def emit(E, waits=waits):
            for s, v in waits:
                E.wait_ge(s, v)
        self.ops[eng].append(emit)

    def build(self):
        self.emit_block()
        self.es.close()
        return self.nc

    def emit_block(self):
        ops = self.ops
        self.ops = {e: [] for e in ENGS}
        with self.nc.Block() as block:
            @block.tensor
            def _(E):
                for f in ops['pe']:
                    f(E)

            @block.scalar
            def _(E):
                for f in ops['act']:
                    f(E)

            @block.vector
            def _(E):
                for f in ops['dve']:
                    f(E)

            @block.gpsimd
            def _(E):
                for f in ops['pool']:
                    f(E)

            @block.sync
            def _(E):
                for f in ops['sp']:
                    f(E)


def load_w_bf16(P, name, w_ap, rows, cols, csplit=1):
    nk = rows // 128
    w = P.sb(name, [128, nk, cols], BF16)
    src = w_ap.rearrange("(k p) c -> p k c", p=128)
    cw = cols // csplit
    for k in range(nk):
        for c in range(csplit):
            P.dma('pool', w[:, k, c * cw:(c + 1) * cw], src[:, k, c * cw:(c + 1) * cw],
                  writes=[w], keep=True)
    return w


class WBlocks:
    def __init__(self, P, name, w_ap, rows, bounds, order=None):
        self.nk = rows // 128
        self.bounds = list(bounds)
        self.bufs = []
        src = w_ap.rearrange("(k p) c -> p k c", p=128)
        nb_ = len(self.bounds) - 1
        for b in range(nb_):
            self.bufs.append(P.sb('%s%d' % (name, b), [128, self.nk, self.bounds[b + 1] - self.bounds[b]], BF16))
        for b in (order if order is not None else range(nb_)):
            lo, hi = self.bounds[b], self.bounds[b + 1]
            for k in range(self.nk):
                P.dma('pool', self.bufs[b][:, k, :], src[:, k, lo:hi], writes=[self.bufs[b]], keep=True)

    def blk(self, c0):
        for b in range(len(self.bounds) - 1):
            if self.bounds[b] <= c0 < self.bounds[b + 1]:
                return b
        raise ValueError(c0)

    def buf(self, c0):
        return self.bufs[self.blk(c0)]

    def ap(self, kc, c0, n):
        b = self.blk(c0)
        lo = self.bounds[b]
        assert c0 + n <= self.bounds[b + 1]
        return self.bufs[b][:, kc, c0 - lo:c0 - lo + n]


class Ctx:
    def __init__(self, P, ident_ap):
        self.P = P
        idf = P.sb('identf', [128, 128], F32)
        P.dma('sp', idf[:, :], ident_ap, writes=[idf])
        self.identf = idf
        self.ident = P.sb('ident', [128, 128], BF16)
        P.op('dve', lambda E: E.tensor_copy(self.ident[:, :], idf[:, :]), reads=[idf], writes=[self.ident])


def transpose_in(P, cx, x_dram, row0, xin, xbf, psT, xT, s):
    P.dma('sp', xin[:, :], x_dram[row0:row0 + 128, :], reads=[x_dram], writes=[xin])
    P.op('act', lambda E: E.copy(xbf[:, :], xin[:, :]), reads=[xin], writes=[xbf])
    for kc in range(8):
        P.op('pe', lambda E, kc=kc: E.transpose(psT[:, kc * 128:(kc + 1) * 128],
                                                 xbf[:, kc * 128:(kc + 1) * 128], cx.ident[:, :]),
             reads=[xbf, cx.ident], writes=[psT], inc=(kc == 7))
    P.op('dve', lambda E: E.tensor_copy(xT[:, :, s * 128:(s + 1) * 128],
                                        psT[:, :].rearrange("p (k t) -> p k t", k=8)),
         reads=[psT], writes=[xT], keep=True)


def layernorm_out(P, y, gb, bb, out_dram, row0, eps, st, mv, rs):
    for h in range(2):
        P.op('dve', lambda E, h=h: E.bn_stats(st[:, h, :], y[:, h * 512:(h + 1) * 512]),
             reads=[y], writes=[st], keep=(h == 1))
    P.op('dve', lambda E: E.bn_aggr(mv[:, :], st[:, :, :].rearrange("p a b -> p (a b)")),
         reads=[st], writes=[mv])
    P.op('dve', lambda E: E.tensor_scalar(rs[:, 0:1], mv[:, 1:2], eps, None, ALU.add), reads=[mv], writes=[rs])
    P.op('act', lambda E: E.activation(out=rs[:, 0:1], in_=rs[:, 0:1], func=AF.Sqrt), reads=[rs], writes=[rs])
    P.op('dve', lambda E: E.reciprocal(rs[:, 0:1], rs[:, 0:1]), reads=[rs], writes=[rs])
    P.op('dve', lambda E: E.tensor_scalar(rs[:, 1:2], mv[:, 0:1], rs[:, 0:1], -1.0, ALU.mult, ALU.mult),
         reads=[mv, rs], writes=[rs])
    P.op('act', lambda E: E.activation(out=y[:, :], in_=y[:, :], func=AF.Identity, scale=rs[:, 0:1], bias=rs[:, 1:2]),
         reads=[y, rs], writes=[y])
    P.op('pool', lambda E: E.tensor_tensor(y[:, :], y[:, :], gb[:, :], ALU.mult), reads=[y, gb], writes=[y])
    P.op('pool', lambda E: E.tensor_tensor(y[:, :], y[:, :], bb[:, :], ALU.add), reads=[y, bb], writes=[y])
    P.dma('sp', out_dram[row0:row0 + 128, :], y[:, :], reads=[y], writes=[out_dram], keep=True)


def ffn_phase(P, cx, x_dram, out_dram, w_in_ap, w_out_ap, g_ap, b_ap, NT):
    NFF = D_FF // 128
    W1 = WBlocks(P, 'w1', w_in_ap, D_MODEL, [0, 1408, 2816, 4224, 5632], order=[0, 2, 1, 3])
    W2 = load_w_bf16(P, 'w2', w_out_ap, D_FF, D_MODEL)
    gb = P.sb('lng', [128, D_MODEL])
    bb = P.sb('lnb', [128, D_MODEL])
    P.dma('sp', gb[:, :], g_ap.partition_broadcast(128), writes=[gb])
    P.dma('sp', bb[:, :], b_ap.partition_broadcast(128), writes=[bb])
    xin = [P.sb('xin', [128, D_MODEL]) for _ in range(3)]
    xbf = [P.sb('xbf', [128, D_MODEL], BF16) for _ in range(2)]
    xT = P.sb('xT', [128, 8, 512], BF16)
    gT = P.sb('gT', [128, NFF, 512], BF16)
    sg = [P.sb('sg', [128, 512]) for _ in range(2)]
    y = [P.sb('y', [128, D_MODEL]) for _ in range(2)]
    st = [P.sb('st', [128, 2, 6]) for _ in range(2)]
    mv = [P.sb('mv', [128, 2]) for _ in range(2)]
    rs = [P.sb('rs', [128, 2]) for _ in range(2)]
    psT = [P.ps('psT', [128, 1024], BF16) for _ in range(2)]
    psA = [P.ps('psA', [128, 512]) for _ in range(4)]
    psO = [P.ps('psO', [128, 512]) for _ in range(2)]
    eps = 1e-5 / (ALPHA * ALPHA)
    c_res = 0.5 / ALPHA
    nx = 0
    ny = 0
    for t in range(NT // 512):
        for s in range(4):
            transpose_in(P, cx, x_dram, t * 512 + s * 128, xin[nx % 3], xbf[nx % 2], psT[nx % 2], xT, s)
            nx += 1
        for j in range(NFF):
            pg = psA[(2 * j) % 4]
            pu = psA[(2 * j + 1) % 4]
            for kc in range(8):
                P.op('pe', lambda E, kc=kc, j=j, pg=pg: E.matmul(
                    pg[:, :], W1.ap(kc, j * 128, 128), xT[:, kc, :], start=(kc == 0), stop=(kc == 7)),
                    reads=[W1.buf(j * 128), xT], writes=[pg], inc=(kc == 7))
            for kc in range(8):
                P.op('pe', lambda E, kc=kc, j=j, pu=pu: E.matmul(
                    pu[:, :], W1.ap(kc, D_FF + j * 128, 128), xT[:, kc, :],
                    start=(kc == 0), stop=(kc == 7)),
                    reads=[W1.buf(D_FF + j * 128), xT], writes=[pu], inc=(kc == 7))
            sgj = sg[j % 2]
            P.op('act', lambda E, pg=pg, sgj=sgj: E.activation(out=sgj[:, :], in_=pg[:, :], func=AF.Silu),
                 reads=[pg], writes=[sgj])
            P.op('dve', lambda E, pu=pu, sgj=sgj, j=j: E.tensor_tensor(gT[:, j, :], sgj[:, :], pu[:, :], ALU.mult),
                 reads=[sgj, pu], writes=[gT], keep=True)
        for s in range(4):
            row0 = t * 512 + s * 128
            xi = xin[nx % 3]
            nx += 1
            P.dma('sp', xi[:, :], x_dram[row0:row0 + 128, :], reads=[x_dram], writes=[xi])
            yy = y[ny % 2]
            for n in range(2):
                po = psO[n]
                for j in range(NFF):
                    P.op('pe', lambda E, j=j, s=s, n=n, po=po: E.matmul(
                        po[:, :], gT[:, j, s * 128:(s + 1) * 128], W2[:, j, n * 512:(n + 1) * 512],
                        start=(j == 0), stop=(j == NFF - 1)),
                        reads=[gT, W2], writes=[po], inc=(j == NFF - 1))
                P.op('dve', lambda E, n=n, po=po, xi=xi, yy=yy: E.scalar_tensor_tensor(
                    yy[:, n * 512:(n + 1) * 512], po[:, :], c_res, xi[:, n * 512:(n + 1) * 512], ALU.mult, ALU.add),
                    reads=[po, xi], writes=[yy], keep=(n == 1))
            layernorm_out(P, yy, gb, bb, out_dram, row0, eps, st[ny % 2], mv[ny % 2], rs[ny % 2])
            ny += 1


RW_COLS = 1792
DA_COLS = 1536
GD_COLS = 2056
O1 = RW_COLS
O2 = O1 + DA_COLS
O3 = O2 + GD_COLS
IN_COLS = O3 + 3072


def mixproj_phase(P, cx, x_dram, w_ap, sc, NT):
    W = WBlocks(P, 'wmix', w_ap, D_MODEL, [0, 1792, 2816, 3328, 4864, 4872, 5384] + [5384 + 512 * (c + 1) for c in range(6)],
                order=[0, 1, 3, 2, 4, 5, 6, 7, 8, 9, 10, 11])
    xin = [P.sb('xin', [128, D_MODEL]) for _ in range(3)]
    xbf = [P.sb('xbf', [128, D_MODEL], BF16) for _ in range(2)]
    xT = P.sb('xT', [128, 8, 512], BF16)
    stf = [P.sb('stf', [128, 512]) for _ in range(4)]
    stb = [P.sb('stb', [128, 512], BF16) for _ in range(3)]
    psT = [P.ps('psT', [128, 1024], BF16) for _ in range(2)]
    psA = [P.ps('psA', [128, 512]) for _ in range(6)]
    nx = 0
    npz = 0
    nf = 0
    nb = 0
    fm = []
    for c in range(14):
        fm.append((c * 128, sc['hr'], c * 128, 'f', 1.0))
    for c in range(4):
        fm.append((O1 + c * 128, sc['daqk'], c * 128, 'b', 0.125))
    for c in range(4):
        fm.append((O1 + 512 + c * 128, sc['daqk'], 512 + c * 128, 'b', 1.0))
    for c in range(12):
        fm.append((O2 + c * 128, sc['gqkv'], c * 128, 'f', 1.0))
    tm = [(O1 + 1024, 512, sc['dav'], 0, 'b', None),
          (O2 + 1536, 8, sc['gab'], 0, 'f', None),
          (O2 + 1544, 512, sc['gz'], 0, 'f', AF.Silu)]
    for c in range(6):
        tm.append((O3 + c * 512, 512, sc['gates'], c * 512, 'f', AF.Sigmoid))
    for t in range(NT // 512):
        for s in range(4):
            transpose_in(P, cx, x_dram, t * 512 + s * 128, xin[nx % 3], xbf[nx % 2], psT[nx % 2], xT, s)
            nx += 1
        for (c0, dst, r0, tag, scl) in fm:
            pp = psA[npz % 6]
            npz += 1
            for kc in range(8):
                P.op('pe', lambda E, kc=kc, c0=c0, pp=pp: E.matmul(
                    pp[:, :], W.ap(kc, c0, 128), xT[:, kc, :], start=(kc == 0), stop=(kc == 7)),
                    reads=[W.buf(c0), xT], writes=[pp], inc=(kc == 7))
            if tag == 'f':
                so = stf[nf % 4]
                nf += 1
            else:
                so = stb[nb % 3]
                nb += 1
            if (npz % 2) == 0:
                P.op('act', lambda E, so=so, pp=pp, scl=scl: E.mul(so[:, :], pp[:, :], scl), reads=[pp], writes=[so])
            else:
                P.op('dve', lambda E, so=so, pp=pp, scl=scl: E.tensor_scalar(so[:, :], pp[:, :], scl, None, ALU.mult),
                     reads=[pp], writes=[so])
            P.dma('sp', dst[r0:r0 + 128, t * 512:(t + 1) * 512], so[:, :], reads=[so], writes=[dst], keep=True)
        for s in range(4):
            row0 = t * 512 + s * 128
            for (c0, ncol, dst, d0, tag, func) in tm:
                pp = psA[npz % 6]
                npz += 1
                for kc in range(8):
                    P.op('pe', lambda E, kc=kc, c0=c0, pp=pp, s=s, ncol=ncol: E.matmul(
                        pp[:, 0:ncol], xT[:, kc, s * 128:(s + 1) * 128], W.ap(kc, c0, ncol),
                        start=(kc == 0), stop=(kc == 7)),
                        reads=[W.buf(c0), xT], writes=[pp], inc=(kc == 7))
                if tag == 'f':
                    so = stf[nf % 4]
                    nf += 1
                else:
                    so = stb[nb % 3]
                    nb += 1
                if func is not None:
                    P.op('act', lambda E, so=so, pp=pp, ncol=ncol, func=func: E.activation(
                        out=so[:, 0:ncol], in_=pp[:, 0:ncol], func=func), reads=[pp], writes=[so])
                else:
                    P.op('dve', lambda E, so=so, pp=pp, ncol=ncol: E.tensor_copy(so[:, 0:ncol], pp[:, 0:ncol]),
                         reads=[pp], writes=[so])
                P.dma('sp', dst[row0:row0 + 128, d0:d0 + ncol], so[:, 0:ncol], reads=[so], writes=[dst], keep=True)


def mix_scratch(P, L, pfx='', ext=()):
    k = lambda n: 'ExternalOutput' if n in ext else 'Internal'
    return {
        'hr': P.dram(pfx + 'hr', [RW_COLS, L], F32, kind=k('hr')),
        'daqk': P.dram(pfx + 'daqk', [1024, L], BF16, kind=k('daqk')),
        'dav': P.dram(pfx + 'dav', [L, 512], BF16, kind=k('dav')),
        'gqkv': P.dram(pfx + 'gqkv', [1536, L], F32, kind=k('gqkv')),
        'gab': P.dram(pfx + 'gab', [L, 8], F32, kind=k('gab')),
        'gz': P.dram(pfx + 'gz', [L, 512], F32, kind=k('gz')),
        'gates': P.dram(pfx + 'gates', [L, 3072], F32, kind=k('gates')),
        'yaT': P.dram(pfx + 'yaT', [512, L], BF16, kind=k('yaT')),
        'ybT': P.dram(pfx + 'ybT', [512, L], BF16, kind=k('ybT')),
        'ycT': P.dram(pfx + 'ycT', [512, L], BF16, kind=k('ycT')),
    }


def da_consts():
    kk = np.arange(128, dtype=np.float64)
    ab = np.zeros((128, 4 * 32), np.float32)
    for h in range(4):
        slope = 2.0 ** (-8.0 * (h + 1) / 4)
        for d in range(32):
            ab[:, h * 32 + d] = slope * (kk - 127.0) - slope * 128.0 * d
    cm = (kk[:, None] <= kk[None, :]).astype(np.float32)
    return ab, np.concatenate([cm, cm], axis=1)


def diffattn_phase(P, cx, sc, lam_aps, normg_ap, lam_init, abias_ap, cmask_ap, L):
    NB = L // 128
    qT = P.sb('qT', [128, 4, L], BF16)
    kT = P.sb('kT', [128, 4, L], BF16)
    V = P.sb('V', [128, NB, 4, 130], BF16)
    for h in range(4):
        P.dma('sp', qT[:, h, :], sc['daqk'][h * 128:(h + 1) * 128, :], reads=[sc['daqk']], writes=[qT], keep=True)
        P.dma('sp', kT[:, h, :], sc['daqk'][512 + h * 128:512 + (h + 1) * 128, :], reads=[sc['daqk']], writes=[kT], keep=True)
        P.dma('sp', V[:, :, h, 0:128], sc['dav'][:, h * 128:(h + 1) * 128].rearrange("(n p) d -> p n d", p=128),
              reads=[sc['dav']], writes=[V], keep=True)
    P.op('pool', lambda E: E.memset(V[:, :, :, 128:129], 1.0), writes=[V], keep=True)
    abias = P.sb('abias', [128, 128])
    P.dma('sp', abias[:, :], abias_ap, writes=[abias])
    cmf = P.sb('cmf', [128, 256])
    P.dma('sp', cmf[:, :], cmask_ap, writes=[cmf])
    cm = P.sb('cm', [128, 256], BF16)
    P.op('dve', lambda E: E.tensor_copy(cm[:, :], cmf[:, :]), reads=[cmf], writes=[cm])
    l4 = P.sb('lam4', [128, 4, 64])
    for i in range(4):
        P.dma('sp', l4[:, i, :], lam_aps[i].partition_broadcast(128), writes=[l4], keep=True)
    lp = P.sb('lamp', [128, 2, 64])
    P.op('dve', lambda E: E.tensor_tensor(lp[:, 0, :], l4[:, 0, :], l4[:, 1, :], ALU.mult), reads=[l4], writes=[lp], keep=True)
    P.op('dve', lambda E: E.tensor_tensor(lp[:, 1, :], l4[:, 2, :], l4[:, 3, :], ALU.mult), reads=[l4], writes=[lp], keep=True)
    ls = P.sb('lams', [128, 2])
    P.op('dve', lambda E: E.tensor_reduce(ls[:, :], lp[:, :, :], AX.X, ALU.add), reads=[lp], writes=[ls])
    P.op('act', lambda E: E.activation(out=ls[:, :], in_=ls[:, :], func=AF.Exp), reads=[ls], writes=[ls])
    nlam = P.sb('nlam', [128, 1])
    P.op('dve', lambda E: E.tensor_tensor(nlam[:, :], ls[:, 1:2], ls[:, 0:1], ALU.subtract), reads=[ls], writes=[nlam])
    P.op('dve', lambda E: E.tensor_scalar(nlam[:, :], nlam[:, :], -lam_init, None, ALU.add), reads=[nlam], writes=[nlam])
    gv = P.sb('gv', [128, 128])
    P.dma('sp', gv[:, :], normg_ap.partition_broadcast(128), writes=[gv])
    P.op('act', lambda E: E.mul(gv[:, :], gv[:, :], 1.0 - lam_init), reads=[gv], writes=[gv])

    psS = [P.ps('psS', [128, 512]) for _ in range(3)]
    psO = [P.ps('psO', [128, 512]) for _ in range(4)]
    psT = P.ps('psTd', [128, 1024], BF16)
    pT = [P.sb('pT', [128, 512], BF16) for _ in range(3)]
    rl = [P.sb('rl', [128, 4]) for _ in range(4)]
    oa = [P.sb('oa', [128, 128]) for _ in range(4)]
    oo = [P.sb('oo', [128, 128]) for _ in range(4)]
    sq = [P.sb('sq', [128, 128]) for _ in range(4)]
    ob = [P.sb('ob', [128, 128], BF16) for _ in range(4)]
    yst = [P.sb('yst', [128, 128], BF16) for _ in range(4)]
    qz = [P.sb('qz', [128, 512], BF16) for _ in range(2)]
    for e_ in range(2):
        P.op('pool', lambda E, e_=e_: E.memset(qz[e_][:, :], 0.0), writes=[qz[e_]])
    NQ = L // 256
    steps = []
    for Q in range(NQ):
        for h in range(4):
            for kj in range(2 * Q + 2):
                steps.append((Q, h, kj))
    state = {'ne': 0, 'n': 0}
    pend = []
    ev0 = [P.sb('ev0', [128, 129]) for _ in range(4)]
    ev1 = [P.sb('ev1', [128, 129]) for _ in range(4)]

    def emit_qk(n):
        Q, h, kj = steps[n]
        g = (Q * 4 + h) % 2
        if kj == 0:
            for m in range(2):
                P.op('pool', lambda E, m=m: E.tensor_copy(qz[g][m * 64:(m + 1) * 64, m * 256:(m + 1) * 256],
                                                          qT[m * 64:(m + 1) * 64, h, Q * 256:(Q + 1) * 256]),
                     reads=[qT], writes=[qz[g]], keep=(m == 1))
        ps = psS[n % 3]
        P.op('pe', lambda E: E.matmul(ps[:, :], kT[:, h, kj * 128:(kj + 1) * 128], qz[g][:, :], start=True, stop=True),
             reads=[kT, qz[g]], writes=[ps])

    def emit_rest(n):
        Q, h, kj = steps[n]
        ps = psS[n % 3]
        pt = pT[n % 3]
        bi = h * 32 + (2 * Q + 1 - kj)
        P.op('act', lambda E: E.activation(out=pt[:, :], in_=ps[:, :], func=AF.Exp, bias=abias[:, bi:bi + 1]),
             reads=[ps, abias], writes=[pt])
        if kj >= 2 * Q:
            sbm = kj - 2 * Q
            P.op('pool', lambda E: E.tensor_tensor(pt[:, :].rearrange("p (m c) -> p m c", m=2)[:, :, sbm * 128:(sbm + 1) * 128],
                                                   pt[:, :].rearrange("p (m c) -> p m c", m=2)[:, :, sbm * 128:(sbm + 1) * 128],
                                                   cm[:, :].rearrange("p (m c) -> p m c", m=2), ALU.mult),
                 reads=[pt, cm], writes=[pt])
        for sb in range(2):
            if kj > 2 * Q + sb:
                continue
            for m in range(2):
                po = psO[m * 2 + sb]
                P.op('pe', lambda E, m=m, sb=sb, po=po: E.matmul(po[:, 0:129], pt[:, m * 256 + sb * 128:m * 256 + (sb + 1) * 128],
                                                                 V[:, kj, h, 0:129], start=(kj == 0), stop=(kj == 2 * Q + sb)),
                     reads=[pt, V], writes=[po], inc=(kj == 2 * Q + sb))
            if kj == 2 * Q + sb:
                epilogue(Q, h, sb)
        while pend and pend[0][0] <= n:
            pend.pop(0)[1]()

    def epilogue(Q, h, sb):
        while len(pend) > 2:
            pend.pop(0)[1]()
        e = state['ne'] % 4
        state['ne'] += 1
        po0, po1 = psO[sb], psO[2 + sb]
        qi = 2 * Q + sb
        r_, oa_, oo_, sq_, ob_, ys_ = rl[e], oa[e], oo[e], sq[e], ob[e], yst[e]
        e0_, e1_ = ev0[e], ev1[e]
        P.op('dve', lambda E: E.tensor_copy(e0_[:, :], po0[:, 0:129]), reads=[po0], writes=[e0_])
        P.op('act', lambda E: E.copy(e1_[:, :], po1[:, 0:129]), reads=[po1], writes=[e1_])
        P.op('dve', lambda E: E.reciprocal(r_[:, 0:1], e0_[:, 128:129]), reads=[e0_], writes=[r_])
        P.op('dve', lambda E: E.reciprocal(r_[:, 1:2], e1_[:, 128:129]), reads=[e1_], writes=[r_], keep=True)
        P.op('dve', lambda E: E.tensor_tensor(r_[:, 1:2], r_[:, 1:2], nlam[:, 0:1], ALU.mult), reads=[r_, nlam], writes=[r_])
        P.op('act', lambda E: E.activation(out=oa_[:, :], in_=e0_[:, 0:128], func=AF.Copy, scale=r_[:, 0:1]), reads=[e0_, r_], writes=[oa_])
        P.op('dve', lambda E: E.scalar_tensor_tensor(oo_[:, :], e1_[:, 0:128], r_[:, 1:2], oa_[:, :], ALU.mult, ALU.add),
             reads=[e1_, r_, oa_], writes=[oo_])
        P.op('pool', lambda E: E.tensor_tensor(sq_[:, :], oo_[:, :], oo_[:, :], ALU.mult), reads=[oo_], writes=[sq_])
        P.op('dve', lambda E: E.tensor_reduce(r_[:, 2:3], sq_[:, :], AX.X, ALU.add), reads=[sq_], writes=[r_], keep=True)
        P.op('dve', lambda E: E.tensor_scalar(r_[:, 2:3], r_[:, 2:3], 1.0 / 128, 1e-5, ALU.mult, ALU.add), reads=[r_], writes=[r_])
        P.op('act', lambda E: E.activation(out=r_[:, 2:3], in_=r_[:, 2:3], func=AF.Ln), reads=[r_], writes=[r_])
        P.op('act', lambda E: E.activation(out=r_[:, 2:3], in_=r_[:, 2:3], func=AF.Exp, scale=-0.5), reads=[r_], writes=[r_])
        P.op('dve', lambda E: E.scalar_tensor_tensor(ob_[:, :], oo_[:, :], r_[:, 2:3], gv[:, :], ALU.mult, ALU.mult),
             reads=[oo_, r_, gv], writes=[ob_])
        def tail():
            P.op('pe', lambda E: E.transpose(psT[:, 0:128], ob_[:, :], cx.ident[:, :]), reads=[ob_, cx.ident], writes=[psT])
            P.op('act', lambda E: E.copy(ys_[:, :], psT[:, 0:128]), reads=[psT], writes=[ys_])
            P.dma('sp', sc['ybT'][h * 128:(h + 1) * 128, qi * 128:(qi + 1) * 128], ys_[:, :],
                  reads=[ys_], writes=[sc['ybT']], keep=True)
        pend.append((state['n'] + 6, tail))

    emit_qk(0)
    if len(steps) > 1:
        emit_qk(1)
    for n in range(len(steps)):
        state['n'] = n
        if n + 2 < len(steps):
            emit_qk(n + 2)
        emit_rest(n)
    while pend:
        pend.pop(0)[1]()


def neumann_TT(P, cx, X0, XT0, bufs, psN, H, C, nstage=5):
    TT = bufs['TT'][0]
    for h in range(H):
        P.op('pool', lambda E, h=h: E.tensor_tensor(TT[:, h, :], XT0[:, h, :], cx.identf[0:C, 0:C], ALU.add),
             reads=[XT0, cx.identf], writes=[TT], keep=(h > 0))
    X, XT = X0, XT0
    for k in range(1, nstage + 1):
        Xn = bufs['X'][k % 2]
        XTn = bufs['XT'][k % 2]
        TTn = bufs['TT'][k % 2]
        pa = psN[0]
        for h in range(H):
            P.op('pe', lambda E, h=h, X=X, XT=XT, pa=pa: E.matmul(pa[0:C, h * C:(h + 1) * C], XT[:, h, :], X[:, h, :],
                                                                   start=True, stop=True),
                 reads=[X, XT], writes=[pa], inc=(h == H - 1))
        P.op('act', lambda E, Xn=Xn, pa=pa: E.copy(Xn[:, :, :].rearrange("p h c -> p (h c)"), pa[0:C, 0:H * C]),
             reads=[pa], writes=[Xn])
        if k < nstage:
            pb = psN[1]
            for h in range(H):
                P.op('pe', lambda E, h=h, X=X, XT=XT, pb=pb: E.matmul(pb[0:C, h * C:(h + 1) * C], X[:, h, :], XT[:, h, :],
                                                                       start=True, stop=True),
                     reads=[X, XT], writes=[pb], inc=(h == H - 1))
            P.op('dve', lambda E, XTn=XTn, pb=pb: E.tensor_copy(XTn[:, :, :].rearrange("p h c -> p (h c)"), pb[0:C, 0:H * C]),
                 reads=[pb], writes=[XTn])
        pc = psN[2]
        for h in range(H):
            P.op('pe', lambda E, h=h, Xn=Xn, TT=TT, pc=pc: E.matmul(pc[0:C, h * C:(h + 1) * C], Xn[:, h, :], TT[:, h, :],
                                                                     start=True, stop=True),
                 reads=[Xn, TT], writes=[pc], inc=(h == H - 1))
        P.op('dve', lambda E, TTn=TTn, TT=TT, pc=pc: E.tensor_tensor(
            TTn[:, :, :].rearrange("p h c -> p (h c)"), pc[0:C, 0:H * C], TT[:, :, :].rearrange("p h c -> p (h c)"), ALU.add),
            reads=[pc, TT], writes=[TTn])
        X, XT, TT = Xn, XTn, TTn
    return TT


def tri_consts(C=64):
    i = np.arange(C)
    tri_le = (i[:, None] <= i[None, :]).astype(np.float32)
    negmask = np.where(i[:, None] >= i[None, :], 0.0, -30000.0).astype(np.float32)
    strict = (i[:, None] > i[None, :]).astype(np.float32)
    return tri_le, negmask, strict


def gdn_phase(P, cx, sc, convw_ap, alog_ap, dtb_ap, normg_ap, cst, L, hs=(0, 1, 2, 3), half=False, C=128):
    NCH = 512 // C
    H = len(hs)
    h0 = hs[0]
    HW_ = H * 128
    cl = list(hs) + [4 + h_ for h_ in hs] + [8 + h_ for h_ in hs]
    V_ = lambda fn, r, w, **k: P.op('dve', fn, reads=r, writes=w, **k)
    A_ = lambda fn, r, w, **k: P.op('act', fn, reads=r, writes=w, **k)
    G_ = lambda fn, r, w, **k: P.op('pool', fn, reads=r, writes=w, **k)
    T_ = lambda fn, r, w, **k: P.op('pe', fn, reads=r, writes=w, **k)
    fl = lambda b: b[:, :, :].rearrange("p h c -> p (h c)")
    tri = P.sb('tri', [C, C]); P.dma('sp', tri[:, :], cst['tri_le'], writes=[tri])
    nmk = P.sb('nmk', [C, H, C])
    smk = P.sb('smk', [C, H, C])
    for h in range(H):
        P.dma('sp', nmk[:, h, :], cst['negmask'], writes=[nmk], keep=True)
        P.dma('sp', smk[:, h, :], cst['strict'], writes=[smk], keep=True)
    ones = P.sb('ones', [128, 128]); G_(lambda E: E.memset(ones[:, :], 1.0), [], [ones])
    cw = P.sb('cw', [128, 4, 12])
    for i in range(4):
        P.dma('sp', cw[:, i, :], convw_ap[i, :].rearrange("(c p) -> p c", p=128), writes=[cw], keep=True,
              allow_slow_non_contiguous=True)
    al = P.sb('al', [C, 2, 4])
    P.dma('sp', al[:, 0, :], alog_ap.partition_broadcast(C), writes=[al], keep=True)
    P.dma('sp', al[:, 1, :], dtb_ap.partition_broadcast(C), writes=[al], keep=True)
    A_(lambda E: E.activation(out=al[:, 0, :], in_=al[:, 0, :], func=AF.Exp), [al], [al])
    ng = P.sb('ng', [C, H, 128])
    for h in range(H):
        P.dma('sp', ng[:, h, :], normg_ap.partition_broadcast(C), writes=[ng], keep=True)
    S = P.sb('S', [128, H, 128]); G_(lambda E: E.memset(S[:, :, :], 0.0), [], [S])
    Sb = P.sb('Sb', [128, H, 128], BF16); G_(lambda E: E.memset(Sb[:, :, :], 0.0), [], [Sb])
    X = P.sb('X', [128, 3 * H, 515])
    Y = P.sb('Y', [128, 3 * H, 512])
    sq = P.sb('sqg', [128, 512])
    rt = P.sb('rtg', [128, 512])
    qT = P.sb('qTg', [128, H, 512], BF16)
    kT = P.sb('kTg', [128, H, 512], BF16)
    vT = P.sb('vTg', [128, H, 512], BF16)
    gab = P.sb('gabt', [C, NCH, 8])
    gz = P.sb('gzt', [C, NCH, HW_])
    g = P.sb('g', [C, NCH, H]); be = P.sb('be', [C, NCH, H]); nbe = P.sb('nbe', [C, NCH, H])
    Gc = P.sb('Gc', [C, NCH, H]); nGc = P.sb('nGc', [C, NCH, H]); eG = P.sb('eG', [C, NCH, H]); beG = P.sb('beG', [C, NCH, H])
    e2 = P.sb('e2', [C, NCH, H]); gl = P.sb('gl', [128, NCH, H]); glc = P.sb('glc', [128, NCH, H])
    def dbl(name, shape, dt=F32):
        return [P.sb(name, shape, dt) for _ in range(2)]
    ktok = dbl('ktok', [C, H, 128], BF16); vtok = dbl('vtok', [C, H, 128], BF16)
    dg = dbl('dg', [C, H, C]); Dm = dbl('Dm', [C, H, C]); Ds = dbl('Ds', [C, H, C])
    N0 = dbl('N0', [C, H, C], NDT); NT0 = dbl('NT0', [C, H, C], NDT); itr = dbl('itr', [C, H, C], BF16); itT = dbl('itT', [C, H, C], BF16)
    nb = {'X': dbl('nX', [C, H, C], NDT), 'XT': dbl('nXT', [C, H, C], NDT), 'TT': dbl('nTT', [C, H, C], NDT)}
    for kk_ in nb:
        for b_ in nb[kk_]:
            G_(lambda E, b_=b_: E.memset(b_[:, :, :], 0.0), [], [b_])
    if C == 128:
        Nd = dbl('Nd', [C, H, C], NDT); NTd = dbl('NTd', [C, H, C], NDT); No = dbl('No', [C, H, C], NDT)
        Td = dbl('Td', [C, H, C], NDT); Y1 = dbl('Y1', [C, H, C], NDT)
        bdm = P.sb('bdm', [C, H, C], NDT)
        bdf = P.sb('bdf', [C, H, C])
        for h in range(H):
            P.dma('sp', bdf[:, h, :], cst['bd'], writes=[bdf], keep=True)
        V_(lambda E: E.tensor_copy(bdm[:, :, :], bdf[:, :, :]), [bdf], [bdm])
    TTb = dbl('TTb', [C, H, C], BF16); kbg = dbl('kbg', [C, H, 128], BF16); vb = dbl('vb', [C, H, 128], BF16)
    kd = dbl('kd', [C, H, 128], BF16); nWT = dbl('nWT', [128, H, C], BF16); vn = dbl('vn', [C, H, 128], BF16)
    oi = dbl('oi', [C, H, 128]); oo = dbl('oog', [C, H, 128]); osq = dbl('osq', [C, H, 128]); rr = dbl('rrg', [C, H])
    yc = dbl('yc', [C, H, 128], BF16); ycs = dbl('ycs', [128, H, C], BF16)
    if half:
        f_ = [P.ps('gpF', [128, 512]) for _ in range(3)]
        psA = [f_[0], f_[1]]
        psN = [f_[0], f_[1], f_[2]]
    else:
        psA = [P.ps('gpA', [128, 512]) for _ in range(2)]
        psN = [P.ps('gpN', [128, 512]) for _ in range(3)]
    psB = [P.ps('gpB', [128, 1024], BF16) for _ in range(1)]
    psC = [f_[2], f_[1]] if half else [P.ps('gpC', [128, 512]) for _ in range(2)]
    for t in range(L // 512):
        c0 = t * 512
        if t == 0:
            G_(lambda E: E.memset(X[:, :, 0:3], 0.0), [], [X])
            for i_, c_ in enumerate(cl):
                P.dma('sp', X[:, i_, 3:515], sc['gqkv'][c_ * 128:(c_ + 1) * 128, 0:512], reads=[sc['gqkv']], writes=[X], keep=True)
        else:
            for i_, c_ in enumerate(cl):
                P.dma('sp', X[:, i_, :], sc['gqkv'][c_ * 128:(c_ + 1) * 128, c0 - 3:c0 + 512], reads=[sc['gqkv']], writes=[X], keep=(i_ > 0))
        P.dma('sp', gab[:, :, :], sc['gab'][c0:c0 + 512, :].rearrange("(c p) n -> p c n", p=C), reads=[sc['gab']], writes=[gab])
        P.dma('sp', gz[:, :, :], sc['gz'][c0:c0 + 512, h0 * 128:h0 * 128 + HW_].rearrange("(c p) n -> p c n", p=C), reads=[sc['gz']], writes=[gz])
        for c, cg in enumerate(cl):
            V_(lambda E, c=c, cg=cg: E.tensor_scalar(Y[:, c, :], X[:, c, 0:512], cw[:, 0, cg:cg + 1], None, ALU.mult), [X, cw], [Y], keep=(c > 0))
            for i in range(1, 4):
                V_(lambda E, c=c, cg=cg, i=i: E.scalar_tensor_tensor(Y[:, c, :], X[:, c, i:i + 512], cw[:, i, cg:cg + 1], Y[:, c, :],
                                                             ALU.mult, ALU.add), [X, cw, Y], [Y], keep=True)
        A_(lambda E: E.activation(out=Y[:, :, :], in_=Y[:, :, :], func=AF.Silu), [Y], [Y])
        for c in range(2 * H):
            G_(lambda E, c=c: E.tensor_tensor(sq[:, :], Y[:, c, :], Y[:, c, :], ALU.mult), [Y], [sq])
            pa = psA[c % 2]
            T_(lambda E, pa=pa: E.matmul(pa[:, :], ones[:, :], sq[:, :], start=True, stop=True), [ones, sq], [pa])
            V_(lambda E, pa=pa: E.tensor_scalar(rt[:, :], pa[:, :], 1e-6, None, ALU.add), [pa], [rt])
            A_(lambda E: E.activation(out=rt[:, :], in_=rt[:, :], func=AF.Sqrt), [rt], [rt])
            V_(lambda E: E.reciprocal(rt[:, :], rt[:, :]), [rt], [rt])
            dst = qT if c < H else kT
            scl = (128 ** -0.5) if c < H else 1.0
            V_(lambda E, c=c, dst=dst, scl=scl: E.scalar_tensor_tensor(dst[:, c % H, :], Y[:, c, :], scl, rt[:, :], ALU.mult, ALU.mult),
               [Y, rt], [dst], keep=True)
        A_(lambda E: E.copy(vT[:, :, :], Y[:, 2 * H:3 * H, :]), [Y], [vT])
        V_(lambda E: E.tensor_tensor(g[:, :, :], gab[:, :, h0:h0 + H], al[:, 1:2, h0:h0 + H].to_broadcast([C, NCH, H]), ALU.add), [gab, al], [g])
        A_(lambda E: E.activation(out=g[:, :, :], in_=g[:, :, :], func=AF.Exp), [g], [g])
        V_(lambda E: E.tensor_scalar(g[:, :, :], g[:, :, :], 1.0, None, ALU.add), [g], [g])
        A_(lambda E: E.activation(out=g[:, :, :], in_=g[:, :, :], func=AF.Ln), [g], [g])
        V_(lambda E: E.tensor_tensor(g[:, :, :], g[:, :, :], al[:, 0:1, h0:h0 + H].to_broadcast([C, NCH, H]), ALU.mult), [g, al], [g])
        V_(lambda E: E.tensor_scalar(g[:, :, :], g[:, :, :], -1.0, None, ALU.mult), [g], [g])
        A_(lambda E: E.activation(out=be[:, :, :], in_=gab[:, :, 4 + h0:4 + h0 + H], func=AF.Sigmoid), [gab], [be])
        V_(lambda E: E.tensor_scalar(nbe[:, :, :], be[:, :, :], -1.0, None, ALU.mult), [be], [nbe])
        pa = psA[0]
        T_(lambda E, pa=pa: E.matmul(pa[0:C, 0:NCH * H], tri[:, :], fl(g), start=True, stop=True), [tri, g], [pa])
        V_(lambda E, pa=pa: E.tensor_copy(fl(Gc), pa[0:C, 0:NCH * H]), [pa], [Gc])
        pb = psA[1]
        T_(lambda E, pb=pb: E.matmul(pb[:, 0:NCH * H], ones[0:C, :], fl(g), start=True, stop=True), [ones, g], [pb])
        V_(lambda E, pb=pb: E.tensor_copy(fl(glc), pb[:, 0:NCH * H]), [pb], [glc])
        V_(lambda E: E.tensor_scalar(nGc[:, :, :], Gc[:, :, :], -1.0, None, ALU.mult), [Gc], [nGc])
        A_(lambda E: E.activation(out=eG[:, :, :], in_=Gc[:, :, :], func=AF.Exp), [Gc], [eG])
        A_(lambda E: E.activation(out=gl[:, :, :], in_=glc[:, :, :], func=AF.Exp), [glc], [gl])
        V_(lambda E: E.tensor_tensor(e2[:, :, :], glc[0:C, :, :], Gc[:, :, :], ALU.subtract), [glc, Gc], [e2])
        A_(lambda E: E.activation(out=e2[:, :, :], in_=e2[:, :, :], func=AF.Exp), [e2], [e2])
        V_(lambda E: E.tensor_tensor(beG[:, :, :], be[:, :, :], eG[:, :, :], ALU.mult), [be, eG], [beG])
        for ci in range(NCH):
            p = ci % 2
            cs = slice(ci * C, (ci + 1) * C)
            pT_ = psB[0]
            for h in range(H):
                T_(lambda E, h=h, cs=cs, pT_=pT_: E.transpose(pT_[0:C, h * 128:(h + 1) * 128], kT[:, h, cs], cx.ident[:, :]),
                   [kT, cx.ident], [pT_], inc=(h == H - 1))
            V_(lambda E, p=p, pT_=pT_: E.tensor_copy(fl(ktok[p]), pT_[0:C, 0:HW_]), [pT_], [ktok[p]])
            for h in range(H):
                T_(lambda E, h=h, cs=cs, pT_=pT_: E.transpose(pT_[0:C, h * 128:(h + 1) * 128], vT[:, h, cs], cx.ident[:, :]),
                   [vT, cx.ident], [pT_], inc=(h == H - 1))
            A_(lambda E, p=p, pT_=pT_: E.copy(fl(vtok[p]), pT_[0:C, 0:HW_]), [pT_], [vtok[p]])
            for h in range(H):
                G_(lambda E, h=h, p=p, ci=ci: E.tensor_scalar(dg[p][:, h, :], cx.identf[0:C, 0:C], nGc[:, ci, h:h + 1], None, ALU.mult),
                   [cx.identf, nGc], [dg[p]], keep=(h > 0))
            pd = psC[0]
            T_(lambda E, p=p, pd=pd: E.matmul(pd[0:C, 0:H * C], ones[0:C, 0:C], fl(dg[p]), start=True, stop=True), [ones, dg[p]], [pd])
            for h in range(H):
                V_(lambda E, h=h, p=p, ci=ci, pd=pd: E.scalar_tensor_tensor(Dm[p][:, h, :], pd[0:C, h * C:(h + 1) * C], Gc[:, ci, h:h + 1],
                                                                          nmk[:, h, :], ALU.add, ALU.add), [pd, Gc, nmk], [Dm[p]], keep=(h > 0))
            A_(lambda E, p=p: E.activation(out=Dm[p][:, :, :], in_=Dm[p][:, :, :], func=AF.Exp), [Dm[p]], [Dm[p]])
            G_(lambda E, p=p: E.tensor_tensor(Ds[p][:, :, :], Dm[p][:, :, :], smk[:, :, :], ALU.mult), [Dm[p], smk], [Ds[p]])
            pk = psC[1]
            for h in range(H):
                T_(lambda E, h=h, cs=cs, pk=pk: E.matmul(pk[0:C, h * C:(h + 1) * C], kT[:, h, cs], kT[:, h, cs], start=True, stop=True),
                   [kT], [pk], inc=False)
            for h in range(H):
                T_(lambda E, h=h, cs=cs, pk=pk: E.matmul(pk[0:C, (H + h) * C:(H + h + 1) * C], qT[:, h, cs], kT[:, h, cs], start=True, stop=True),
                   [kT, qT], [pk], inc=(h == H - 1))
            for h in range(H):
                V_(lambda E, h=h, p=p, ci=ci, pk=pk: E.scalar_tensor_tensor(N0[p][:, h, :], pk[0:C, h * C:(h + 1) * C], nbe[:, ci, h:h + 1],
                                                                          Ds[p][:, h, :], ALU.mult, ALU.mult), [pk, nbe, Ds[p]], [N0[p]], keep=(h > 0))
            V_(lambda E, p=p, pk=pk: E.tensor_tensor(fl(itr[p]), pk[0:C, H * C:2 * H * C], fl(Dm[p]), ALU.mult), [pk, Dm[p]], [itr[p]])
            pn = psB[0]
            for h in range(H):
                T_(lambda E, h=h, p=p, pn=pn: E.transpose(pn[0:C, h * C:(h + 1) * C], N0[p][:, h, :], cx.ident[0:C, 0:C]),
                   [N0[p], cx.ident], [pn], inc=(h == H - 1))
            A_(lambda E, p=p, pn=pn: E.copy(fl(NT0[p]), pn[0:C, 0:H * C]), [pn], [NT0[p]])
            pT_ = psB[0]
            for h in range(H):
                T_(lambda E, h=h, p=p, pT_=pT_: E.transpose(pT_[0:C, h * C:(h + 1) * C], itr[p][:, h, :], cx.ident[0:C, 0:C]),
                   [itr[p], cx.ident], [pT_], inc=(h == H - 1))
            A_(lambda E, p=p, pT_=pT_: E.copy(fl(itT[p]), pT_[0:C, 0:H * C]), [pT_], [itT[p]])
            if C == 128:
                G_(lambda E, p=p: E.tensor_tensor(Nd[p][:, :, :], N0[p][:, :, :], bdm[:, :, :], ALU.mult), [N0[p], bdm], [Nd[p]])
                G_(lambda E, p=p: E.tensor_tensor(NTd[p][:, :, :], NT0[p][:, :, :], bdm[:, :, :], ALU.mult), [NT0[p], bdm], [NTd[p]])
                V_(lambda E, p=p: E.tensor_tensor(No[p][:, :, :], N0[p][:, :, :], Nd[p][:, :, :], ALU.subtract), [N0[p], Nd[p]], [No[p]])
                TTd = neumann_TT(P, cx, Nd[p], NTd[p], nb, psN, H, C, nstage=5)
                pT_ = psB[0]
                for h in range(H):
                    T_(lambda E, h=h, pT_=pT_, TTd=TTd: E.transpose(pT_[0:C, h * C:(h + 1) * C], TTd[:, h, :], cx.ident[0:C, 0:C]),
                       [TTd, cx.ident], [pT_], inc=(h == H - 1))
                A_(lambda E, p=p, pT_=pT_: E.copy(fl(Td[p]), pT_[0:C, 0:H * C]), [pT_], [Td[p]])
                py1 = psN[0]
                for h in range(H):
                    T_(lambda E, h=h, p=p, py1=py1, TTd=TTd: E.matmul(py1[0:C, h * C:(h + 1) * C], No[p][:, h, :], TTd[:, h, :], start=True, stop=True),
                       [No[p], TTd], [py1], inc=(h == H - 1))
                V_(lambda E, p=p, py1=py1: E.tensor_copy(fl(Y1[p]), py1[0:C, 0:H * C]), [py1], [Y1[p]])
                pc2 = psN[1]
                for h in range(H):
                    T_(lambda E, h=h, p=p, pc2=pc2: E.matmul(pc2[0:C, h * C:(h + 1) * C], Td[p][:, h, :], Y1[p][:, h, :], start=True, stop=True),
                       [Td[p], Y1[p]], [pc2], inc=(h == H - 1))
                V_(lambda E, p=p, pc2=pc2, TTd=TTd: E.tensor_tensor(fl(TTb[p]), pc2[0:C, 0:H * C], fl(TTd), ALU.add), [pc2, TTd], [TTb[p]])
                TT = TTb[p]
            else:
                TT = neumann_TT(P, cx, N0[p], NT0[p], nb, psN, H, C, nstage=5)
                A_(lambda E, p=p, TT=TT: E.copy(TTb[p][:, :, :], TT[:, :, :]), [TT], [TTb[p]])
            if t == 0 and ci == 0:
                P.dbg('qT', qT, qT[:, :, :], [128, H, 512], BF16)
                P.dbg('kT', kT, kT[:, :, :], [128, H, 512], BF16)
                P.dbg('g', g, g[:, :, :], [C, NCH, H])
                P.dbg('Gc', Gc, Gc[:, :, :], [C, NCH, H])
                P.dbg('be', be, be[:, :, :], [C, NCH, H])
                P.dbg('Dm', Dm[p], Dm[p][:, :, :], [C, H, C])
                P.dbg('N0', N0[p], N0[p][:, :, :], [C, H, C])
                P.dbg('NT0', NT0[p], NT0[p][:, :, :], [C, H, C])
                P.dbg('TT', TT, TT[:, :, :], [C, H, C])
                P.dbg('ktok', ktok[p], ktok[p][:, :, :], [C, H, 128], BF16)
            for h in range(H):
                V_(lambda E, h=h, p=p, ci=ci: E.tensor_scalar(kbg[p][:, h, :], ktok[p][:, h, :], beG[:, ci, h:h + 1], None, ALU.mult),
                   [ktok[p], beG], [kbg[p]], keep=(h > 0))
                A_(lambda E, h=h, p=p, ci=ci: E.activation(out=vb[p][:, h, :], in_=vtok[p][:, h, :], func=AF.Copy, scale=be[:, ci, h:h + 1]),
                   [vtok[p], be], [vb[p]], keep=(h > 0))
                A_(lambda E, h=h, p=p, ci=ci: E.activation(out=kd[p][:, h, :], in_=ktok[p][:, h, :], func=AF.Copy, scale=e2[:, ci, h:h + 1]),
                   [ktok[p], e2], [kd[p]], keep=(h > 0))
            pw = psC[0]
            for h in range(H):
                T_(lambda E, h=h, p=p, pw=pw: E.matmul(pw[:, h * C:(h + 1) * C], kbg[p][:, h, :], TTb[p][:, h, :], start=True, stop=True),
                   [kbg[p], TTb[p]], [pw], inc=(h == H - 1))
            A_(lambda E, p=p, pw=pw: E.mul(fl(nWT[p]), pw[:, 0:H * C], -1.0), [pw], [nWT[p]])
            pv = psA[0]
            for h in range(H):
                T_(lambda E, h=h, p=p, pv=pv: E.matmul(pv[0:C, h * 128:(h + 1) * 128], TTb[p][:, h, :], vb[p][:, h, :], start=True, stop=False),
                   [TTb[p], vb[p]], [pv], inc=False)
                T_(lambda E, h=h, p=p, pv=pv: E.matmul(pv[0:C, h * 128:(h + 1) * 128], nWT[p][:, h, :], Sb[:, h, :], start=False, stop=True),
                   [nWT[p], Sb], [pv], inc=(h == H - 1))
            V_(lambda E, p=p, pv=pv: E.tensor_copy(fl(vn[p]), pv[0:C, 0:HW_]), [pv], [vn[p]])
            po = psA[1]
            for h in range(H):
                T_(lambda E, h=h, p=p, po=po: E.matmul(po[0:C, h * 128:(h + 1) * 128], itT[p][:, h, :], vn[p][:, h, :], start=True, stop=True),
                   [itT[p], vn[p]], [po], inc=(h == H - 1))
            A_(lambda E, p=p, po=po: E.copy(fl(oi[p]), po[0:C, 0:HW_]), [po], [oi[p]])
            pq = psC[1]
            for h in range(H):
                T_(lambda E, h=h, cs=cs, pq=pq: E.matmul(pq[0:C, h * 128:(h + 1) * 128], qT[:, h, cs], Sb[:, h, :], start=True, stop=True),
                   [qT, Sb], [pq], inc=(h == H - 1))
            for h in range(H):
                V_(lambda E, h=h, p=p, ci=ci, pq=pq: E.scalar_tensor_tensor(oo[p][:, h, :], pq[0:C, h * 128:(h + 1) * 128], eG[:, ci, h:h + 1],
                                                                          oi[p][:, h, :], ALU.mult, ALU.add), [pq, eG, oi[p]], [oo[p]], keep=(h > 0))
            ps_ = psC[0]
            for h in range(H):
                T_(lambda E, h=h, p=p, ps_=ps_: E.matmul(ps_[:, h * 128:(h + 1) * 128], kd[p][:, h, :], vn[p][:, h, :], start=True, stop=True),
                   [kd[p], vn[p]], [ps_], inc=(h == H - 1))
            for h in range(H):
                V_(lambda E, h=h, ci=ci, ps_=ps_: E.scalar_tensor_tensor(S[:, h, :], S[:, h, :], gl[:, ci, h:h + 1], ps_[:, h * 128:(h + 1) * 128],
                                                                       ALU.mult, ALU.add), [S, gl, ps_], [S], keep=(h > 0))
            A_(lambda E: E.copy(Sb[:, :, :], S[:, :, :]), [S], [Sb])
            if t == 0 and ci == 0:
                P.dbg('vn', vn[p], vn[p][:, :, :], [C, H, 128], BF16)
                P.dbg('oo', oo[p], oo[p][:, :, :], [C, H, 128])
                P.dbg('S', S, S[:, :, :], [128, H, 128])
            G_(lambda E, p=p: E.tensor_tensor(osq[p][:, :, :], oo[p][:, :, :], oo[p][:, :, :], ALU.mult), [oo[p]], [osq[p]])
            V_(lambda E, p=p: E.tensor_reduce(rr[p][:, :], osq[p][:, :, :], AX.X, ALU.add), [osq[p]], [rr[p]])
            V_(lambda E, p=p: E.tensor_scalar(rr[p][:, :], rr[p][:, :], 1.0 / 128, 1e-6, ALU.mult, ALU.add), [rr[p]], [rr[p]])
            A_(lambda E, p=p: E.activation(out=rr[p][:, :], in_=rr[p][:, :], func=AF.Ln), [rr[p]], [rr[p]])
            A_(lambda E, p=p: E.activation(out=rr[p][:, :], in_=rr[p][:, :], func=AF.Exp, scale=-0.5), [rr[p]], [rr[p]])
            for h in range(H):
                V_(lambda E, h=h, p=p: E.scalar_tensor_tensor(oo[p][:, h, :], oo[p][:, h, :], rr[p][:, h:h + 1], ng[:, h, :], ALU.mult, ALU.mult),
                   [oo[p], rr[p], ng], [oo[p]], keep=(h > 0))
            V_(lambda E, p=p, ci=ci: E.tensor_tensor(fl(yc[p]), fl(oo[p]), gz[:, ci, :], ALU.mult), [oo[p], gz], [yc[p]])
            pT_ = psB[0]
            for h in range(H):
                T_(lambda E, h=h, p=p, pT_=pT_: E.transpose(pT_[:, h * C:(h + 1) * C], yc[p][:, h, :], cx.ident[0:C, 0:C]),
                   [yc[p], cx.ident], [pT_], inc=(h == H - 1))
            A_(lambda E, p=p, pT_=pT_: E.copy(fl(ycs[p]), pT_[:, 0:H * C]), [pT_], [ycs[p]])
            P.dma('sp', sc['ycT'][h0 * 128:h0 * 128 + HW_, c0 + ci * C:c0 + (ci + 1) * C].rearrange("(h p) t -> p h t", p=128), ycs[p][:, :, :],
                  reads=[ycs[p]], writes=[sc['ycT']], keep=True)


def rw_consts(C=64):
    i = np.arange(C)
    su = (i[:, None] < i[None, :]).astype(np.float32)
    iu = (i[:, None] <= i[None, :]).astype(np.float32)
    sl = (i[:, None] > i[None, :]).astype(np.float32)
    bo = np.zeros((128, 128), np.float32)
    bo[:64, :64] = 1
    bo[64:, 64:] = 1
    hs = np.zeros((128, 2), np.float32)
    hs[:64, 0] = 1
    hs[64:, 1] = 1
    rm = np.ones((128, 512), np.float32)
    rm[:, ::C] = 0
    return {'nsu': -su, 'su': su, 'iu': iu, 'niu': -iu, 'nsl': -sl, 'bo': bo, 'hs': hs, 'rm': rm}


def rwkv_phase(P, cx, sc, w, cst, L, js=(0, 1, 2, 3), half=False, pipe=False, C=128):
    NV = 64
    NJ = len(js)
    H = 2 * NJ
    TW = 256
    NB = 2 if C == 64 else 1
    H2 = NB * H
    cl = list(js) + [4 + j for j in js] + [8 + j for j in js] + [12, 13]
    NCk = len(cl)
    iR, iK, iV, iW, iG = 0, NJ, 2 * NJ, 3 * NJ, 3 * NJ + 1
    g0, g1 = js[0] * 128, (js[-1] + 1) * 128
    GW = g1 - g0
    V_ = lambda fn, r, w_, **k: P.op('dve', fn, reads=r, writes=w_, **k)
    A_ = lambda fn, r, w_, **k: P.op('act', fn, reads=r, writes=w_, **k)
    G_ = lambda fn, r, w_, **k: P.op('pool', fn, reads=r, writes=w_, **k)
    T_ = lambda fn, r, w_, **k: P.op('pe', fn, reads=r, writes=w_, **k)
    fl = lambda b: b[:, :, :].rearrange("p h c -> p (h c)")
    msk = {}
    for n in ('nsu', 'su', 'iu', 'niu', 'nsl'):
        m = P.sb('m' + n, [C, H2, C], BF16)
        for h in range(H2):
            P.dma('pool', m[:, h, :], cst[n], writes=[m], keep=True)
        msk[n] = m
    bo = P.sb('bo', [128, 128]); P.dma('sp', bo[:, :], cst['bo'], writes=[bo])
    hsf = P.sb('hsf', [128, 2]); P.dma('sp', hsf[:, :], cst['hs'], writes=[hsf])
    rm = P.sb('rm', [128, TW]); P.dma('sp', rm[:, :], cst['rm'][:, 0:TW], writes=[rm])
    mu = P.sb('mu', [128, 14]); P.dma('sp', mu[:, :], w['mu'].rearrange("(c p) -> p c", p=128), writes=[mu], allow_slow_non_contiguous=True)
    pv = P.sb('pvec', [128, 7, 4])
    for i, n in enumerate(('w0', 'a0', 'k_k', 'k_a', 'r_k')):
        P.dma('sp', pv[:, i, :], w[n].rearrange("(c p) -> p c", p=128), writes=[pv], keep=True, allow_slow_non_contiguous=True)
    V_(lambda E: E.tensor_scalar(pv[:, 5, :], pv[:, 3, :], -1.0, 1.0, ALU.mult, ALU.add), [pv], [pv])
    wupf = P.sb('wupf', [128, 512]); wup = P.sb('wup', [128, 512], BF16)
    P.dma('sp', wupf[0:64, :], w['w_up'], writes=[wupf], keep=True)
    P.dma('sp', wupf[64:128, :], w['a_up'], writes=[wupf], keep=True)
    V_(lambda E: E.tensor_copy(wup[:, :], wupf[:, :]), [wupf], [wup])
    gupf = P.sb('gupf', [128, 512]); gup = P.sb('gup', [128, 512], BF16)
    P.dma('sp', gupf[:, :], w['g_up'], writes=[gupf])
    V_(lambda E: E.tensor_copy(gup[:, :], gupf[:, :]), [gupf], [gup])
    epst = P.sb('epst', [128, 1]); G_(lambda E: E.memset(epst[:, :], 1e-6), [], [epst])
    lng = P.sb('rlng', [C, GW]); lnb = P.sb('rlnb', [C, GW])
    P.dma('sp', lng[:, :], w['ln_g'][g0:g1].partition_broadcast(C), writes=[lng])
    P.dma('sp', lnb[:, :], w['ln_b'][g0:g1].partition_broadcast(C), writes=[lnb])
    M = P.sb('M', [128, NJ, 64]); G_(lambda E: E.memset(M[:, :, :], 0.0), [], [M])
    Mb = P.sb('Mb', [128, NJ, 64], BF16); G_(lambda E: E.memset(Mb[:, :, :], 0.0), [], [Mb])
    X = P.sb('rX', [128, NCk, TW + 1])
    Y = P.sb('rY', [128, NCk, TW])
    lw = P.sb('lw', [128, NJ, TW]); a_ = P.sb('ra', [128, NJ, TW])
    Gc = P.sb('rGc', [128, NJ, TW]); eP = P.sb('eP', [128, NJ, TW]); eN = P.sb('eN', [128, NJ, TW]); ePe = P.sb('ePe', [128, NJ, TW])
    kk = P.sb('kk', [128, NJ, TW]); t1 = P.sb('t1', [128, NJ, TW]); t2 = P.sb('t2', [128, TW])
    At = P.sb('At', [128, NJ, TW], BF16); Bt = P.sb('Bt', [128, NJ, TW], BF16); Kt = P.sb('Kt', [128, NJ, TW], BF16)
    Rt = P.sb('Rt', [128, NJ, TW], BF16); Kh = P.sb('Kh', [128, NJ, TW], BF16); Bh = P.sb('Bh', [128, NJ, TW], BF16)
    vT = P.sb('rvT', [128, NJ, TW], BF16); rkb = P.sb('rkb', [128, NJ, TW])
    AtZ = P.sb('AtZ', [128, NJ, 2, TW], BF16)
    BtZ = P.sb('BtZ', [128, NJ, 2, TW], BF16)
    RtZ = P.sb('RtZ', [128, NJ, 2, TW], BF16)
    MbZ = P.sb('MbZ', [128, NJ, 2, 64], BF16)
    G_(lambda E: E.memset(MbZ[:, :, :, :], 0.0), [], [MbZ])
    twb = P.sb('twb', [128, TW], BF16); sgb = P.sb('sgb', [128, TW], BF16)

    def dbl(name, shape, dt=F32):
        return [P.sb(name, shape, dt) for _ in range(2)]
    vtok = dbl('rvtok', [C, NB, GW], BF16); khtok = dbl('khtok', [C, NB, GW], BF16); nbhtok = dbl('nbhtok', [C, NB, GW], BF16)
    N0 = dbl('rN0', [C, H2, C], NDT); NT0 = dbl('rNT0', [C, H2, C], NDT)
    AKm = dbl('AKm', [C, H2, C], BF16); RKm = dbl('RKm', [C, H2, C], BF16); RBm = dbl('RBm', [C, H2, C], BF16)
    nb = {'X': dbl('rnX', [C, H2, C], NDT), 'XT': dbl('rnXT', [C, H2, C], NDT), 'TT': dbl('rnTT', [C, H2, C], NDT)}
    if C == 128:
        sgl = lambda name: [P.sb(name, [C, H2, C], NDT)] * 2
        Nd = sgl('rNd'); NTd = sgl('rNTd'); No = sgl('rNo'); Td = sgl('rTd'); Y1 = sgl('rY1')
        bdm = P.sb('rbdm', [C, H2, C], NDT)
        for h in range(H2):
            P.dma('pool', bdm[:, h, :], cst['bd'], writes=[bdm], keep=True)
    TTb = dbl('rTTb', [C, H2, C], BF16); RHb = dbl('RHb', [C, H, NV], BF16); Ub = dbl('Ub', [C, H, NV], BF16)
    ysb = dbl('ysb', [C, H, NV]); ysq = dbl('ysq', [C, H, NV]); st = dbl('rst', [C, 4, H]); bon = dbl('bon', [C, H])
    yab = dbl('yab', [C, GW]); yas = dbl('yas', [128, NJ, C], BF16)
    if half:
        f_ = [P.ps('rpF', [128, 512]) for _ in range(3)]
        if pipe:
            psA = [f_[1], f_[2]]
            psN = [f_[0], f_[0], f_[0]]
            psC = [f_[0], f_[0]]
        else:
            psA = [f_[0], f_[1]]
            psN = [f_[0], f_[1], f_[2]]
            psC = [f_[2], f_[1]]
    else:
        psA = [P.ps('rpA', [128, 512]) for _ in range(2)]
        psN = [P.ps('rpN', [128, 512]) for _ in range(3)]
        psC = [P.ps('rpC', [128, 512]) for _ in range(2)]
    psB = P.ps('rpB', [128, 1024], BF16)
    NEG = -math.exp(-0.5)
    for t in range(L // TW):
        c0 = t * TW
        if t == 0:
            G_(lambda E: E.memset(X[:, :, 0:1], 0.0), [], [X])
            for i_, c_ in enumerate(cl):
                P.dma('sp', X[:, i_, 1:TW + 1], sc['hr'][c_ * 128:(c_ + 1) * 128, 0:TW], reads=[sc['hr']], writes=[X], keep=True)
        else:
            for i_, c_ in enumerate(cl):
                P.dma('sp', X[:, i_, :], sc['hr'][c_ * 128:(c_ + 1) * 128, c0 - 1:c0 + TW], reads=[sc['hr']], writes=[X], keep=(i_ > 0))
        G_(lambda E: E.tensor_tensor(Y[:, :, :], X[:, :, 0:TW], X[:, :, 1:TW + 1], ALU.subtract), [X], [Y])
        for c, cg in enumerate(cl):
            V_(lambda E, c=c, cg=cg: E.scalar_tensor_tensor(Y[:, c, :], Y[:, c, :], mu[:, cg:cg + 1], X[:, c, 1:TW + 1], ALU.mult, ALU.add),
               [Y, mu, X], [Y], keep=True)
        A_(lambda E: E.activation(out=twb[0:64, :], in_=Y[0:64, iW, :], func=AF.Tanh), [Y], [twb])
        A_(lambda E: E.copy(twb[64:128, :], Y[64:128, iW, :]), [Y], [twb], keep=True)
        A_(lambda E: E.activation(out=sgb[:, :], in_=Y[:, iG, :], func=AF.Sigmoid), [Y], [sgb])
        for jl, j in enumerate(js):
            pa = psA[jl % 2]
            T_(lambda E, j=j, pa=pa: E.matmul(pa[:, 0:TW], wup[0:64, j * 128:(j + 1) * 128], twb[0:64, :], start=True, stop=True), [wup, twb], [pa])
            A_(lambda E, j=j, jl=jl, pa=pa: E.activation(out=lw[:, jl, :], in_=pa[:, 0:TW], func=AF.Sigmoid, bias=pv[:, 0, j:j + 1]), [pa, pv], [lw], keep=(jl > 0))
            pb = psC[jl % 2]
            T_(lambda E, j=j, pb=pb: E.matmul(pb[:, 0:TW], wup[64:128, j * 128:(j + 1) * 128], twb[64:128, :], start=True, stop=True), [wup, twb], [pb])
            A_(lambda E, j=j, jl=jl, pb=pb: E.activation(out=a_[:, jl, :], in_=pb[:, 0:TW], func=AF.Sigmoid, bias=pv[:, 1, j:j + 1]), [pb, pv], [a_], keep=(jl > 0))
        A_(lambda E: E.mul(lw[:, :, :], lw[:, :, :], NEG), [lw], [lw])
        for j in range(NJ):
            V_(lambda E, j=j: E.tensor_tensor_scan(Gc[:, j, :], rm[:, :], lw[:, j, :], 0.0, ALU.mult, ALU.add), [rm, lw], [Gc], keep=(j > 0))
        A_(lambda E: E.activation(out=eP[:, :, :], in_=Gc[:, :, :], func=AF.Exp), [Gc], [eP])
        A_(lambda E: E.activation(out=eN[:, :, :], in_=Gc[:, :, :], func=AF.Exp, scale=-1.0), [Gc], [eN])
        V_(lambda E: E.tensor_tensor(t1[:, :, :], Gc[:, :, :], lw[:, :, :], ALU.subtract), [Gc, lw], [t1])
        A_(lambda E: E.activation(out=ePe[:, :, :], in_=t1[:, :, :], func=AF.Exp), [t1], [ePe])
        for j, jg in enumerate(js):
            A_(lambda E, j=j, jg=jg: E.activation(out=kk[:, j, :], in_=Y[:, iK + j, :], func=AF.Copy, scale=pv[:, 2, jg:jg + 1]), [Y, pv], [kk], keep=(j > 0))
        for j in range(NJ):
            G_(lambda E, j=j: E.tensor_tensor(t2[:, :], kk[:, j, :], kk[:, j, :], ALU.mult), [kk], [t2])
            pa = psA[j % 2]
            T_(lambda E, pa=pa: E.matmul(pa[:, 0:TW], bo[:, :], t2[:, :], start=True, stop=True), [bo, t2], [pa])
            A_(lambda E, j=j, pa=pa: E.activation(out=t1[:, j, :], in_=pa[:, 0:TW], func=AF.Sqrt, bias=epst[:, 0:1]), [pa, epst], [t1], keep=(j > 0))
        V_(lambda E: E.reciprocal(t1[:, :, :], t1[:, :, :]), [t1], [t1])
        V_(lambda E: E.tensor_tensor(kk[:, :, :], kk[:, :, :], t1[:, :, :], ALU.mult), [kk, t1], [kk])
        G_(lambda E: E.tensor_tensor(At[:, :, :], kk[:, :, :], ePe[:, :, :], ALU.mult), [kk, ePe], [At])
        V_(lambda E: E.tensor_tensor(kk[:, :, :], kk[:, :, :], a_[:, :, :], ALU.mult), [kk, a_], [kk])
        V_(lambda E: E.tensor_tensor(t1[:, :, :], kk[:, :, :], eN[:, :, :], ALU.mult), [kk, eN], [t1])
        A_(lambda E: E.copy(Bt[:, :, :], t1[:, :, :]), [t1], [Bt])
        V_(lambda E: E.tensor_tensor(Bh[:, :, :].rearrange("p j (c t) -> p j c t", t=C), t1[:, :, :].rearrange("p j (c t) -> p j c t", t=C),
                                     eP[:, :, :].rearrange("p j (c t) -> p j c t", t=C)[:, :, :, C - 1:C].to_broadcast([128, NJ, TW // C, C]), ALU.mult),
           [t1, eP], [Bh])
        for j, jg in enumerate(js):
            A_(lambda E, j=j, jg=jg: E.activation(out=kk[:, j, :], in_=a_[:, j, :], func=AF.Identity, scale=pv[:, 3, jg:jg + 1], bias=pv[:, 5, jg:jg + 1]), [a_, pv], [kk], keep=(j > 0))
        V_(lambda E: E.tensor_tensor(kk[:, :, :], kk[:, :, :], Y[:, iK:iK + NJ, :], ALU.mult), [kk, Y], [kk])
        V_(lambda E: E.tensor_tensor(t1[:, :, :], kk[:, :, :], eN[:, :, :], ALU.mult), [kk, eN], [t1])
        A_(lambda E: E.copy(Kt[:, :, :], t1[:, :, :]), [t1], [Kt])
        V_(lambda E: E.tensor_tensor(Kh[:, :, :].rearrange("p j (c t) -> p j c t", t=C), t1[:, :, :].rearrange("p j (c t) -> p j c t", t=C),
                                     eP[:, :, :].rearrange("p j (c t) -> p j c t", t=C)[:, :, :, C - 1:C].to_broadcast([128, NJ, TW // C, C]), ALU.mult),
           [t1, eP], [Kh])
        G_(lambda E: E.tensor_tensor(Rt[:, :, :], Y[:, iR:iR + NJ, :], eP[:, :, :], ALU.mult), [Y, eP], [Rt])
        V_(lambda E: E.tensor_tensor(kk[:, :, :], kk[:, :, :], Y[:, iR:iR + NJ, :], ALU.mult), [kk, Y], [kk])
        for j, jg in enumerate(js):
            A_(lambda E, j=j, jg=jg: E.activation(out=rkb[:, j, :], in_=kk[:, j, :], func=AF.Copy, scale=pv[:, 4, jg:jg + 1]), [kk, pv], [rkb], keep=(j > 0))
        A_(lambda E: E.copy(vT[:, :, :], Y[:, iV:iV + NJ, :]), [Y], [vT])
        for e_ in range(2):
            G_(lambda E, e_=e_: E.tensor_scalar(AtZ[:, :, e_, :], At[:, :, :], hsf[:, e_:e_ + 1], None, ALU.mult), [At, hsf], [AtZ], keep=(e_ > 0))
            G_(lambda E, e_=e_: E.tensor_scalar(BtZ[:, :, e_, :], Bt[:, :, :], hsf[:, e_:e_ + 1], None, ALU.mult), [Bt, hsf], [BtZ], keep=(e_ > 0))
            A_(lambda E, e_=e_: E.activation(out=RtZ[:, :, e_, :], in_=Rt[:, :, :], func=AF.Copy, scale=hsf[:, e_:e_ + 1]), [Rt, hsf], [RtZ], keep=(e_ > 0))
        def part1(cp):
            p = cp % 2
            css = [slice((cp * NB + k) * C, (cp * NB + k + 1) * C) for k in range(NB)]
            for src, dst, scl in ((vT, vtok[p], 1.0), (Kh, khtok[p], 1.0), (Bh, nbhtok[p], -1.0)):
                for k in range(NB):
                    for j in range(NJ):
                        T_(lambda E, j=j, k=k, src=src: E.transpose(psB[0:C, (k * NJ + j) * 128:(k * NJ + j + 1) * 128], src[:, j, css[k]], cx.ident[:, :]),
                           [src, cx.ident], [psB], inc=(k == NB - 1 and j == NJ - 1))
                A_(lambda E, dst=dst, scl=scl: E.mul(dst[:, :, :].rearrange("p k g -> p (k g)"), psB[0:C, 0:NB * GW], scl), [psB], [dst])
            ops5 = ((Bt, AtZ, 'nsu', NT0[p]), (At, BtZ, 'nsl', N0[p]), (Kt, AtZ, 'su', AKm[p]), (Kt, RtZ, 'iu', RKm[p]), (Bt, RtZ, 'niu', RBm[p]))
            for n_, (la, rb, mk, dst) in enumerate(ops5):
                pp = psC[n_ % 2]
                for k in range(NB):
                    for j in range(NJ):
                        v0 = k * H + 2 * j
                        T_(lambda E, v0=v0, k=k, j=j, la=la, rb=rb, pp=pp: E.matmul(pp[0:C, v0 * C:(v0 + 2) * C], la[:, j, css[k]], rb[:, j, :, css[k]],
                                                                                 start=True, stop=True), [la, rb], [pp], inc=(k == NB - 1 and j == NJ - 1))
                V_(lambda E, dst=dst, pp=pp, mk=mk: E.tensor_tensor(fl(dst), pp[0:C, 0:H2 * C], fl(msk[mk]), ALU.mult), [pp, msk[mk]], [dst])
            if C == 128:
                G_(lambda E: E.tensor_tensor(Nd[p][:, :, :], N0[p][:, :, :], bdm[:, :, :], ALU.mult), [N0[p], bdm], [Nd[p]])
                G_(lambda E: E.tensor_tensor(NTd[p][:, :, :], NT0[p][:, :, :], bdm[:, :, :], ALU.mult), [NT0[p], bdm], [NTd[p]])
                V_(lambda E: E.tensor_tensor(No[p][:, :, :], N0[p][:, :, :], Nd[p][:, :, :], ALU.subtract), [N0[p], Nd[p]], [No[p]])
                TTd = neumann_TT(P, cx, Nd[p], NTd[p], nb, psN, H2, C, nstage=5)
                for h in range(H2):
                    T_(lambda E, h=h: E.transpose(psB[0:C, h * C:(h + 1) * C], TTd[:, h, :], cx.ident[0:C, 0:C]),
                       [TTd, cx.ident], [psB], inc=(h == H2 - 1))
                A_(lambda E: E.copy(fl(Td[p]), psB[0:C, 0:H2 * C]), [psB], [Td[p]])
                py1 = psN[0]
                for h in range(H2):
                    T_(lambda E, h=h: E.matmul(py1[0:C, h * C:(h + 1) * C], No[p][:, h, :], TTd[:, h, :], start=True, stop=True),
                       [No[p], TTd], [py1], inc=(h == H2 - 1))
                V_(lambda E: E.tensor_copy(fl(Y1[p]), py1[0:C, 0:H2 * C]), [py1], [Y1[p]])
                pc2 = psN[1]
                for h in range(H2):
                    T_(lambda E, h=h: E.matmul(pc2[0:C, h * C:(h + 1) * C], Td[p][:, h, :], Y1[p][:, h, :], start=True, stop=True),
                       [Td[p], Y1[p]], [pc2], inc=(h == H2 - 1))
                V_(lambda E: E.tensor_tensor(fl(TTb[p]), pc2[0:C, 0:H2 * C], fl(TTd), ALU.add), [pc2, TTd], [TTb[p]])
            else:
                TT = neumann_TT(P, cx, N0[p], NT0[p], nb, psN, H2, C)
                A_(lambda E, TT=TT: E.copy(TTb[p][:, :, :], TT[:, :, :]), [TT], [TTb[p]])

        def part2(cp, k):
            p = cp % 2
            ci = cp * NB + k
            q = ci % 2
            cs = slice(ci * C, (ci + 1) * C)
            pr = psA[0]
            for j in range(NJ):
                T_(lambda E, j=j: E.matmul(pr[0:C, 2 * j * NV:(2 * j + 2) * NV], At[:, j, cs], MbZ[:, j, :, :], start=True, stop=False, skip_group_check=True),
                   [At, MbZ], [pr], inc=False)
                for hd in (2 * j, 2 * j + 1):
                    T_(lambda E, hd=hd: E.matmul(pr[0:C, hd * NV:(hd + 1) * NV], AKm[p][:, k * H + hd, :], vtok[p][:, k, hd * NV:(hd + 1) * NV], start=False, stop=True,
                                                 skip_group_check=True), [AKm[p], vtok[p]], [pr], inc=(hd == H - 1))
            A_(lambda E: E.copy(fl(RHb[q]), pr[0:C, 0:H * NV]), [pr], [RHb[q]])
            pu = psA[1]
            for hd in range(H):
                T_(lambda E, hd=hd: E.matmul(pu[0:C, hd * NV:(hd + 1) * NV], TTb[p][:, k * H + hd, :], RHb[q][:, hd, :], start=True, stop=True),
                   [TTb[p], RHb[q]], [pu], inc=(hd == H - 1))
            A_(lambda E: E.copy(fl(Ub[q]), pu[0:C, 0:H * NV]), [pu], [Ub[q]])
            py = psA[0]
            for j in range(NJ):
                T_(lambda E, j=j: E.matmul(py[0:C, 2 * j * NV:(2 * j + 2) * NV], Rt[:, j, cs], MbZ[:, j, :, :], start=True, stop=False, skip_group_check=True),
                   [Rt, MbZ], [py], inc=False)
                for hd in (2 * j, 2 * j + 1):
                    T_(lambda E, hd=hd: E.matmul(py[0:C, hd * NV:(hd + 1) * NV], RKm[p][:, k * H + hd, :], vtok[p][:, k, hd * NV:(hd + 1) * NV], start=False, stop=False,
                                                 skip_group_check=True), [RKm[p], vtok[p]], [py], inc=False)
                    T_(lambda E, hd=hd: E.matmul(py[0:C, hd * NV:(hd + 1) * NV], RBm[p][:, k * H + hd, :], Ub[q][:, hd, :], start=False, stop=True,
                                                 skip_group_check=True), [RBm[p], Ub[q]], [py], inc=(hd == H - 1))
            A_(lambda E: E.copy(fl(ysb[q]), py[0:C, 0:H * NV]), [py], [ysb[q]])
            pm = psA[0] if pipe else psC[0]
            for j in range(NJ):
                T_(lambda E, j=j: E.matmul(pm[:, j * 128:(j + 1) * 128], khtok[p][:, k, j * 128:(j + 1) * 128], vtok[p][:, k, j * 128:(j + 1) * 128],
                                           start=True, stop=False), [khtok[p], vtok[p]], [pm], inc=False)
                T_(lambda E, j=j: E.matmul(pm[:, j * 128:(j + 1) * 128], nbhtok[p][:, k, j * 128:(j + 1) * 128], fl(Ub[q])[:, j * 128:(j + 1) * 128],
                                           start=False, stop=True), [nbhtok[p], Ub[q]], [pm], inc=(j == NJ - 1))
            for j in range(NJ):
                for po in (0, 64):
                    V_(lambda E, j=j, po=po: E.scalar_tensor_tensor(M[po:po + 64, j, :], M[po:po + 64, j, :], eP[po:po + 64, j, ci * C + C - 1:ci * C + C],
                                                                     pm[po:po + 64, j * 128 + po:j * 128 + po + 64], ALU.mult, ALU.add),
                       [M, eP, pm], [M], keep=not (j == 0 and po == 0))
            A_(lambda E: E.copy(MbZ[0:64, :, 0, :], M[0:64, :, :]), [M], [MbZ])
            A_(lambda E: E.copy(MbZ[64:128, :, 1, :], M[64:128, :, :]), [M], [MbZ], keep=True)
            pbn = psA[1] if pipe else psC[1]
            for j in range(NJ):
                T_(lambda E, j=j: E.matmul(pbn[0:C, j * 2:(j + 1) * 2], rkb[:, j, cs], hsf[:, :], start=True, stop=True), [rkb, hsf], [pbn], inc=(j == NJ - 1))
            A_(lambda E: E.copy(bon[q][:, :], pbn[0:C, 0:H]), [pbn], [bon[q]])
            pg = psA[1]
            T_(lambda E: E.matmul(pg[0:C, 0:GW], sgb[:, cs], gup[:, g0:g1], start=True, stop=True), [sgb, gup], [pg])
            s_ = st[q]
            y_ = ysb[q]
            V_(lambda E: E.tensor_reduce(s_[:, 0, :], y_[:, :, :], AX.X, ALU.add), [y_], [s_])
            V_(lambda E: E.tensor_scalar(s_[:, 0, :], s_[:, 0, :], 1.0 / 64, None, ALU.mult), [s_], [s_])
            V_(lambda E: E.tensor_tensor(y_[:, :, :], y_[:, :, :], s_[:, 0, :].unsqueeze(2).to_broadcast([C, H, NV]), ALU.subtract), [y_, s_], [y_])
            G_(lambda E: E.tensor_tensor(ysq[q][:, :, :], y_[:, :, :], y_[:, :, :], ALU.mult), [y_], [ysq[q]])
            V_(lambda E: E.tensor_reduce(s_[:, 1, :], ysq[q][:, :, :], AX.X, ALU.add), [ysq[q]], [s_], keep=True)
            V_(lambda E: E.tensor_scalar(s_[:, 1, :], s_[:, 1, :], 1.0 / 64, 64e-5, ALU.mult, ALU.add), [s_], [s_])
            A_(lambda E: E.activation(out=s_[:, 1, :], in_=s_[:, 1, :], func=AF.Sqrt), [s_], [s_])
            V_(lambda E: E.reciprocal(s_[:, 1, :], s_[:, 1, :]), [s_], [s_])
            V_(lambda E: E.tensor_tensor(y_[:, :, :], y_[:, :, :], s_[:, 1, :].unsqueeze(2).to_broadcast([C, H, NV]), ALU.mult), [y_, s_], [y_])
            V_(lambda E: E.tensor_tensor(fl(y_), fl(y_), lng[:, :], ALU.mult), [y_, lng], [y_])
            G_(lambda E: E.tensor_tensor(fl(y_), fl(y_), lnb[:, :], ALU.add), [y_, lnb], [y_])
            V_(lambda E: E.tensor_tensor(ysq[q][:, :, :], vtok[p][:, k, :].rearrange("p (h c) -> p h c", c=NV),
                                         bon[q][:, :].unsqueeze(2).to_broadcast([C, H, NV]), ALU.mult), [vtok[p], bon[q]], [ysq[q]])
            G_(lambda E: E.tensor_tensor(y_[:, :, :], y_[:, :, :], ysq[q][:, :, :], ALU.add), [y_, ysq[q]], [y_])
            V_(lambda E: E.tensor_tensor(yab[q][:, :], fl(y_), pg[0:C, 0:GW], ALU.mult), [y_, pg], [yab[q]])
            for j in range(NJ):
                T_(lambda E, j=j: E.transpose(psA[0][:, j * C:(j + 1) * C], yab[q][:, j * 128:(j + 1) * 128], cx.identf[0:C, 0:C]),
                   [yab[q], cx.identf], [psA[0]], inc=(j == NJ - 1))
            A_(lambda E: E.copy(fl(yas[q]), psA[0][:, 0:NJ * C]), [psA[0]], [yas[q]])
            P.dma('sp', sc['yaT'][g0:g1, c0 + ci * C:c0 + (ci + 1) * C].rearrange("(h p) t -> p h t", p=128), yas[q][:, :, :],
                  reads=[yas[q]], writes=[sc['yaT']], keep=True)

        NCt = TW // C
        for cp in range(NCt // NB):
            part1(cp)
            for k in range(NB):
                part2(cp, k)


def merge_phase(P, cx, sc, x_dram, out_dram, wbr_ap, wout_ap, g_ap, b_ap, L):
    Wb = load_w_bf16(P, 'wbr', wbr_ap, 1536, D_MODEL)
    Wo = load_w_bf16(P, 'wo', wout_ap, D_MODEL, D_MODEL)
    gb = P.sb('lng2', [128, D_MODEL]); bb = P.sb('lnb2', [128, D_MODEL])
    P.dma('sp', gb[:, :], g_ap.partition_broadcast(128), writes=[gb])
    P.dma('sp', bb[:, :], b_ap.partition_broadcast(128), writes=[bb])
    n = L // 128
    NS = 4
    P.interleave([(lambda k=k: merge_tiles(P, cx, sc, x_dram, out_dram, Wb, Wo, gb, bb, range(k, n, NS))) for k in range(NS)])


def merge_tiles(P, cx, sc, x_dram, out_dram, Wb, Wo, gb, bb, tiles):
    yT = [[P.sb('myT', [128, 4, 128], BF16) for _ in range(3)] for _ in range(1)]
    gt = [P.sb('mgt', [128, 3072]) for _ in range(1)]
    xi = [P.sb('mxi', [128, D_MODEL]) for _ in range(1)]
    mm = [P.sb('mmm', [128, D_MODEL]) for _ in range(1)]
    tmp = [P.sb('mtmp', [128, 512]) for _ in range(2)]
    mb = [P.sb('mmb', [128, D_MODEL], BF16) for _ in range(1)]
    mT = [P.sb('mmT', [128, 8, 128], BF16) for _ in range(1)]
    y = [P.sb('my', [128, D_MODEL]) for _ in range(1)]
    st = [P.sb('mst', [128, 2, 6]) for _ in range(1)]
    mv = [P.sb('mmv', [128, 2]) for _ in range(1)]
    rs = [P.sb('mrs', [128, 2]) for _ in range(1)]
    psA = [P.ps('mpA', [128, 512]) for _ in range(1)]
    psT = [P.ps('mpT', [128, 1024], BF16) for _ in range(1)]
    psO = [psA[0], psA[0]]
    eps = 1e-5 / (ALPHA * ALPHA)
    names = ('yaT', 'ybT', 'ycT')
    na = 0
    for it_, i in enumerate(tiles):
        p = 0
        r0 = i * 128
        for n in range(3):
            P.dma('sp', yT[p][n][:, :, :], sc[names[n]][:, r0:r0 + 128].rearrange("(c p) t -> p c t", p=128),
                  reads=[sc[names[n]]], writes=[yT[p][n]])
        P.dma('sp', gt[p][:, :], sc['gates'][r0:r0 + 128, :], reads=[sc['gates']], writes=[gt[p]])
        P.dma('sp', xi[p][:, :], x_dram[r0:r0 + 128, :], reads=[x_dram], writes=[xi[p]])
        for half in range(2):
            hs = slice(half * 512, (half + 1) * 512)
            for n in range(3):
                ps = psA[0]
                na += 1
                for c in range(4):
                    P.op('pe', lambda E, c=c: E.matmul(ps[:, :], yT[p][n][:, c, :], Wb[:, n * 4 + c, hs], start=(c == 0), stop=(c == 3)),
                         reads=[yT[p][n], Wb], writes=[ps], inc=(c == 3))
                gs = gt[p][:, n * 1024 + half * 512:n * 1024 + (half + 1) * 512]
                if n == 0:
                    P.op('dve', lambda E: E.tensor_tensor(mm[p][:, hs], ps[:, :], gs, ALU.mult), reads=[ps, gt[p]], writes=[mm[p]], keep=(half == 1))
                else:
                    tp = tmp[n % 2]
                    P.op('dve', lambda E: E.tensor_tensor(tp[:, :], ps[:, :], gs, ALU.mult), reads=[ps, gt[p]], writes=[tp])
                    P.op('pool', lambda E: E.tensor_tensor(mm[p][:, hs], mm[p][:, hs], tp[:, :], ALU.add), reads=[mm[p], tp], writes=[mm[p]], keep=True)
        P.op('act', lambda E: E.copy(mb[p][:, :], mm[p][:, :]), reads=[mm[p]], writes=[mb[p]])
        pt = psT[0]
        for kc in range(8):
            P.op('pe', lambda E, kc=kc: E.transpose(pt[:, kc * 128:(kc + 1) * 128], mb[p][:, kc * 128:(kc + 1) * 128], cx.ident[:, :]),
                 reads=[mb[p], cx.ident], writes=[pt], inc=(kc == 7))
        P.op('dve', lambda E: E.tensor_copy(mT[p][:, :, :].rearrange("p k t -> p (k t)"), pt[:, :]), reads=[pt], writes=[mT[p]])
        for half in range(2):
            po = psO[half]
            for kc in range(8):
                P.op('pe', lambda E, kc=kc: E.matmul(po[:, :], mT[p][:, kc, :], Wo[:, kc, half * 512:(half + 1) * 512], start=(kc == 0), stop=(kc == 7)),
                     reads=[mT[p], Wo], writes=[po], inc=(kc == 7))
            P.op('dve', lambda E: E.scalar_tensor_tensor(y[p][:, half * 512:(half + 1) * 512], po[:, :], 1.0 / ALPHA,
                                                         xi[p][:, half * 512:(half + 1) * 512], ALU.mult, ALU.add),
                 reads=[po, xi[p]], writes=[y[p]], keep=(half == 1))
        layernorm_out(P, y[p], gb, bb, out_dram, r0, eps, st[p], mv[p], rs[p])


STOP_AFTER = 1000
RW_C = 128
GD_C = 128
NDT = BF16
PARAM_NAMES = ["ffn1_w_in", "ffn1_w_out", "ln1_g", "ln1_b", "mix_w_in", "rw_shift_mu", "rw_w0", "rw_w_up", "rw_a0", "rw_a_up",
               "rw_g_up", "rw_k_k", "rw_k_a", "rw_r_k", "rw_ln_g", "rw_ln_b", "da_lam_q1", "da_lam_k1", "da_lam_q2", "da_lam_k2",
               "da_norm_g", "gd_conv_w", "gd_a_log", "gd_dt_bias", "gd_norm_g", "mix_w_branch", "mix_w_out", "ln2_g", "ln2_b",
               "ffn2_w_in", "ffn2_w_out", "ln3_g", "ln3_b"]


def host_consts():
    c = {'ident': np.eye(128, dtype=np.float32)}
    ab, cm = da_consts()
    c['abias'] = ab
    c['cmask'] = cm
    tl, nm, st = tri_consts(GD_C)
    c['tri_le'] = tl
    c['negmask'] = nm
    c['strict'] = st
    bd = np.zeros((128, 128), np.float32)
    bd[:64, :64] = 1
    bd[64:, 64:] = 1
    c['bd'] = bd
    for n, v in rw_consts(RW_C).items():
        c['rw_' + n] = v
    return c


def build_model(L, shapes, layers=(0, 1), phases=None):
    nc = bass.Bass("TRN2", target_bir_lowering=False)
    P = Prog(nc)
    x = P.dram('x', [L, D_MODEL], kind='ExternalInput')
    out = P.dram('out', [L, D_MODEL], kind='ExternalOutput')
    prm = {n: nc.dram_tensor(n, list(shapes[n]), F32, kind='ExternalInput').ap() for n in PARAM_NAMES}
    hc = host_consts()
    cst = {n: nc.dram_tensor('c_' + n, list(v.shape), F32, kind='ExternalInput').ap() for n, v in hc.items()}
    cx = Ctx(P, cst['ident'])
    xs = [P.dram('xs%d' % i, [L, D_MODEL]) for i in range(2)]
    sc = mix_scratch(P, L)
    cur = x
    nl = len(layers)
    nph = [0]

    def _go():
        nph[0] += 1
        return nph[0] <= STOP_AFTER
    for li, l in enumerate(layers):
        a, b = xs[0], xs[1]
        if _go():
            P.phase_begin()
            ffn_phase(P, cx, cur, a, prm['ffn1_w_in'][l], prm['ffn1_w_out'][l], prm['ln1_g'][l], prm['ln1_b'][l], L)
            P.phase_end()
        if _go():
            P.phase_begin()
            mixproj_phase(P, cx, a, prm['mix_w_in'][l], sc, L)
            P.phase_end()
        if _go():
            P.phase_begin()
            rw_w = {'mu': prm['rw_shift_mu'][l], 'w0': prm['rw_w0'][l], 'w_up': prm['rw_w_up'][l], 'a0': prm['rw_a0'][l],
                    'a_up': prm['rw_a_up'][l], 'g_up': prm['rw_g_up'][l], 'k_k': prm['rw_k_k'][l], 'k_a': prm['rw_k_a'][l],
                    'r_k': prm['rw_r_k'][l].rearrange("h n -> (h n)"), 'ln_g': prm['rw_ln_g'][l], 'ln_b': prm['rw_ln_b'][l]}
            rw_c = {n: cst['rw_' + n] for n in ('nsu', 'su', 'iu', 'niu', 'nsl', 'bo', 'hs', 'rm')}
            rw_c['bd'] = cst['bd']
            P.interleave([lambda: rwkv_phase(P, cx, sc, rw_w, rw_c, L, js=(0, 1), half=True, C=RW_C),
                          lambda: rwkv_phase(P, cx, sc, rw_w, rw_c, L, js=(2, 3), half=True, C=RW_C)])
            P.phase_end()
        if _go():
            P.phase_begin()
            lam_init = 0.8 - 0.6 * math.exp(-0.3 * l)
            diffattn_phase(P, cx, sc, [prm[n][l] for n in ('da_lam_q1', 'da_lam_k1', 'da_lam_q2', 'da_lam_k2')], prm['da_norm_g'][l],
                           lam_init, cst['abias'], cst['cmask'], L)
            P.phase_end()
        if _go():
            P.phase_begin()
            gd_a = (prm['gd_conv_w'][l], prm['gd_a_log'][l], prm['gd_dt_bias'][l], prm['gd_norm_g'][l],
                    {n: cst[n] for n in ('tri_le', 'negmask', 'strict', 'bd')}, L)
            P.interleave([lambda: gdn_phase(P, cx, sc, *gd_a, hs=(0, 1), half=True, C=GD_C),
                          lambda: gdn_phase(P, cx, sc, *gd_a, hs=(2, 3), half=True, C=GD_C)])
            P.phase_end()
        if _go():
            P.phase_begin()
            merge_phase(P, cx, sc, a, b, prm['mix_w_branch'][l].rearrange("n c d -> (n c) d"), prm['mix_w_out'][l], prm['ln2_g'][l], prm['ln2_b'][l], L)
            P.phase_end()
        dst = out if li == nl - 1 else a
        if _go():
            P.phase_begin()
            ffn_phase(P, cx, b, dst, prm['ffn2_w_in'][l], prm['ffn2_w_out'][l], prm['ln3_g'][l], prm['ln3_b'][l], L)
            P.phase_end()
        cur = a
    P.finish([out])
    return P.build(), hc, P


_CACHE = {}


def kernel(**inputs):
    x = np.asarray(inputs['x'], dtype=np.float32)
    B, L, D = x.shape
    shapes = {n: np.asarray(inputs[n]).shape for n in PARAM_NAMES}
    nc, hc, _ = build_model(L, shapes)
    base = {n: np.ascontiguousarray(np.asarray(inputs[n], dtype=np.float32)) for n in PARAM_NAMES}
    for n, v in hc.items():
        base['c_' + n] = v
    in_maps = []
    for b in range(B):
        m = dict(base)
        m['x'] = np.ascontiguousarray(x[b])
        in_maps.append(m)
    res = run_bass_kernel_spmd(nc, in_maps, core_ids=list(range(B)))
    return np.stack([np.asarray(r['out'], dtype=np.float32) for r in res.results], axis=0)
```

```python
import math
import threading
from contextlib import ExitStack
import numpy as np
import concourse.bass as bass
import concourse.mybir as mybir
from concourse.bass_utils import run_bass_kernel_spmd

F32 = mybir.dt.float32
BF16 = mybir.dt.bfloat16
AF = mybir.ActivationFunctionType
ALU = mybir.AluOpType
AX = mybir.AxisListType

ENGS = ('pe', 'act', 'dve', 'pool', 'sp')

D_MODEL = 1024
SEQ = 4096
DEPTH = 2
D_FF = 2816
ALPHA = (2.0 * DEPTH) ** 0.25


class Buf:
    __slots__ = ('t', 'w', 'r', 'name')

    def __init__(self, t, name):
        self.t = t
        self.w = {}
        self.r = {}
        self.name = name

    def __getitem__(self, k):
        return self.t[k]


class _Rec:
    def __init__(self):
        self.call = None

    def __getattr__(self, name):
        def f(*a, **k):
            self.call = (name, a, k)
            return self
        return f


class Prog:
    NDMA = 40

    def __init__(self, nc):
        self.nc = nc
        self.es = ExitStack()
        self.ops = {e: [] for e in ENGS}
        self.sem = {}
        self.cnt = {}
        for e in ('pe', 'act', 'dve', 'pool'):
            self.sem[e] = self.es.enter_context(nc.semaphore('s_' + e))
            self.cnt[e] = 0
        self.known = {e: {} for e in ENGS}
        self.dsem = [[self.es.enter_context(nc.semaphore('d%d' % i)), 0] for i in range(self.NDMA)]
        self.dq = {'sp': (0, 28), 'pool': (28, 8), 'act': (36, 4)}
        self.dnext = {'sp': 0, 'pool': 0, 'act': 0}
        self.nalloc = 0
        self.ninst = 0
        self.nwait = 0
        self.cur = self.es

    _il = None

    def interleave(self, fns):
        if self._il is not None and getattr(self._il['tl'], 'idx', None) is not None:
            return self._interleave_nested(fns)
        n = len(fns)
        il = {'turn': 0, 'alive': [True] * n, 'cv': threading.Condition(), 'err': None, 'tl': threading.local()}
        self._il = il

        def advance(i):
            m = len(il['alive'])
            for d in range(1, m + 1):
                j = (i + d) % m
                if il['alive'][j]:
                    il['turn'] = j
                    return
            il['turn'] = -1

        il['advance'] = advance
        ths = [threading.Thread(target=self._il_runner, args=(i, f)) for i, f in enumerate(fns)]
        for t in ths:
            t.start()
        for t in ths:
            t.join()
        self._il = None
        if il['err'] is not None:
            raise il['err']

    def _il_runner(self, i, fn, on_exit=None):
        il = self._il
        il['tl'].idx = i
        with il['cv']:
            while il['turn'] != i:
                il['cv'].wait()
        try:
            fn()
        except BaseException as e:
            il['err'] = e
        finally:
            with il['cv']:
                il['alive'][i] = False
                if on_exit is not None:
                    on_exit()
                il['advance'](i)
                il['cv'].notify_all()

    def _interleave_nested(self, fns):
        il = self._il
        me = il['tl'].idx
        left = [len(fns)]

        def on_exit():
            left[0] -= 1
            if left[0] == 0:
                il['alive'][me] = True
        with il['cv']:
            base = len(il['alive'])
            il['alive'].extend([True] * len(fns))
            il['alive'][me] = False
            ths = [threading.Thread(target=self._il_runner, args=(base + k, f, on_exit)) for k, f in enumerate(fns)]
            il['advance'](me)
            il['cv'].notify_all()
        for t in ths:
            t.start()
        with il['cv']:
            while not (il['turn'] == me and il['alive'][me]):
                il['cv'].wait()
        for t in ths:
            t.join()
        if il['err'] is not None:
            raise il['err']

    def _yield(self):
        il = self._il
        if il is None:
            return
        i = getattr(il['tl'], 'idx', None)
        if i is None:
            return
        with il['cv']:
            il['advance'](i)
            il['cv'].notify_all()
            while il['turn'] != i:
                il['cv'].wait()

    def phase_begin(self):
        self.cur = ExitStack()

    def phase_end(self):
        self.barrier()
        self.emit_block()
        self.cur.close()
        self.cur = self.es

    def barrier(self):
        for e in ENGS:
            kn = self.known[e]
            waits = []
            for e2 in ('pe', 'act', 'dve', 'pool'):
                if e2 != e and self.cnt[e2] > kn.get(id(self.sem[e2]), 0):
                    waits.append((self.sem[e2], self.cnt[e2]))
                    kn[id(self.sem[e2])] = self.cnt[e2]
            for s, tot in self.dsem:
                if tot > kn.get(id(s), 0):
                    waits.append((s, tot))
                    kn[id(s)] = tot

            def emit(E, waits=waits):
                for s, v in waits:
                    E.wait_ge(s, v)
            self.ops[e].append(emit)

    def sb(self, name, shape, dtype=F32):
        self.nalloc += 1
        t = self.cur.enter_context(self.nc.sbuf_tensor('%s_%d' % (name, self.nalloc), list(shape), dtype))
        return Buf(t, name)

    def ps(self, name, shape, dtype=F32):
        self.nalloc += 1
        t = self.cur.enter_context(self.nc.psum_tensor('%s_%d' % (name, self.nalloc), list(shape), dtype))
        return Buf(t, name)

    def dram(self, name, shape, dtype=F32, kind='Internal'):
        t = self.nc.dram_tensor(name, list(shape), dtype, kind=kind)
        return Buf(t.ap(), name)

    def _collect(self, eng, reads, writes, keep):
        need = {}
        known = self.known[eng]

        def add(tok, skip_same):
            s, v, e = tok
            if skip_same and e == eng and eng == 'pe':
                return
            k = id(s)
            if known.get(k, 0) >= v:
                return
            if k not in need or need[k][1] < v:
                need[k] = (s, v)
        for b in reads:
            for tok in b.w.values():
                add(tok, False)
        for b in writes:
            if not keep:
                for tok in b.w.values():
                    add(tok, True)
            for tok in b.r.values():
                add(tok, True)
        waits = list(need.values())
        for s, v in waits:
            known[id(s)] = v
        self.nwait += len(waits)
        return waits

    def _update(self, key, tok, reads, writes, keep):
        for b in reads:
            b.r[key] = tok
        for b in writes:
            if keep:
                b.w[key] = tok
            else:
                b.w = {key: tok}
                b.r = {}

    def op(self, eng, fn, reads=(), writes=(), inc=True, keep=False):
        waits = self._collect(eng, reads, writes, keep)
        sem = self.sem[eng]
        if inc:
            self.cnt[eng] += 1
            tok = (sem, self.cnt[eng], eng)
        else:
            tok = (sem, self.cnt[eng] + 1, eng)

        rec = _Rec()
        fn(rec)
        cname, ca, ck = rec.call

        def emit(E, waits=waits, inc=inc, sem=sem, cname=cname, ca=ca, ck=ck):
            for s, v in waits:
                E.wait_ge(s, v)
            ins = getattr(E, cname)(*ca, **ck)
            if inc:
                ins.then_inc(sem, 1)
        self.ops[eng].append(emit)
        self.ninst += 1
        self._update(eng, tok, reads, writes, keep)
        self._yield()

    def dma(self, q, out, in_, reads=(), writes=(), keep=False, **kw):
        b0, nq = self.dq[q]
        slot = self.dsem[b0 + self.dnext[q]]
        self.dnext[q] = (self.dnext[q] + 1) % nq
        waits = self._collect(q, reads, writes, keep)
        s, tot = slot
        kn = self.known[q]
        if tot > 0 and kn.get(id(s), 0) < tot:
            waits.append((s, tot))
            kn[id(s)] = tot
        slot[1] = tot + 16
        tok = (s, tot + 16, 'dma')

        def emit(E, waits=waits, out=out, in_=in_, s=s, kw=kw):
            for ws, v in waits:
                E.wait_ge(ws, v)
            E.dma_start(out=out, in_=in_, **kw).then_inc(s, 16)
        self.ops[q].append(emit)
        self.ninst += 1
        self._update(('dma', id(s)), tok, reads, writes, keep)
        self._yield()

    DBG = False

    def dbg(self, name, buf, ap, shape, dtype=F32):
        if not self.DBG:
            return
        d = self.dram('dbg_' + name, shape, dtype, kind='ExternalOutput')
        self.dma('sp', d[tuple(slice(None) for _ in shape)], ap, reads=[buf], writes=[d])
        self.dbgs = getattr(self, 'dbgs', []) + [d]

    def finish(self, bufs, eng='sp'):
        bufs = list(bufs) + getattr(self, 'dbgs', [])
        waits = self._collect(eng, bufs, (), False)

        def emit(E, waits=waits):
            for s, v in waits:
                E.wait_ge(s, v)
        self.ops[eng].append(emit)

    def build(self):
        self.emit_block()
        self.es.close()
        return self.nc

    def emit_block(self):
        ops = self.ops
        self.ops = {e: [] for e in ENGS}
        with self.nc.Block() as block:
            @block.tensor
            def _(E):
                for f in ops['pe']:
                    f(E)

            @block.scalar
            def _(E):
                for f in ops['act']:
                    f(E)

            @block.vector
            def _(E):
                for f in ops['dve']:
                    f(E)

            @block.gpsimd
            def _(E):
                for f in ops['pool']:
                    f(E)

            @block.sync
            def _(E):
                for f in ops['sp']:
                    f(E)


def load_w_bf16(P, name, w_ap, rows, cols, csplit=1):
    nk = rows // 128
    w = P.sb(name, [128, nk, cols], BF16)
    src = w_ap.rearrange("(k p) c -> p k c", p=128)
    cw = cols // csplit
    for k in range(nk):
        for c in range(csplit):
            P.dma('pool', w[:, k, c * cw:(c + 1) * cw], src[:, k, c * cw:(c + 1) * cw],
                  writes=[w], keep=True)
    return w


class WBlocks:
    def __init__(self, P, name, w_ap, rows, bounds, order=None):
        self.nk = rows // 128
        self.bounds = list(bounds)
        self.bufs = []
        src = w_ap.rearrange("(k p) c -> p k c", p=128)
        nb_ = len(self.bounds) - 1
        for b in range(nb_):
            self.bufs.append(P.sb('%s%d' % (name, b), [128, self.nk, self.bounds[b + 1] - self.bounds[b]], BF16))
        for b in (order if order is not None else range(nb_)):
            lo, hi = self.bounds[b], self.bounds[b + 1]
            for k in range(self.nk):
                P.dma('pool', self.bufs[b][:, k, :], src[:, k, lo:hi], writes=[self.bufs[b]], keep=True)

    def blk(self, c0):
        for b in range(len(self.bounds) - 1):
            if self.bounds[b] <= c0 < self.bounds[b + 1]:
                return b
        raise ValueError(c0)

    def buf(self, c0):
        return self.bufs[self.blk(c0)]

    def ap(self, kc, c0, n):
        b = self.blk(c0)
        lo = self.bounds[b]
        assert c0 + n <= self.bounds[b + 1]
        return self.bufs[b][:, kc, c0 - lo:c0 - lo + n]


class Ctx:
    def __init__(self, P, ident_ap):
        self.P = P
        idf = P.sb('identf', [128, 128], F32)
        P.dma('sp', idf[:, :], ident_ap, writes=[idf])
        self.identf = idf
        self.ident = P.sb('ident', [128, 128], BF16)
        P.op('dve', lambda E: E.tensor_copy(self.ident[:, :], idf[:, :]), reads=[idf], writes=[self.ident])


def transpose_in(P, cx, x_dram, row0, xin, xbf, psT, xT, s):
    P.dma('sp', xin[:, :], x_dram[row0:row0 + 128, :], reads=[x_dram], writes=[xin])
    P.op('act', lambda E: E.copy(xbf[:, :], xin[:, :]), reads=[xin], writes=[xbf])
    for kc in range(8):
        P.op('pe', lambda E, kc=kc: E.transpose(psT[:, kc * 128:(kc + 1) * 128],
                                                 xbf[:, kc * 128:(kc + 1) * 128], cx.ident[:, :]),
             reads=[xbf, cx.ident], writes=[psT], inc=(kc == 7))
    P.op('dve', lambda E: E.tensor_copy(xT[:, :, s * 128:(s + 1) * 128],
                                        psT[:, :].rearrange("p (k t) -> p k t", k=8)),
         reads=[psT], writes=[xT], keep=True)


def layernorm_out(P, y, gb, bb, out_dram, row0, eps, st, mv, rs):
    for h in range(2):
        P.op('dve', lambda E, h=h: E.bn_stats(st[:, h, :], y[:, h * 512:(h + 1) * 512]),
             reads=[y], writes=[st], keep=(h == 1))
    P.op('dve', lambda E: E.bn_aggr(mv[:, :], st[:, :, :].rearrange("p a b -> p (a b)")),
         reads=[st], writes=[mv])
    P.op('dve', lambda E: E.tensor_scalar(rs[:, 0:1], mv[:, 1:2], eps, None, ALU.add), reads=[mv], writes=[rs])
    P.op('act', lambda E: E.activation(out=rs[:, 0:1], in_=rs[:, 0:1], func=AF.Sqrt), reads=[rs], writes=[rs])
    P.op('dve', lambda E: E.reciprocal(rs[:, 0:1], rs[:, 0:1]), reads=[rs], writes=[rs])
    P.op('dve', lambda E: E.tensor_scalar(rs[:, 1:2], mv[:, 0:1], rs[:, 0:1], -1.0, ALU.mult, ALU.mult),
         reads=[mv, rs], writes=[rs])
    P.op('act', lambda E: E.activation(out=y[:, :], in_=y[:, :], func=AF.Identity, scale=rs[:, 0:1], bias=rs[:, 1:2]),
         reads=[y, rs], writes=[y])
    P.op('pool', lambda E: E.tensor_tensor(y[:, :], y[:, :], gb[:, :], ALU.mult), reads=[y, gb], writes=[y])
    P.op('pool', lambda E: E.tensor_tensor(y[:, :], y[:, :], bb[:, :], ALU.add), reads=[y, bb], writes=[y])
    P.dma('sp', out_dram[row0:row0 + 128, :], y[:, :], reads=[y], writes=[out_dram], keep=True)


def ffn_phase(P, cx, x_dram, out_dram, w_in_ap, w_out_ap, g_ap, b_ap, NT):
    NFF = D_FF // 128
    W1 = WBlocks(P, 'w1', w_in_ap, D_MODEL, [0, 1408, 2816, 4224, 5632], order=[0, 2, 1, 3])
    W2 = load_w_bf16(P, 'w2', w_out_ap, D_FF, D_MODEL)
    gb = P.sb('lng', [128, D_MODEL])
    bb = P.sb('lnb', [128, D_MODEL])
    P.dma('sp', gb[:, :], g_ap.partition_broadcast(128), writes=[gb])
    P.dma('sp', bb[:, :], b_ap.partition_broadcast(128), writes=[bb])
    xin = [P.sb('xin', [128, D_MODEL]) for _ in range(3)]
    xbf = [P.sb('xbf', [128, D_MODEL], BF16) for _ in range(2)]
    xT = P.sb('xT', [128, 8, 512], BF16)
    gT = P.sb('gT', [128, NFF, 512], BF16)
    sg = [P.sb('sg', [128, 512]) for _ in range(2)]
    y = [P.sb('y', [128, D_MODEL]) for _ in range(2)]
    st = [P.sb('st', [128, 2, 6]) for _ in range(2)]
    mv = [P.sb('mv', [128, 2]) for _ in range(2)]
    rs = [P.sb('rs', [128, 2]) for _ in range(2)]
    psT = [P.ps('psT', [128, 1024], BF16) for _ in range(2)]
    psA = [P.ps('psA', [128, 512]) for _ in range(4)]
    psO = [P.ps('psO', [128, 512]) for _ in range(2)]
    eps = 1e-5 / (ALPHA * ALPHA)
    c_res = 0.5 / ALPHA
    nx = 0
    ny = 0
    for t in range(NT // 512):
        for s in range(4):
            transpose_in(P, cx, x_dram, t * 512 + s * 128, xin[nx % 3], xbf[nx % 2], psT[nx % 2], xT, s)
            nx += 1
        for j in range(NFF):
            pg = psA[(2 * j) % 4]
            pu = psA[(2 * j + 1) % 4]
            for kc in range(8):
                P.op('pe', lambda E, kc=kc, j=j, pg=pg: E.matmul(
                    pg[:, :], W1.ap(kc, j * 128, 128), xT[:, kc, :], start=(kc == 0), stop=(kc == 7)),
                    reads=[W1.buf(j * 128), xT], writes=[pg], inc=(kc == 7))
            for kc in range(8):
                P.op('pe', lambda E, kc=kc, j=j, pu=pu: E.matmul(
                    pu[:, :], W1.ap(kc, D_FF + j * 128, 128), xT[:, kc, :],
                    start=(kc == 0), stop=(kc == 7)),
                    reads=[W1.buf(D_FF + j * 128), xT], writes=[pu], inc=(kc == 7))
            sgj = sg[j % 2]
            P.op('act', lambda E, pg=pg, sgj=sgj: E.activation(out=sgj[:, :], in_=pg[:, :], func=AF.Silu),
                 reads=[pg], writes=[sgj])
            P.op('dve', lambda E, pu=pu, sgj=sgj, j=j: E.tensor_tensor(gT[:, j, :], sgj[:, :], pu[:, :], ALU.mult),
                 reads=[sgj, pu], writes=[gT], keep=True)
        for s in range(4):
            row0 = t * 512 + s * 128
            xi = xin[nx % 3]
            nx += 1
            P.dma('sp', xi[:, :], x_dram[row0:row0 + 128, :], reads=[x_dram], writes=[xi])
            yy = y[ny % 2]
            for n in range(2):
                po = psO[n]
                for j in range(NFF):
                    P.op('pe', lambda E, j=j, s=s, n=n, po=po: E.matmul(
                        po[:, :], gT[:, j, s * 128:(s + 1) * 128], W2[:, j, n * 512:(n + 1) * 512],
                        start=(j == 0), stop=(j == NFF - 1)),
                        reads=[gT, W2], writes=[po], inc=(j == NFF - 1))
                P.op('dve', lambda E, n=n, po=po, xi=xi, yy=yy: E.scalar_tensor_tensor(
                    yy[:, n * 512:(n + 1) * 512], po[:, :], c_res, xi[:, n * 512:(n + 1) * 512], ALU.mult, ALU.add),
                    reads=[po, xi], writes=[yy], keep=(n == 1))
            layernorm_out(P, yy, gb, bb, out_dram, row0, eps, st[ny % 2], mv[ny % 2], rs[ny % 2])
            ny += 1


RW_COLS = 1792
DA_COLS = 1536
GD_COLS = 2056
O1 = RW_COLS
O2 = O1 + DA_COLS
O3 = O2 + GD_COLS
IN_COLS = O3 + 3072


def mixproj_phase(P, cx, x_dram, w_ap, sc, NT):
    W = WBlocks(P, 'wmix', w_ap, D_MODEL, [0, 1792, 2816, 3328, 4864, 4872, 5384] + [5384 + 512 * (c + 1) for c in range(6)],
                order=[0, 1, 3, 2, 4, 5, 6, 7, 8, 9, 10, 11])
    xin = [P.sb('xin', [128, D_MODEL]) for _ in range(3)]
    xbf = [P.sb('xbf', [128, D_MODEL], BF16) for _ in range(2)]
    xT = P.sb('xT', [128, 8, 512], BF16)
    stf = [P.sb('stf', [128, 512]) for _ in range(4)]
    stb = [P.sb('stb', [128, 512], BF16) for _ in range(3)]
    psT = [P.ps('psT', [128, 1024], BF16) for _ in range(2)]
    psA = [P.ps('psA', [128, 512]) for _ in range(6)]
    nx = 0
    npz = 0
    nf = 0
    nb = 0
    fm = []
    for c in range(14):
        fm.append((c * 128, sc['hr'], c * 128, 'f', 1.0))
    for c in range(4):
        fm.append((O1 + c * 128, sc['daqk'], c * 128, 'b', 0.125))
    for c in range(4):
        fm.append((O1 + 512 + c * 128, sc['daqk'], 512 + c * 128, 'b', 1.0))
    for c in range(12):
        fm.append((O2 + c * 128, sc['gqkv'], c * 128, 'f', 1.0))
    tm = [(O1 + 1024, 512, sc['dav'], 0, 'b', None),
          (O2 + 1536, 8, sc['gab'], 0, 'f', None),
          (O2 + 1544, 512, sc['gz'], 0, 'f', AF.Silu)]
    for c in range(6):
        tm.append((O3 + c * 512, 512, sc['gates'], c * 512, 'f', AF.Sigmoid))
    for t in range(NT // 512):
        for s in range(4):
            transpose_in(P, cx, x_dram, t * 512 + s * 128, xin[nx % 3], xbf[nx % 2], psT[nx % 2], xT, s)
            nx += 1
        for (c0, dst, r0, tag, scl) in fm:
            pp = psA[npz % 6]
            npz += 1
            for kc in range(8):
                P.op('pe', lambda E, kc=kc, c0=c0, pp=pp: E.matmul(
                    pp[:, :], W.ap(kc, c0, 128), xT[:, kc, :], start=(kc == 0), stop=(kc == 7)),
                    reads=[W.buf(c0), xT], writes=[pp], inc=(kc == 7))
            if tag == 'f':
                so = stf[nf % 4]
                nf += 1
            else:
                so = stb[nb % 3]
                nb += 1
            if (npz % 2) == 0:
                P.op('act', lambda E, so=so, pp=pp, scl=scl: E.mul(so[:, :], pp[:, :], scl), reads=[pp], writes=[so])
            else:
                P.op('dve', lambda E, so=so, pp=pp, scl=scl: E.tensor_scalar(so[:, :], pp[:, :], scl, None, ALU.mult),
                     reads=[pp], writes=[so])
            P.dma('sp', dst[r0:r0 + 128, t * 512:(t + 1) * 512], so[:, :], reads=[so], writes=[dst], keep=True)
        for s in range(4):
            row0 = t * 512 + s * 128
            for (c0, ncol, dst, d0, tag, func) in tm:
                pp = psA[npz % 6]
                npz += 1
                for kc in range(8):
                    P.op('pe', lambda E, kc=kc, c0=c0, pp=pp, s=s, ncol=ncol: E.matmul(
                        pp[:, 0:ncol], xT[:, kc, s * 128:(s + 1) * 128], W.ap(kc, c0, ncol),
                        start=(kc == 0), stop=(kc == 7)),
                        reads=[W.buf(c0), xT], writes=[pp], inc=(kc == 7))
                if tag == 'f':
                    so = stf[nf % 4]
                    nf += 1
                else:
                    so = stb[nb % 3]
                    nb += 1
                if func is not None:
                    P.op('act', lambda E, so=so, pp=pp, ncol=ncol, func=func: E.activation(
                        out=so[:, 0:ncol], in_=pp[:, 0:ncol], func=func), reads=[pp], writes=[so])
                else:
                    P.op('dve', lambda E, so=so, pp=pp, ncol=ncol: E.tensor_copy(so[:, 0:ncol], pp[:, 0:ncol]),
                         reads=[pp], writes=[so])
                P.dma('sp', dst[row0:row0 + 128, d0:d0 + ncol], so[:, 0:ncol], reads=[so], writes=[dst], keep=True)


def mix_scratch(P, L, pfx='', ext=()):
    k = lambda n: 'ExternalOutput' if n in ext else 'Internal'
    return {
        'hr': P.dram(pfx + 'hr', [RW_COLS, L], F32, kind=k('hr')),
        'daqk': P.dram(pfx + 'daqk', [1024, L], BF16, kind=k('daqk')),
        'dav': P.dram(pfx + 'dav', [L, 512], BF16, kind=k('dav')),
        'gqkv': P.dram(pfx + 'gqkv', [1536, L], F32, kind=k('gqkv')),
        'gab': P.dram(pfx + 'gab', [L, 8], F32, kind=k('gab')),
        'gz': P.dram(pfx + 'gz', [L, 512], F32, kind=k('gz')),
        'gates': P.dram(pfx + 'gates', [L, 3072], F32, kind=k('gates')),
        'yaT': P.dram(pfx + 'yaT', [512, L], BF16, kind=k('yaT')),
        'ybT': P.dram(pfx + 'ybT', [512, L], BF16, kind=k('ybT')),
        'ycT': P.dram(pfx + 'ycT', [512, L], BF16, kind=k('ycT')),
    }


def da_consts():
    kk = np.arange(128, dtype=np.float64)
    ab = np.zeros((128, 4 * 32), np.float32)
    for h in range(4):
        slope = 2.0 ** (-8.0 * (h + 1) / 4)
        for d in range(32):
            ab[:, h * 32 + d] = slope * (kk - 127.0) - slope * 128.0 * d
    cm = (kk[:, None] <= kk[None, :]).astype(np.float32)
    return ab, np.concatenate([cm, cm], axis=1)


def diffattn_phase(P, cx, sc, lam_aps, normg_ap, lam_init, abias_ap, cmask_ap, L):
    NB = L // 128
    qT = P.sb('qT', [128, 4, L], BF16)
    kT = P.sb('kT', [128, 4, L], BF16)
    V = P.sb('V', [128, NB, 4, 130], BF16)
    for h in range(4):
        P.dma('sp', qT[:, h, :], sc['daqk'][h * 128:(h + 1) * 128, :], reads=[sc['daqk']], writes=[qT], keep=True)
        P.dma('sp', kT[:, h, :], sc['daqk'][512 + h * 128:512 + (h + 1) * 128, :], reads=[sc['daqk']], writes=[kT], keep=True)
        P.dma('sp', V[:, :, h, 0:128], sc['dav'][:, h * 128:(h + 1) * 128].rearrange("(n p) d -> p n d", p=128),
              reads=[sc['dav']], writes=[V], keep=True)
    P.op('pool', lambda E: E.memset(V[:, :, :, 128:129], 1.0), writes=[V], keep=True)
    abias = P.sb('abias', [128, 128])
    P.dma('sp', abias[:, :], abias_ap, writes=[abias])
    cmf = P.sb('cmf', [128, 256])
    P.dma('sp', cmf[:, :], cmask_ap, writes=[cmf])
    cm = P.sb('cm', [128, 256], BF16)
    P.op('dve', lambda E: E.tensor_copy(cm[:, :], cmf[:, :]), reads=[cmf], writes=[cm])
    l4 = P.sb('lam4', [128, 4, 64])
    for i in range(4):
        P.dma('sp', l4[:, i, :], lam_aps[i].partition_broadcast(128), writes=[l4], keep=True)
    lp = P.sb('lamp', [128, 2, 64])
    P.op('dve', lambda E: E.tensor_tensor(lp[:, 0, :], l4[:, 0, :], l4[:, 1, :], ALU.mult), reads=[l4], writes=[lp], keep=True)
    P.op('dve', lambda E: E.tensor_tensor(lp[:, 1, :], l4[:, 2, :], l4[:, 3, :], ALU.mult), reads=[l4], writes=[lp], keep=True)
    ls = P.sb('lams', [128, 2])
    P.op('dve', lambda E: E.tensor_reduce(ls[:, :], lp[:, :, :], AX.X, ALU.add), reads=[lp], writes=[ls])
    P.op('act', lambda E: E.activation(out=ls[:, :], in_=ls[:, :], func=AF.Exp), reads=[ls], writes=[ls])
    nlam = P.sb('nlam', [128, 1])
    P.op('dve', lambda E: E.tensor_tensor(nlam[:, :], ls[:, 1:2], ls[:, 0:1], ALU.subtract), reads=[ls], writes=[nlam])
    P.op('dve', lambda E: E.tensor_scalar(nlam[:, :], nlam[:, :], -lam_init, None, ALU.add), reads=[nlam], writes=[nlam])
    gv = P.sb('gv', [128, 128])
    P.dma('sp', gv[:, :], normg_ap.partition_broadcast(128), writes=[gv])
    P.op('act', lambda E: E.mul(gv[:, :], gv[:, :], 1.0 - lam_init), reads=[gv], writes=[gv])

    psS = [P.ps('psS', [128, 512]) for _ in range(3)]
    psO = [P.ps('psO', [128, 512]) for _ in range(4)]
    psT = P.ps('psTd', [128, 1024], BF16)
    pT = [P.sb('pT', [128, 512], BF16) for _ in range(3)]
    rl = [P.sb('rl', [128, 4]) for _ in range(4)]
    oa = [P.sb('oa', [128, 128]) for _ in range(4)]
    oo = [P.sb('oo', [128, 128]) for _ in range(4)]
    sq = [P.sb('sq', [128, 128]) for _ in range(4)]
    ob = [P.sb('ob', [128, 128], BF16) for _ in range(4)]
    yst = [P.sb('yst', [128, 128], BF16) for _ in range(4)]
    qz = [P.sb('qz', [128, 512], BF16) for _ in range(2)]
    for e_ in range(2):
        P.op('pool', lambda E, e_=e_: E.memset(qz[e_][:, :], 0.0), writes=[qz[e_]])
    NQ = L // 256
    steps = []
    for Q in range(NQ):
        for h in range(4):
            for kj in range(2 * Q + 2):
                steps.append((Q, h, kj))
    state = {'ne': 0, 'n': 0}
    pend = []
    ev0 = [P.sb('ev0', [128, 129]) for _ in range(4)]
    ev1 = [P.sb('ev1', [128, 129]) for _ in range(4)]

    def emit_qk(n):
        Q, h, kj = steps[n]
        g = (Q * 4 + h) % 2
        if kj == 0:
            for m in range(2):
                P.op('pool', lambda E, m=m: E.tensor_copy(qz[g][m * 64:(m + 1) * 64, m * 256:(m + 1) * 256],
                                                          qT[m * 64:(m + 1) * 64, h, Q * 256:(Q + 1) * 256]),
                     reads=[qT], writes=[qz[g]], keep=(m == 1))
        ps = psS[n % 3]
        P.op('pe', lambda E: E.matmul(ps[:, :], kT[:, h, kj * 128:(kj + 1) * 128], qz[g][:, :], start=True, stop=True),
             reads=[kT, qz[g]], writes=[ps])

    def emit_rest(n):
        Q, h, kj = steps[n]
        ps = psS[n % 3]
        pt = pT[n % 3]
        bi = h * 32 + (2 * Q + 1 - kj)
        P.op('act', lambda E: E.activation(out=pt[:, :], in_=ps[:, :], func=AF.Exp, bias=abias[:, bi:bi + 1]),
             reads=[ps, abias], writes=[pt])
        if kj >= 2 * Q:
            sbm = kj - 2 * Q
            P.op('pool', lambda E: E.tensor_tensor(pt[:, :].rearrange("p (m c) -> p m c", m=2)[:, :, sbm * 128:(sbm + 1) * 128],
                                                   pt[:, :].rearrange("p (m c) -> p m c", m=2)[:, :, sbm * 128:(sbm + 1) * 128],
                                                   cm[:, :].rearrange("p (m c) -> p m c", m=2), ALU.mult),
                 reads=[pt, cm], writes=[pt])
        for sb in range(2):
            if kj > 2 * Q + sb:
                continue
            for m in range(2):
                po = psO[m * 2 + sb]
                P.op('pe', lambda E, m=m, sb=sb, po=po: E.matmul(po[:, 0:129], pt[:, m * 256 + sb * 128:m * 256 + (sb + 1) * 128],
                                                                 V[:, kj, h, 0:129], start=(kj == 0), stop=(kj == 2 * Q + sb)),
                     reads=[pt, V], writes=[po], inc=(kj == 2 * Q + sb))
            if kj == 2 * Q + sb:
                epilogue(Q, h, sb)
        while pend and pend[0][0] <= n:
            pend.pop(0)[1]()

    def epilogue(Q, h, sb):
        while len(pend) > 2:
            pend.pop(0)[1]()
        e = state['ne'] % 4
        state['ne'] += 1
        po0, po1 = psO[sb], psO[2 + sb]
        qi = 2 * Q + sb
        r_, oa_, oo_, sq_, ob_, ys_ = rl[e], oa[e], oo[e], sq[e], ob[e], yst[e]
        e0_, e1_ = ev0[e], ev1[e]
        P.op('dve', lambda E: E.tensor_copy(e0_[:, :], po0[:, 0:129]), reads=[po0], writes=[e0_])
        P.op('act', lambda E: E.copy(e1_[:, :], po1[:, 0:129]), reads=[po1], writes=[e1_])
        P.op('dve', lambda E: E.reciprocal(r_[:, 0:1], e0_[:, 128:129]), reads=[e0_], writes=[r_])
        P.op('dve', lambda E: E.reciprocal(r_[:, 1:2], e1_[:, 128:129]), reads=[e1_], writes=[r_], keep=True)
        P.op('dve', lambda E: E.tensor_tensor(r_[:, 1:2], r_[:, 1:2], nlam[:, 0:1], ALU.mult), reads=[r_, nlam], writes=[r_])
        P.op('act', lambda E: E.activation(out=oa_[:, :], in_=e0_[:, 0:128], func=AF.Copy, scale=r_[:, 0:1]), reads=[e0_, r_], writes=[oa_])
        P.op('dve', lambda E: E.scalar_tensor_tensor(oo_[:, :], e1_[:, 0:128], r_[:, 1:2], oa_[:, :], ALU.mult, ALU.add),
             reads=[e1_, r_, oa_], writes=[oo_])
        P.op('pool', lambda E: E.tensor_tensor(sq_[:, :], oo_[:, :], oo_[:, :], ALU.mult), reads=[oo_], writes=[sq_])
        P.op('dve', lambda E: E.tensor_reduce(r_[:, 2:3], sq_[:, :], AX.X, ALU.add), reads=[sq_], writes=[r_], keep=True)
        P.op('dve', lambda E: E.tensor_scalar(r_[:, 2:3], r_[:, 2:3], 1.0 / 128, 1e-5, ALU.mult, ALU.add), reads=[r_], writes=[r_])
        P.op('act', lambda E: E.activation(out=r_[:, 2:3], in_=r_[:, 2:3], func=AF.Ln), reads=[r_], writes=[r_])
        P.op('act', lambda E: E.activation(out=r_[:, 2:3], in_=r_[:, 2:3], func=AF.Exp, scale=-0.5), reads=[r_], writes=[r_])
        P.op('dve', lambda E: E.scalar_tensor_tensor(ob_[:, :], oo_[:, :], r_[:, 2:3], gv[:, :], ALU.mult, ALU.mult),
             reads=[oo_, r_, gv], writes=[ob_])
        def tail():
            P.op('pe', lambda E: E.transpose(psT[:, 0:128], ob_[:, :], cx.ident[:, :]), reads=[ob_, cx.ident], writes=[psT])
            P.op('act', lambda E: E.copy(ys_[:, :], psT[:, 0:128]), reads=[psT], writes=[ys_])
            P.dma('sp', sc['ybT'][h * 128:(h + 1) * 128, qi * 128:(qi + 1) * 128], ys_[:, :],
                  reads=[ys_], writes=[sc['ybT']], keep=True)
        pend.append((state['n'] + 6, tail))

    emit_qk(0)
    if len(steps) > 1:
        emit_qk(1)
    for n in range(len(steps)):
        state['n'] = n
        if n + 2 < len(steps):
            emit_qk(n + 2)
        emit_rest(n)
    while pend:
        pend.pop(0)[1]()


def neumann_TT(P, cx, X0, XT0, bufs, psN, H, C, nstage=5):
    TT = bufs['TT'][0]
    for h in range(H):
        P.op('pool', lambda E, h=h: E.tensor_tensor(TT[:, h, :], XT0[:, h, :], cx.identf[0:C, 0:C], ALU.add),
             reads=[XT0, cx.identf], writes=[TT], keep=(h > 0))
    X, XT = X0, XT0
    for k in range(1, nstage + 1):
        Xn = bufs['X'][k % 2]
        XTn = bufs['XT'][k % 2]
        TTn = bufs['TT'][k % 2]
        pa = psN[0]
        for h in range(H):
            P.op('pe', lambda E, h=h, X=X, XT=XT, pa=pa: E.matmul(pa[0:C, h * C:(h + 1) * C], XT[:, h, :], X[:, h, :],
                                                                   start=True, stop=True),
                 reads=[X, XT], writes=[pa], inc=(h == H - 1))
        P.op('act', lambda E, Xn=Xn, pa=pa: E.copy(Xn[:, :, :].rearrange("p h c -> p (h c)"), pa[0:C, 0:H * C]),
             reads=[pa], writes=[Xn])
        if k < nstage:
            pb = psN[1]
            for h in range(H):
                P.op('pe', lambda E, h=h, X=X, XT=XT, pb=pb: E.matmul(pb[0:C, h * C:(h + 1) * C], X[:, h, :], XT[:, h, :],
                                                                       start=True, stop=True),
                     reads=[X, XT], writes=[pb], inc=(h == H - 1))
            P.op('dve', lambda E, XTn=XTn, pb=pb: E.tensor_copy(XTn[:, :, :].rearrange("p h c -> p (h c)"), pb[0:C, 0:H * C]),
                 reads=[pb], writes=[XTn])
        pc = psN[2]
        for h in range(H):
            P.op('pe', lambda E, h=h, Xn=Xn, TT=TT, pc=pc: E.matmul(pc[0:C, h * C:(h + 1) * C], Xn[:, h, :], TT[:, h, :],
                                                                     start=True, stop=True),
                 reads=[Xn, TT], writes=[pc], inc=(h == H - 1))
        P.op('dve', lambda E, TTn=TTn, TT=TT, pc=pc: E.tensor_tensor(
            TTn[:, :, :].rearrange("p h c -> p (h c)"), pc[0:C, 0:H * C], TT[:, :, :].rearrange("p h c -> p (h c)"), ALU.add),
            reads=[pc, TT], writes=[TTn])
        X, XT, TT = Xn, XTn, TTn
    return TT


def tri_consts(C=64):
    i = np.arange(C)
    tri_le = (i[:, None] <= i[None, :]).astype(np.float32)
    negmask = np.where(i[:, None] >= i[None, :], 0.0, -30000.0).astype(np.float32)
    strict = (i[:, None] > i[None, :]).astype(np.float32)
    return tri_le, negmask, strict


def gdn_phase(P, cx, sc, convw_ap, alog_ap, dtb_ap, normg_ap, cst, L, hs=(0, 1, 2, 3), half=False, C=128):
    NCH = 512 // C
    H = len(hs)
    h0 = hs[0]
    HW_ = H * 128
    cl = list(hs) + [4 + h_ for h_ in hs] + [8 + h_ for h_ in hs]
    V_ = lambda fn, r, w, **k: P.op('dve', fn, reads=r, writes=w, **k)
    A_ = lambda fn, r, w, **k: P.op('act', fn, reads=r, writes=w, **k)
    G_ = lambda fn, r, w, **k: P.op('pool', fn, reads=r, writes=w, **k)
    T_ = lambda fn, r, w, **k: P.op('pe', fn, reads=r, writes=w, **k)
    fl = lambda b: b[:, :, :].rearrange("p h c -> p (h c)")
    tri = P.sb('tri', [C, C]); P.dma('sp', tri[:, :], cst['tri_le'], writes=[tri])
    nmk = P.sb('nmk', [C, H, C])
    smk = P.sb('smk', [C, H, C])
    for h in range(H):
        P.dma('sp', nmk[:, h, :], cst['negmask'], writes=[nmk], keep=True)
        P.dma('sp', smk[:, h, :], cst['strict'], writes=[smk], keep=True)
    ones = P.sb('ones', [128, 128]); G_(lambda E: E.memset(ones[:, :], 1.0), [], [ones])
    cw = P.sb('cw', [128, 4, 12])
    for i in range(4):
        P.dma('sp', cw[:, i, :], convw_ap[i, :].rearrange("(c p) -> p c", p=128), writes=[cw], keep=True,
              allow_slow_non_contiguous=True)
    al = P.sb('al', [C, 2, 4])
    P.dma('sp', al[:, 0, :], alog_ap.partition_broadcast(C), writes=[al], keep=True)
    P.dma('sp', al[:, 1, :], dtb_ap.partition_broadcast(C), writes=[al], keep=True)
    A_(lambda E: E.activation(out=al[:, 0, :], in_=al[:, 0, :], func=AF.Exp), [al], [al])
    ng = P.sb('ng', [C, H, 128])
    for h in range(H):
        P.dma('sp', ng[:, h, :], normg_ap.partition_broadcast(C), writes=[ng], keep=True)
    S = P.sb('S', [128, H, 128]); G_(lambda E: E.memset(S[:, :, :], 0.0), [], [S])
    Sb = P.sb('Sb', [128, H, 128], BF16); G_(lambda E: E.memset(Sb[:, :, :], 0.0), [], [Sb])
    X = P.sb('X', [128, 3 * H, 515])
    Y = P.sb('Y', [128, 3 * H, 512])
    sq = P.sb('sqg', [128, 512])
    rt = P.sb('rtg', [128, 512])
    qT = P.sb('qTg', [128, H, 512], BF16)
    kT = P.sb('kTg', [128, H, 512], BF16)
    vT = P.sb('vTg', [128, H, 512], BF16)
    gab = P.sb('gabt', [C, NCH, 8])
    gz = P.sb('gzt', [C, NCH, HW_])
    g = P.sb('g', [C, NCH, H]); be = P.sb('be', [C, NCH, H]); nbe = P.sb('nbe', [C, NCH, H])
    Gc = P.sb('Gc', [C, NCH, H]); nGc = P.sb('nGc', [C, NCH, H]); eG = P.sb('eG', [C, NCH, H]); beG = P.sb('beG', [C, NCH, H])
    e2 = P.sb('e2', [C, NCH, H]); gl = P.sb('gl', [128, NCH, H]); glc = P.sb('glc', [128, NCH, H])
    def dbl(name, shape, dt=F32):
        return [P.sb(name, shape, dt) for _ in range(2)]
    ktok = dbl('ktok', [C, H, 128], BF16); vtok = dbl('vtok', [C, H, 128], BF16)
    dg = dbl('dg', [C, H, C]); Dm = dbl('Dm', [C, H, C]); Ds = dbl('Ds', [C, H, C])
    N0 = dbl('N0', [C, H, C], NDT); NT0 = dbl('NT0', [C, H, C], NDT); itr = dbl('itr', [C, H, C], BF16); itT = dbl('itT', [C, H, C], BF16)
    nb = {'X': dbl('nX', [C, H, C], NDT), 'XT': dbl('nXT', [C, H, C], NDT), 'TT': dbl('nTT', [C, H, C], NDT)}
    for kk_ in nb:
        for b_ in nb[kk_]:
            G_(lambda E, b_=b_: E.memset(b_[:, :, :], 0.0), [], [b_])
    if C == 128:
        Nd = dbl('Nd', [C, H, C], NDT); NTd = dbl('NTd', [C, H, C], NDT); No = dbl('No', [C, H, C], NDT)
        Td = dbl('Td', [C, H, C], NDT); Y1 = dbl('Y1', [C, H, C], NDT)
        bdm = P.sb('bdm', [C, H, C], NDT)
        bdf = P.sb('bdf', [C, H, C])
        for h in range(H):
            P.dma('sp', bdf[:, h, :], cst['bd'], writes=[bdf], keep=True)
        V_(lambda E: E.tensor_copy(bdm[:, :, :], bdf[:, :, :]), [bdf], [bdm])
    TTb = dbl('TTb', [C, H, C], BF16); kbg = dbl('kbg', [C, H, 128], BF16); vb = dbl('vb', [C, H, 128], BF16)
    kd = dbl('kd', [C, H, 128], BF16); nWT = dbl('nWT', [128, H, C], BF16); vn = dbl('vn', [C, H, 128], BF16)
    oi = dbl('oi', [C, H, 128]); oo = dbl('oog', [C, H, 128]); osq = dbl('osq', [C, H, 128]); rr = dbl('rrg', [C, H])
    yc = dbl('yc', [C, H, 128], BF16); ycs = dbl('ycs', [128, H, C], BF16)
    if half:
        f_ = [P.ps('gpF', [128, 512]) for _ in range(3)]
        psA = [f_[0], f_[1]]
        psN = [f_[0], f_[1], f_[2]]
    else:
        psA = [P.ps('gpA', [128, 512]) for _ in range(2)]
        psN = [P.ps('gpN', [128, 512]) for _ in range(3)]
    psB = [P.ps('gpB', [128, 1024], BF16) for _ in range(1)]
    psC = [f_[2], f_[1]] if half else [P.ps('gpC', [128, 512]) for _ in range(2)]
    for t in range(L // 512):
        c0 = t * 512
        if t == 0:
            G_(lambda E: E.memset(X[:, :, 0:3], 0.0), [], [X])
            for i_, c_ in enumerate(cl):
                P.dma('sp', X[:, i_, 3:515], sc['gqkv'][c_ * 128:(c_ + 1) * 128, 0:512], reads=[sc['gqkv']], writes=[X], keep=True)
        else:
            for i_, c_ in enumerate(cl):
                P.dma('sp', X[:, i_, :], sc['gqkv'][c_ * 128:(c_ + 1) * 128, c0 - 3:c0 + 512], reads=[sc['gqkv']], writes=[X], keep=(i_ > 0))
        P.dma('sp', gab[:, :, :], sc['gab'][c0:c0 + 512, :].rearrange("(c p) n -> p c n", p=C), reads=[sc['gab']], writes=[gab])
        P.dma('sp', gz[:, :, :], sc['gz'][c0:c0 + 512, h0 * 128:h0 * 128 + HW_].rearrange("(c p) n -> p c n", p=C), reads=[sc['gz']], writes=[gz])
        for c, cg in enumerate(cl):
            V_(lambda E, c=c, cg=cg: E.tensor_scalar(Y[:, c, :], X[:, c, 0:512], cw[:, 0, cg:cg + 1], None, ALU.mult), [X, cw], [Y], keep=(c > 0))
            for i in range(1, 4):
                V_(lambda E, c=c, cg=cg, i=i: E.scalar_tensor_tensor(Y[:, c, :], X[:, c, i:i + 512], cw[:, i, cg:cg + 1], Y[:, c, :],
                                                             ALU.mult, ALU.add), [X, cw, Y], [Y], keep=True)
        A_(lambda E: E.activation(out=Y[:, :, :], in_=Y[:, :, :], func=AF.Silu), [Y], [Y])
        for c in range(2 * H):
            G_(lambda E, c=c: E.tensor_tensor(sq[:, :], Y[:, c, :], Y[:, c, :], ALU.mult), [Y], [sq])
            pa = psA[c % 2]
            T_(lambda E, pa=pa: E.matmul(pa[:, :], ones[:, :], sq[:, :], start=True, stop=True), [ones, sq], [pa])
            V_(lambda E, pa=pa: E.tensor_scalar(rt[:, :], pa[:, :], 1e-6, None, ALU.add), [pa], [rt])
            A_(lambda E: E.activation(out=rt[:, :], in_=rt[:, :], func=AF.Sqrt), [rt], [rt])
            V_(lambda E: E.reciprocal(rt[:, :], rt[:, :]), [rt], [rt])
            dst = qT if c < H else kT
            scl = (128 ** -0.5) if c < H else 1.0
            V_(lambda E, c=c, dst=dst, scl=scl: E.scalar_tensor_tensor(dst[:, c % H, :], Y[:, c, :], scl, rt[:, :], ALU.mult, ALU.mult),
               [Y, rt], [dst], keep=True)
        A_(lambda E: E.copy(vT[:, :, :], Y[:, 2 * H:3 * H, :]), [Y], [vT])
        V_(lambda E: E.tensor_tensor(g[:, :, :], gab[:, :, h0:h0 + H], al[:, 1:2, h0:h0 + H].to_broadcast([C, NCH, H]), ALU.add), [gab, al], [g])
        A_(lambda E: E.activation(out=g[:, :, :], in_=g[:, :, :], func=AF.Exp), [g], [g])
        V_(lambda E: E.tensor_scalar(g[:, :, :], g[:, :, :], 1.0, None, ALU.add), [g], [g])
        A_(lambda E: E.activation(out=g[:, :, :], in_=g[:, :, :], func=AF.Ln), [g], [g])
        V_(lambda E: E.tensor_tensor(g[:, :, :], g[:, :, :], al[:, 0:1, h0:h0 + H].to_broadcast([C, NCH, H]), ALU.mult), [g, al], [g])
        V_(lambda E: E.tensor_scalar(g[:, :, :], g[:, :, :], -1.0, None, ALU.mult), [g], [g])
        A_(lambda E: E.activation(out=be[:, :, :], in_=gab[:, :, 4 + h0:4 + h0 + H], func=AF.Sigmoid), [gab], [be])
        V_(lambda E: E.tensor_scalar(nbe[:, :, :], be[:, :, :], -1.0, None, ALU.mult), [be], [nbe])
        pa = psA[0]
        T_(lambda E, pa=pa: E.matmul(pa[0:C, 0:NCH * H], tri[:, :], fl(g), start=True, stop=True), [tri, g], [pa])
        V_(lambda E, pa=pa: E.tensor_copy(fl(Gc), pa[0:C, 0:NCH * H]), [pa], [Gc])
        pb = psA[1]
        T_(lambda E, pb=pb: E.matmul(pb[:, 0:NCH * H], ones[0:C, :], fl(g), start=True, stop=True), [ones, g], [pb])
        V_(lambda E, pb=pb: E.tensor_copy(fl(glc), pb[:, 0:NCH * H]), [pb], [glc])
        V_(lambda E: E.tensor_scalar(nGc[:, :, :], Gc[:, :, :], -1.0, None, ALU.mult), [Gc], [nGc])
        A_(lambda E: E.activation(out=eG[:, :, :], in_=Gc[:, :, :], func=AF.Exp), [Gc], [eG])
        A_(lambda E: E.activation(out=gl[:, :, :], in_=glc[:, :, :], func=AF.Exp), [glc], [gl])
        V_(lambda E: E.tensor_tensor(e2[:, :, :], glc[0:C, :, :], Gc[:, :, :], ALU.subtract), [glc, Gc], [e2])
        A_(lambda E: E.activation(out=e2[:, :, :], in_=e2[:, :, :], func=AF.Exp), [e2], [e2])
        V_(lambda E: E.tensor_tensor(beG[:, :, :], be[:, :, :], eG[:, :, :], ALU.mult), [be, eG], [beG])
        for ci in range(NCH):
            p = ci % 2
            cs = slice(ci * C, (ci + 1) * C)
            pT_ = psB[0]
            for h in range(H):
                T_(lambda E, h=h, cs=cs, pT_=pT_: E.transpose(pT_[0:C, h * 128:(h + 1) * 128], kT[:, h, cs], cx.ident[:, :]),
                   [kT, cx.ident], [pT_], inc=(h == H - 1))
            V_(lambda E, p=p, pT_=pT_: E.tensor_copy(fl(ktok[p]), pT_[0:C, 0:HW_]), [pT_], [ktok[p]])
            for h in range(H):
                T_(lambda E, h=h, cs=cs, pT_=pT_: E.transpose(pT_[0:C, h * 128:(h + 1) * 128], vT[:, h, cs], cx.ident[:, :]),
                   [vT, cx.ident], [pT_], inc=(h == H - 1))
            A_(lambda E, p=p, pT_=pT_: E.copy(fl(vtok[p]), pT_[0:C, 0:HW_]), [pT_], [vtok[p]])
            for h in range(H):
                A_(lambda E, h=h, p=p, ci=ci: E.activation(out=dg[p][:, h, :], in_=cx.identf[0:C, 0:C], func=AF.Copy, scale=nGc[:, ci, h:h + 1]),
                   [cx.identf, nGc], [dg[p]], keep=(h > 0))
            pd = psC[0]
            T_(lambda E, p=p, pd=pd: E.matmul(pd[0:C, 0:H * C], ones[0:C, 0:C], fl(dg[p]), start=True, stop=True), [ones, dg[p]], [pd])
            for h in range(H):
                V_(lambda E, h=h, p=p, ci=ci, pd=pd: E.scalar_tensor_tensor(Dm[p][:, h, :], pd[0:C, h * C:(h + 1) * C], Gc[:, ci, h:h + 1],
                                                                          nmk[:, h, :], ALU.add, ALU.add), [pd, Gc, nmk], [Dm[p]], keep=(h > 0))
            A_(lambda E, p=p: E.activation(out=Dm[p][:, :, :], in_=Dm[p][:, :, :], func=AF.Exp), [Dm[p]], [Dm[p]])
            G_(lambda E, p=p: E.tensor_tensor(Ds[p][:, :, :], Dm[p][:, :, :], smk[:, :, :], ALU.mult), [Dm[p], smk], [Ds[p]])
            pk = psC[1]
            for h in range(H):
                T_(lambda E, h=h, cs=cs, pk=pk: E.matmul(pk[0:C, h * C:(h + 1) * C], kT[:, h, cs], kT[:, h, cs], start=True, stop=True),
                   [kT], [pk], inc=False)
            for h in range(H):
                T_(lambda E, h=h, cs=cs, pk=pk: E.matmul(pk[0:C, (H + h) * C:(H + h + 1) * C], qT[:, h, cs], kT[:, h, cs], start=True, stop=True),
                   [kT, qT], [pk], inc=(h == H - 1))
            for h in range(H):
                V_(lambda E, h=h, p=p, ci=ci, pk=pk: E.scalar_tensor_tensor(N0[p][:, h, :], pk[0:C, h * C:(h + 1) * C], nbe[:, ci, h:h + 1],
                                                                          Ds[p][:, h, :], ALU.mult, ALU.mult), [pk, nbe, Ds[p]], [N0[p]], keep=(h > 0))
            V_(lambda E, p=p, pk=pk: E.tensor_tensor(fl(itr[p]), pk[0:C, H * C:2 * H * C], fl(Dm[p]), ALU.mult), [pk, Dm[p]], [itr[p]])
            pn = psB[0]
            for h in range(H):
                T_(lambda E, h=h, p=p, pn=pn: E.transpose(pn[0:C, h * C:(h + 1) * C], N0[p][:, h, :], cx.ident[0:C, 0:C]),
                   [N0[p], cx.ident], [pn], inc=(h == H - 1))
            A_(lambda E, p=p, pn=pn: E.copy(fl(NT0[p]), pn[0:C, 0:H * C]), [pn], [NT0[p]])
            pT_ = psB[0]
            for h in range(H):
                T_(lambda E, h=h, p=p, pT_=pT_: E.transpose(pT_[0:C, h * C:(h + 1) * C], itr[p][:, h, :], cx.ident[0:C, 0:C]),
                   [itr[p], cx.ident], [pT_], inc=(h == H - 1))
            A_(lambda E, p=p, pT_=pT_: E.copy(fl(itT[p]), pT_[0:C, 0:H * C]), [pT_], [itT[p]])
            if C == 128:
                G_(lambda E, p=p: E.tensor_tensor(Nd[p][:, :, :], N0[p][:, :, :], bdm[:, :, :], ALU.mult), [N0[p], bdm], [Nd[p]])
                G_(lambda E, p=p: E.tensor_tensor(NTd[p][:, :, :], NT0[p][:, :, :], bdm[:, :, :], ALU.mult), [NT0[p], bdm], [NTd[p]])
                V_(lambda E, p=p: E.tensor_tensor(No[p][:, :, :], N0[p][:, :, :], Nd[p][:, :, :], ALU.subtract), [N0[p], Nd[p]], [No[p]])
                TTd = neumann_TT(P, cx, Nd[p], NTd[p], nb, psN, H, C, nstage=5)
                pT_ = psB[0]
                for h in range(H):
                    T_(lambda E, h=h, pT_=pT_, TTd=TTd: E.transpose(pT_[0:C, h * C:(h + 1) * C], TTd[:, h, :], cx.ident[0:C, 0:C]),
                       [TTd, cx.ident], [pT_], inc=(h == H - 1))
                A_(lambda E, p=p, pT_=pT_: E.copy(fl(Td[p]), pT_[0:C, 0:H * C]), [pT_], [Td[p]])
                py1 = psN[0]
                for h in range(H):
                    T_(lambda E, h=h, p=p, py1=py1, TTd=TTd: E.matmul(py1[0:C, h * C:(h + 1) * C], No[p][:, h, :], TTd[:, h, :], start=True, stop=True),
                       [No[p], TTd], [py1], inc=(h == H - 1))
                V_(lambda E, p=p, py1=py1: E.tensor_copy(fl(Y1[p]), py1[0:C, 0:H * C]), [py1], [Y1[p]])
                pc2 = psN[1]
                for h in range(H):
                    T_(lambda E, h=h, p=p, pc2=pc2: E.matmul(pc2[0:C, h * C:(h + 1) * C], Td[p][:, h, :], Y1[p][:, h, :], start=True, stop=True),
                       [Td[p], Y1[p]], [pc2], inc=(h == H - 1))
                V_(lambda E, p=p, pc2=pc2, TTd=TTd: E.tensor_tensor(fl(TTb[p]), pc2[0:C, 0:H * C], fl(TTd), ALU.add), [pc2, TTd], [TTb[p]])
                TT = TTb[p]
            else:
                TT = neumann_TT(P, cx, N0[p], NT0[p], nb, psN, H, C, nstage=5)
                A_(lambda E, p=p, TT=TT: E.copy(TTb[p][:, :, :], TT[:, :, :]), [TT], [TTb[p]])
            if t == 0 and ci == 0:
                P.dbg('qT', qT, qT[:, :, :], [128, H, 512], BF16)
                P.dbg('kT', kT, kT[:, :, :], [128, H, 512], BF16)
                P.dbg('g', g, g[:, :, :], [C, NCH, H])
                P.dbg('Gc', Gc, Gc[:, :, :], [C, NCH, H])
                P.dbg('be', be, be[:, :, :], [C, NCH, H])
                P.dbg('Dm', Dm[p], Dm[p][:, :, :], [C, H, C])
                P.dbg('N0', N0[p], N0[p][:, :, :], [C, H, C])
                P.dbg('NT0', NT0[p], NT0[p][:, :, :], [C, H, C])
                P.dbg('TT', TT, TT[:, :, :], [C, H, C])
                P.dbg('ktok', ktok[p], ktok[p][:, :, :], [C, H, 128], BF16)
            for h in range(H):
                V_(lambda E, h=h, p=p, ci=ci: E.tensor_scalar(kbg[p][:, h, :], ktok[p][:, h, :], beG[:, ci, h:h + 1], None, ALU.mult),
                   [ktok[p], beG], [kbg[p]], keep=(h > 0))
                A_(lambda E, h=h, p=p, ci=ci: E.activation(out=vb[p][:, h, :], in_=vtok[p][:, h, :], func=AF.Copy, scale=be[:, ci, h:h + 1]),
                   [vtok[p], be], [vb[p]], keep=(h > 0))
                A_(lambda E, h=h, p=p, ci=ci: E.activation(out=kd[p][:, h, :], in_=ktok[p][:, h, :], func=AF.Copy, scale=e2[:, ci, h:h + 1]),
                   [ktok[p], e2], [kd[p]], keep=(h > 0))
            pw = psC[0]
            for h in range(H):
                T_(lambda E, h=h, p=p, pw=pw: E.matmul(pw[:, h * C:(h + 1) * C], kbg[p][:, h, :], TTb[p][:, h, :], start=True, stop=True),
                   [kbg[p], TTb[p]], [pw], inc=(h == H - 1))
            A_(lambda E, p=p, pw=pw: E.mul(fl(nWT[p]), pw[:, 0:H * C], -1.0), [pw], [nWT[p]])
            pv = psA[0]
            for h in range(H):
                T_(lambda E, h=h, p=p, pv=pv: E.matmul(pv[0:C, h * 128:(h + 1) * 128], TTb[p][:, h, :], vb[p][:, h, :], start=True, stop=False),
                   [TTb[p], vb[p]], [pv], inc=False)
                T_(lambda E, h=h, p=p, pv=pv: E.matmul(pv[0:C, h * 128:(h + 1) * 128], nWT[p][:, h, :], Sb[:, h, :], start=False, stop=True),
                   [nWT[p], Sb], [pv], inc=(h == H - 1))
            V_(lambda E, p=p, pv=pv: E.tensor_copy(fl(vn[p]), pv[0:C, 0:HW_]), [pv], [vn[p]])
            po = psA[1]
            for h in range(H):
                T_(lambda E, h=h, p=p, po=po: E.matmul(po[0:C, h * 128:(h + 1) * 128], itT[p][:, h, :], vn[p][:, h, :], start=True, stop=True),
                   [itT[p], vn[p]], [po], inc=(h == H - 1))
            A_(lambda E, p=p, po=po: E.copy(fl(oi[p]), po[0:C, 0:HW_]), [po], [oi[p]])
            pq = psC[1]
            for h in range(H):
                T_(lambda E, h=h, cs=cs, pq=pq: E.matmul(pq[0:C, h * 128:(h + 1) * 128], qT[:, h, cs], Sb[:, h, :], start=True, stop=True),
                   [qT, Sb], [pq], inc=(h == H - 1))
            for h in range(H):
                V_(lambda E, h=h, p=p, ci=ci, pq=pq: E.scalar_tensor_tensor(oo[p][:, h, :], pq[0:C, h * 128:(h + 1) * 128], eG[:, ci, h:h + 1],
                                                                          oi[p][:, h, :], ALU.mult, ALU.add), [pq, eG, oi[p]], [oo[p]], keep=(h > 0))
            ps_ = psC[0]
            for h in range(H):
                T_(lambda E, h=h, p=p, ps_=ps_: E.matmul(ps_[:, h * 128:(h + 1) * 128], kd[p][:, h, :], vn[p][:, h, :], start=True, stop=True),
                   [kd[p], vn[p]], [ps_], inc=(h == H - 1))
            for h in range(H):
                V_(lambda E, h=h, ci=ci, ps_=ps_: E.scalar_tensor_tensor(S[:, h, :], S[:, h, :], gl[:, ci, h:h + 1], ps_[:, h * 128:(h + 1) * 128],
                                                                       ALU.mult, ALU.add), [S, gl, ps_], [S], keep=(h > 0))
            A_(lambda E: E.copy(Sb[:, :, :], S[:, :, :]), [S], [Sb])
            if t == 0 and ci == 0:
                P.dbg('vn', vn[p], vn[p][:, :, :], [C, H, 128], BF16)
                P.dbg('oo', oo[p], oo[p][:, :, :], [C, H, 128])
                P.dbg('S', S, S[:, :, :], [128, H, 128])
            G_(lambda E, p=p: E.tensor_tensor(osq[p][:, :, :], oo[p][:, :, :], oo[p][:, :, :], ALU.mult), [oo[p]], [osq[p]])
            V_(lambda E, p=p: E.tensor_reduce(rr[p][:, :], osq[p][:, :, :], AX.X, ALU.add), [osq[p]], [rr[p]])
            V_(lambda E, p=p: E.tensor_scalar(rr[p][:, :], rr[p][:, :], 1.0 / 128, 1e-6, ALU.mult, ALU.add), [rr[p]], [rr[p]])
            A_(lambda E, p=p: E.activation(out=rr[p][:, :], in_=rr[p][:, :], func=AF.Ln), [rr[p]], [rr[p]])
            A_(lambda E, p=p: E.activation(out=rr[p][:, :], in_=rr[p][:, :], func=AF.Exp, scale=-0.5), [rr[p]], [rr[p]])
            for h in range(H):
                V_(lambda E, h=h, p=p: E.scalar_tensor_tensor(oo[p][:, h, :], oo[p][:, h, :], rr[p][:, h:h + 1], ng[:, h, :], ALU.mult, ALU.mult),
                   [oo[p], rr[p], ng], [oo[p]], keep=(h > 0))
            V_(lambda E, p=p, ci=ci: E.tensor_tensor(fl(yc[p]), fl(oo[p]), gz[:, ci, :], ALU.mult), [oo[p], gz], [yc[p]])
            pT_ = psB[0]
            for h in range(H):
                T_(lambda E, h=h, p=p, pT_=pT_: E.transpose(pT_[:, h * C:(h + 1) * C], yc[p][:, h, :], cx.ident[0:C, 0:C]),
                   [yc[p], cx.ident], [pT_], inc=(h == H - 1))
            A_(lambda E, p=p, pT_=pT_: E.copy(fl(ycs[p]), pT_[:, 0:H * C]), [pT_], [ycs[p]])
            P.dma('sp', sc['ycT'][h0 * 128:h0 * 128 + HW_, c0 + ci * C:c0 + (ci + 1) * C].rearrange("(h p) t -> p h t", p=128), ycs[p][:, :, :],
                  reads=[ycs[p]], writes=[sc['ycT']], keep=True)


def rw_consts(C=64):
    i = np.arange(C)
    su = (i[:, None] < i[None, :]).astype(np.float32)
    iu = (i[:, None] <= i[None, :]).astype(np.float32)
    sl = (i[:, None] > i[None, :]).astype(np.float32)
    bo = np.zeros((128, 128), np.float32)
    bo[:64, :64] = 1
    bo[64:, 64:] = 1
    hs = np.zeros((128, 2), np.float32)
    hs[:64, 0] = 1
    hs[64:, 1] = 1
    rm = np.ones((128, 512), np.float32)
    rm[:, ::C] = 0
    return {'nsu': -su, 'su': su, 'iu': iu, 'niu': -iu, 'nsl': -sl, 'bo': bo, 'hs': hs, 'rm': rm}


def rwkv_phase(P, cx, sc, w, cst, L, js=(0, 1, 2, 3), half=False, pipe=False, C=128):
    NV = 64
    NJ = len(js)
    H = 2 * NJ
    TW = 256
    NB = 2 if C == 64 else 1
    H2 = NB * H
    cl = list(js) + [4 + j for j in js] + [8 + j for j in js] + [12, 13]
    NCk = len(cl)
    iR, iK, iV, iW, iG = 0, NJ, 2 * NJ, 3 * NJ, 3 * NJ + 1
    g0, g1 = js[0] * 128, (js[-1] + 1) * 128
    GW = g1 - g0
    V_ = lambda fn, r, w_, **k: P.op('dve', fn, reads=r, writes=w_, **k)
    A_ = lambda fn, r, w_, **k: P.op('act', fn, reads=r, writes=w_, **k)
    G_ = lambda fn, r, w_, **k: P.op('pool', fn, reads=r, writes=w_, **k)
    T_ = lambda fn, r, w_, **k: P.op('pe', fn, reads=r, writes=w_, **k)
    fl = lambda b: b[:, :, :].rearrange("p h c -> p (h c)")
    msk = {}
    for n in ('nsu', 'su', 'iu', 'niu', 'nsl'):
        m = P.sb('m' + n, [C, H2, C], BF16)
        for h in range(H2):
            P.dma('pool', m[:, h, :], cst[n], writes=[m], keep=True)
        msk[n] = m
    bo = P.sb('bo', [128, 128]); P.dma('sp', bo[:, :], cst['bo'], writes=[bo])
    hsf = P.sb('hsf', [128, 2]); P.dma('sp', hsf[:, :], cst['hs'], writes=[hsf])
    rm = P.sb('rm', [128, TW]); P.dma('sp', rm[:, :], cst['rm'][:, 0:TW], writes=[rm])
    mu = P.sb('mu', [128, 14]); P.dma('sp', mu[:, :], w['mu'].rearrange("(c p) -> p c", p=128), writes=[mu], allow_slow_non_contiguous=True)
    pv = P.sb('pvec', [128, 7, 4])
    for i, n in enumerate(('w0', 'a0', 'k_k', 'k_a', 'r_k')):
        P.dma('sp', pv[:, i, :], w[n].rearrange("(c p) -> p c", p=128), writes=[pv], keep=True, allow_slow_non_contiguous=True)
    V_(lambda E: E.tensor_scalar(pv[:, 5, :], pv[:, 3, :], -1.0, 1.0, ALU.mult, ALU.add), [pv], [pv])
    wupf = P.sb('wupf', [128, 512]); wup = P.sb('wup', [128, 512], BF16)
    P.dma('sp', wupf[0:64, :], w['w_up'], writes=[wupf], keep=True)
    P.dma('sp', wupf[64:128, :], w['a_up'], writes=[wupf], keep=True)
    V_(lambda E: E.tensor_copy(wup[:, :], wupf[:, :]), [wupf], [wup])
    gupf = P.sb('gupf', [128, 512]); gup = P.sb('gup', [128, 512], BF16)
    P.dma('sp', gupf[:, :], w['g_up'], writes=[gupf])
    V_(lambda E: E.tensor_copy(gup[:, :], gupf[:, :]), [gupf], [gup])
    epst = P.sb('epst', [128, 1]); G_(lambda E: E.memset(epst[:, :], 1e-6), [], [epst])
    lng = P.sb('rlng', [C, GW]); lnb = P.sb('rlnb', [C, GW])
    P.dma('sp', lng[:, :], w['ln_g'][g0:g1].partition_broadcast(C), writes=[lng])
    P.dma('sp', lnb[:, :], w['ln_b'][g0:g1].partition_broadcast(C), writes=[lnb])
    M = P.sb('M', [128, NJ, 64]); G_(lambda E: E.memset(M[:, :, :], 0.0), [], [M])
    Mb = P.sb('Mb', [128, NJ, 64], BF16); G_(lambda E: E.memset(Mb[:, :, :], 0.0), [], [Mb])
    X = P.sb('rX', [128, NCk, TW + 1])
    Y = P.sb('rY', [128, NCk, TW])
    lw = P.sb('lw', [128, NJ, TW]); a_ = P.sb('ra', [128, NJ, TW])
    Gc = P.sb('rGc', [128, NJ, TW]); eP = P.sb('eP', [128, NJ, TW]); eN = P.sb('eN', [128, NJ, TW]); ePe = P.sb('ePe', [128, NJ, TW])
    kk = P.sb('kk', [128, NJ, TW]); t1 = P.sb('t1', [128, NJ, TW]); t2 = P.sb('t2', [128, TW])
    At = P.sb('At', [128, NJ, TW], BF16); Bt = P.sb('Bt', [128, NJ, TW], BF16); Kt = P.sb('Kt', [128, NJ, TW], BF16)
    Rt = P.sb('Rt', [128, NJ, TW], BF16); Kh = P.sb('Kh', [128, NJ, TW], BF16); Bh = P.sb('Bh', [128, NJ, TW], BF16)
    vT = P.sb('rvT', [128, NJ, TW], BF16); rkb = P.sb('rkb', [128, NJ, TW])
    AtZ = P.sb('AtZ', [128, NJ, 2, TW], BF16)
    BtZ = P.sb('BtZ', [128, NJ, 2, TW], BF16)
    RtZ = P.sb('RtZ', [128, NJ, 2, TW], BF16)
    MbZ = P.sb('MbZ', [128, NJ, 2, 64], BF16)
    G_(lambda E: E.memset(MbZ[:, :, :, :], 0.0), [], [MbZ])
    twb = P.sb('twb', [128, TW], BF16); sgb = P.sb('sgb', [128, TW], BF16)

    def dbl(name, shape, dt=F32):
        return [P.sb(name, shape, dt) for _ in range(2)]
    vtok = dbl('rvtok', [C, NB, GW], BF16); khtok = dbl('khtok', [C, NB, GW], BF16); nbhtok = dbl('nbhtok', [C, NB, GW], BF16)
    N0 = dbl('rN0', [C, H2, C], NDT); NT0 = dbl('rNT0', [C, H2, C], NDT)
    AKm = dbl('AKm', [C, H2, C], BF16); RKm = dbl('RKm', [C, H2, C], BF16); RBm = dbl('RBm', [C, H2, C], BF16)
    nb = {'X': dbl('rnX', [C, H2, C], NDT), 'XT': dbl('rnXT', [C, H2, C], NDT), 'TT': dbl('rnTT', [C, H2, C], NDT)}
    if C == 128:
        sgl = lambda name: [P.sb(name, [C, H2, C], NDT)] * 2
        Nd = sgl('rNd'); NTd = sgl('rNTd'); No = sgl('rNo'); Td = sgl('rTd'); Y1 = sgl('rY1')
        bdm = P.sb('rbdm', [C, H2, C], NDT)
        for h in range(H2):
            P.dma('pool', bdm[:, h, :], cst['bd'], writes=[bdm], keep=True)
    TTb = dbl('rTTb', [C, H2, C], BF16); RHb = dbl('RHb', [C, H, NV], BF16); Ub = dbl('Ub', [C, H, NV], BF16)
    ysb = dbl('ysb', [C, H, NV]); ysq = dbl('ysq', [C, H, NV]); st = dbl('rst', [C, 4, H]); bon = dbl('bon', [C, H])
    yab = dbl('yab', [C, GW]); yas = dbl('yas', [128, NJ, C], BF16)
    if half:
        f_ = [P.ps('rpF', [128, 512]) for _ in range(3)]
        if pipe:
            psA = [f_[1], f_[2]]
            psN = [f_[0], f_[0], f_[0]]
            psC = [f_[0], f_[0]]
        else:
            psA = [f_[0], f_[1]]
            psN = [f_[0], f_[1], f_[2]]
            psC = [f_[2], f_[1]]
    else:
        psA = [P.ps('rpA', [128, 512]) for _ in range(2)]
        psN = [P.ps('rpN', [128, 512]) for _ in range(3)]
        psC = [P.ps('rpC', [128, 512]) for _ in range(2)]
    psB = P.ps('rpB', [128, 1024], BF16)
    NEG = -math.exp(-0.5)
    for t in range(L // TW):
        c0 = t * TW
        if t == 0:
            G_(lambda E: E.memset(X[:, :, 0:1], 0.0), [], [X])
            for i_, c_ in enumerate(cl):
                P.dma('sp', X[:, i_, 1:TW + 1], sc['hr'][c_ * 128:(c_ + 1) * 128, 0:TW], reads=[sc['hr']], writes=[X], keep=True)
        else:
            for i_, c_ in enumerate(cl):
                P.dma('sp', X[:, i_, :], sc['hr'][c_ * 128:(c_ + 1) * 128, c0 - 1:c0 + TW], reads=[sc['hr']], writes=[X], keep=(i_ > 0))
        G_(lambda E: E.tensor_tensor(Y[:, :, :], X[:, :, 0:TW], X[:, :, 1:TW + 1], ALU.subtract), [X], [Y])
        for c, cg in enumerate(cl):
            V_(lambda E, c=c, cg=cg: E.scalar_tensor_tensor(Y[:, c, :], Y[:, c, :], mu[:, cg:cg + 1], X[:, c, 1:TW + 1], ALU.mult, ALU.add),
               [Y, mu, X], [Y], keep=True)
        A_(lambda E: E.activation(out=twb[0:64, :], in_=Y[0:64, iW, :], func=AF.Tanh), [Y], [twb])
        A_(lambda E: E.copy(twb[64:128, :], Y[64:128, iW, :]), [Y], [twb], keep=True)
        A_(lambda E: E.activation(out=sgb[:, :], in_=Y[:, iG, :], func=AF.Sigmoid), [Y], [sgb])
        for jl, j in enumerate(js):
            pa = psA[jl % 2]
            T_(lambda E, j=j, pa=pa: E.matmul(pa[:, 0:TW], wup[0:64, j * 128:(j + 1) * 128], twb[0:64, :], start=True, stop=True), [wup, twb], [pa])
            A_(lambda E, j=j, jl=jl, pa=pa: E.activation(out=lw[:, jl, :], in_=pa[:, 0:TW], func=AF.Sigmoid, bias=pv[:, 0, j:j + 1]), [pa, pv], [lw], keep=(jl > 0))
            pb = psC[jl % 2]
            T_(lambda E, j=j, pb=pb: E.matmul(pb[:, 0:TW], wup[64:128, j * 128:(j + 1) * 128], twb[64:128, :], start=True, stop=True), [wup, twb], [pb])
            A_(lambda E, j=j, jl=jl, pb=pb: E.activation(out=a_[:, jl, :], in_=pb[:, 0:TW], func=AF.Sigmoid, bias=pv[:, 1, j:j + 1]), [pb, pv], [a_], keep=(jl > 0))
        A_(lambda E: E.mul(lw[:, :, :], lw[:, :, :], NEG), [lw], [lw])
        for j in range(NJ):
            V_(lambda E, j=j: E.tensor_tensor_scan(Gc[:, j, :], rm[:, :], lw[:, j, :], 0.0, ALU.mult, ALU.add), [rm, lw], [Gc], keep=(j > 0))
        A_(lambda E: E.activation(out=eP[:, :, :], in_=Gc[:, :, :], func=AF.Exp), [Gc], [eP])
        A_(lambda E: E.activation(out=eN[:, :, :], in_=Gc[:, :, :], func=AF.Exp, scale=-1.0), [Gc], [eN])
        V_(lambda E: E.tensor_tensor(t1[:, :, :], Gc[:, :, :], lw[:, :, :], ALU.subtract), [Gc, lw], [t1])
        A_(lambda E: E.activation(out=ePe[:, :, :], in_=t1[:, :, :], func=AF.Exp), [t1], [ePe])
        for j, jg in enumerate(js):
            A_(lambda E, j=j, jg=jg: E.activation(out=kk[:, j, :], in_=Y[:, iK + j, :], func=AF.Copy, scale=pv[:, 2, jg:jg + 1]), [Y, pv], [kk], keep=(j > 0))
        for j in range(NJ):
            G_(lambda E, j=j: E.tensor_tensor(t2[:, :], kk[:, j, :], kk[:, j, :], ALU.mult), [kk], [t2])
            pa = psA[j % 2]
            T_(lambda E, pa=pa: E.matmul(pa[:, 0:TW], bo[:, :], t2[:, :], start=True, stop=True), [bo, t2], [pa])
            A_(lambda E, j=j, pa=pa: E.activation(out=t1[:, j, :], in_=pa[:, 0:TW], func=AF.Sqrt, bias=epst[:, 0:1]), [pa, epst], [t1], keep=(j > 0))
        V_(lambda E: E.reciprocal(t1[:, :, :], t1[:, :, :]), [t1], [t1])
        V_(lambda E: E.tensor_tensor(kk[:, :, :], kk[:, :, :], t1[:, :, :], ALU.mult), [kk, t1], [kk])
        G_(lambda E: E.tensor_tensor(At[:, :, :], kk[:, :, :], ePe[:, :, :], ALU.mult), [kk, ePe], [At])
        V_(lambda E: E.tensor_tensor(kk[:, :, :], kk[:, :, :], a_[:, :, :], ALU.mult), [kk, a_], [kk])
        V_(lambda E: E.tensor_tensor(t1[:, :, :], kk[:, :, :], eN[:, :, :], ALU.mult), [kk, eN], [t1])
        A_(lambda E: E.copy(Bt[:, :, :], t1[:, :, :]), [t1], [Bt])
        V_(lambda E: E.tensor_tensor(Bh[:, :, :].rearrange("p j (c t) -> p j c t", t=C), t1[:, :, :].rearrange("p j (c t) -> p j c t", t=C),
                                     eP[:, :, :].rearrange("p j (c t) -> p j c t", t=C)[:, :, :, C - 1:C].to_broadcast([128, NJ, TW // C, C]), ALU.mult),
           [t1, eP], [Bh])
        for j, jg in enumerate(js):
            A_(lambda E, j=j, jg=jg: E.activation(out=kk[:, j, :], in_=a_[:, j, :], func=AF.Identity, scale=pv[:, 3, jg:jg + 1], bias=pv[:, 5, jg:jg + 1]), [a_, pv], [kk], keep=(j > 0))
        V_(lambda E: E.tensor_tensor(kk[:, :, :], kk[:, :, :], Y[:, iK:iK + NJ, :], ALU.mult), [kk, Y], [kk])
        V_(lambda E: E.tensor_tensor(t1[:, :, :], kk[:, :, :], eN[:, :, :], ALU.mult), [kk, eN], [t1])
        A_(lambda E: E.copy(Kt[:, :, :], t1[:, :, :]), [t1], [Kt])
        V_(lambda E: E.tensor_tensor(Kh[:, :, :].rearrange("p j (c t) -> p j c t", t=C), t1[:, :, :].rearrange("p j (c t) -> p j c t", t=C),
                                     eP[:, :, :].rearrange("p j (c t) -> p j c t", t=C)[:, :, :, C - 1:C].to_broadcast([128, NJ, TW // C, C]), ALU.mult),
           [t1, eP], [Kh])
        G_(lambda E: E.tensor_tensor(Rt[:, :, :], Y[:, iR:iR + NJ, :], eP[:, :, :], ALU.mult), [Y, eP], [Rt])
        V_(lambda E: E.tensor_tensor(kk[:, :, :], kk[:, :, :], Y[:, iR:iR + NJ, :], ALU.mult), [kk, Y], [kk])
        for j, jg in enumerate(js):
            A_(lambda E, j=j, jg=jg: E.activation(out=rkb[:, j, :], in_=kk[:, j, :], func=AF.Copy, scale=pv[:, 4, jg:jg + 1]), [kk, pv], [rkb], keep=(j > 0))
        A_(lambda E: E.copy(vT[:, :, :], Y[:, iV:iV + NJ, :]), [Y], [vT])
        for e_ in range(2):
            G_(lambda E, e_=e_: E.tensor_scalar(AtZ[:, :, e_, :], At[:, :, :], hsf[:, e_:e_ + 1], None, ALU.mult), [At, hsf], [AtZ], keep=(e_ > 0))
            G_(lambda E, e_=e_: E.tensor_scalar(BtZ[:, :, e_, :], Bt[:, :, :], hsf[:, e_:e_ + 1], None, ALU.mult), [Bt, hsf], [BtZ], keep=(e_ > 0))
            A_(lambda E, e_=e_: E.activation(out=RtZ[:, :, e_, :], in_=Rt[:, :, :], func=AF.Copy, scale=hsf[:, e_:e_ + 1]), [Rt, hsf], [RtZ], keep=(e_ > 0))
        def part1(cp):
            p = cp % 2
            css = [slice((cp * NB + k) * C, (cp * NB + k + 1) * C) for k in range(NB)]
            for src, dst, scl in ((vT, vtok[p], 1.0), (Kh, khtok[p], 1.0), (Bh, nbhtok[p], -1.0)):
                for k in range(NB):
                    for j in range(NJ):
                        T_(lambda E, j=j, k=k, src=src: E.transpose(psB[0:C, (k * NJ + j) * 128:(k * NJ + j + 1) * 128], src[:, j, css[k]], cx.ident[:, :]),
                           [src, cx.ident], [psB], inc=(k == NB - 1 and j == NJ - 1))
                A_(lambda E, dst=dst, scl=scl: E.mul(dst[:, :, :].rearrange("p k g -> p (k g)"), psB[0:C, 0:NB * GW], scl), [psB], [dst])
            ops5 = ((Bt, AtZ, 'nsu', NT0[p]), (At, BtZ, 'nsl', N0[p]), (Kt, AtZ, 'su', AKm[p]), (Kt, RtZ, 'iu', RKm[p]), (Bt, RtZ, 'niu', RBm[p]))
            for n_, (la, rb, mk, dst) in enumerate(ops5):
                pp = psC[n_ % 2]
                for k in range(NB):
                    for j in range(NJ):
                        v0 = k * H + 2 * j
                        T_(lambda E, v0=v0, k=k, j=j, la=la, rb=rb, pp=pp: E.matmul(pp[0:C, v0 * C:(v0 + 2) * C], la[:, j, css[k]], rb[:, j, :, css[k]],
                                                                                 start=True, stop=True), [la, rb], [pp], inc=(k == NB - 1 and j == NJ - 1))
                V_(lambda E, dst=dst, pp=pp, mk=mk: E.tensor_tensor(fl(dst), pp[0:C, 0:H2 * C], fl(msk[mk]), ALU.mult), [pp, msk[mk]], [dst])
            if C == 128:
                G_(lambda E: E.tensor_tensor(Nd[p][:, :, :], N0[p][:, :, :], bdm[:, :, :], ALU.mult), [N0[p], bdm], [Nd[p]])
                G_(lambda E: E.tensor_tensor(NTd[p][:, :, :], NT0[p][:, :, :], bdm[:, :, :], ALU.mult), [NT0[p], bdm], [NTd[p]])
                V_(lambda E: E.tensor_tensor(No[p][:, :, :], N0[p][:, :, :], Nd[p][:, :, :], ALU.subtract), [N0[p], Nd[p]], [No[p]])
                TTd = neumann_TT(P, cx, Nd[p], NTd[p], nb, psN, H2, C, nstage=5)
                for h in range(H2):
                    T_(lambda E, h=h: E.transpose(psB[0:C, h * C:(h + 1) * C], TTd[:, h, :], cx.ident[0:C, 0:C]),
                       [TTd, cx.ident], [psB], inc=(h == H2 - 1))
                A_(lambda E: E.copy(fl(Td[p]), psB[0:C, 0:H2 * C]), [psB], [Td[p]])
                py1 = psN[0]
                for h in range(H2):
                    T_(lambda E, h=h: E.matmul(py1[0:C, h * C:(h + 1) * C], No[p][:, h, :], TTd[:, h, :], start=True, stop=True),
                       [No[p], TTd], [py1], inc=(h == H2 - 1))
                V_(lambda E: E.tensor_copy(fl(Y1[p]), py1[0:C, 0:H2 * C]), [py1], [Y1[p]])
                pc2 = psN[1]
                for h in range(H2):
                    T_(lambda E, h=h: E.matmul(pc2[0:C, h * C:(h + 1) * C], Td[p][:, h, :], Y1[p][:, h, :], start=True, stop=True),
                       [Td[p], Y1[p]], [pc2], inc=(h == H2 - 1))
                V_(lambda E: E.tensor_tensor(fl(TTb[p]), pc2[0:C, 0:H2 * C], fl(TTd), ALU.add), [pc2, TTd], [TTb[p]])
            else:
                TT = neumann_TT(P, cx, N0[p], NT0[p], nb, psN, H2, C)
                A_(lambda E, TT=TT: E.copy(TTb[p][:, :, :], TT[:, :, :]), [TT], [TTb[p]])

        def part2(cp, k):
            p = cp % 2
            ci = cp * NB + k
            q = ci % 2
            cs = slice(ci * C, (ci + 1) * C)
            pr = psA[0]
            for j in range(NJ):
                T_(lambda E, j=j: E.matmul(pr[0:C, 2 * j * NV:(2 * j + 2) * NV], At[:, j, cs], MbZ[:, j, :, :], start=True, stop=False, skip_group_check=True),
                   [At, MbZ], [pr], inc=False)
                for hd in (2 * j, 2 * j + 1):
                    T_(lambda E, hd=hd: E.matmul(pr[0:C, hd * NV:(hd + 1) * NV], AKm[p][:, k * H + hd, :], vtok[p][:, k, hd * NV:(hd + 1) * NV], start=False, stop=True,
                                                 skip_group_check=True), [AKm[p], vtok[p]], [pr], inc=(hd == H - 1))
            A_(lambda E: E.copy(fl(RHb[q]), pr[0:C, 0:H * NV]), [pr], [RHb[q]])
            pu = psA[1]
            for hd in range(H):
                T_(lambda E, hd=hd: E.matmul(pu[0:C, hd * NV:(hd + 1) * NV], TTb[p][:, k * H + hd, :], RHb[q][:, hd, :], start=True, stop=True),
                   [TTb[p], RHb[q]], [pu], inc=(hd == H - 1))
            A_(lambda E: E.copy(fl(Ub[q]), pu[0:C, 0:H * NV]), [pu], [Ub[q]])
            py = psA[0]
            for j in range(NJ):
                T_(lambda E, j=j: E.matmul(py[0:C, 2 * j * NV:(2 * j + 2) * NV], Rt[:, j, cs], MbZ[:, j, :, :], start=True, stop=False, skip_group_check=True),
                   [Rt, MbZ], [py], inc=False)
                for hd in (2 * j, 2 * j + 1):
                    T_(lambda E, hd=hd: E.matmul(py[0:C, hd * NV:(hd + 1) * NV], RKm[p][:, k * H + hd, :], vtok[p][:, k, hd * NV:(hd + 1) * NV], start=False, stop=False,
                                                 skip_group_check=True), [RKm[p], vtok[p]], [py], inc=False)
                    T_(lambda E, hd=hd: E.matmul(py[0:C, hd * NV:(hd + 1) * NV], RBm[p][:, k * H + hd, :], Ub[q][:, hd, :], start=False, stop=True,
                                                 skip_group_check=True), [RBm[p], Ub[q]], [py], inc=(hd == H - 1))
            A_(lambda E: E.copy(fl(ysb[q]), py[0:C, 0:H * NV]), [py], [ysb[q]])
            pm = psA[0] if pipe else psC[0]
            for j in range(NJ):
                T_(lambda E, j=j: E.matmul(pm[:, j * 128:(j + 1) * 128], khtok[p][:, k, j * 128:(j + 1) * 128], vtok[p][:, k, j * 128:(j + 1) * 128],
                                           start=True, stop=False), [khtok[p], vtok[p]], [pm], inc=False)
                T_(lambda E, j=j: E.matmul(pm[:, j * 128:(j + 1) * 128], nbhtok[p][:, k, j * 128:(j + 1) * 128], fl(Ub[q])[:, j * 128:(j + 1) * 128],
                                           start=False, stop=True), [nbhtok[p], Ub[q]], [pm], inc=(j == NJ - 1))
            for j in range(NJ):
                for po in (0, 64):
                    V_(lambda E, j=j, po=po: E.scalar_tensor_tensor(M[po:po + 64, j, :], M[po:po + 64, j, :], eP[po:po + 64, j, ci * C + C - 1:ci * C + C],
                                                                     pm[po:po + 64, j * 128 + po:j * 128 + po + 64], ALU.mult, ALU.add),
                       [M, eP, pm], [M], keep=not (j == 0 and po == 0))
            A_(lambda E: E.copy(MbZ[0:64, :, 0, :], M[0:64, :, :]), [M], [MbZ])
            A_(lambda E: E.copy(MbZ[64:128, :, 1, :], M[64:128, :, :]), [M], [MbZ], keep=True)
            pbn = psA[1] if pipe else psC[1]
            for j in range(NJ):
                T_(lambda E, j=j: E.matmul(pbn[0:C, j * 2:(j + 1) * 2], rkb[:, j, cs], hsf[:, :], start=True, stop=True), [rkb, hsf], [pbn], inc=(j == NJ - 1))
            A_(lambda E: E.copy(bon[q][:, :], pbn[0:C, 0:H]), [pbn], [bon[q]])
            pg = psA[1]
            T_(lambda E: E.matmul(pg[0:C, 0:GW], sgb[:, cs], gup[:, g0:g1], start=True, stop=True), [sgb, gup], [pg])
            s_ = st[q]
            y_ = ysb[q]
            V_(lambda E: E.tensor_reduce(s_[:, 0, :], y_[:, :, :], AX.X, ALU.add), [y_], [s_])
            V_(lambda E: E.tensor_scalar(s_[:, 0, :], s_[:, 0, :], 1.0 / 64, None, ALU.mult), [s_], [s_])
            V_(lambda E: E.tensor_tensor(y_[:, :, :], y_[:, :, :], s_[:, 0, :].unsqueeze(2).to_broadcast([C, H, NV]), ALU.subtract), [y_, s_], [y_])
            G_(lambda E: E.tensor_tensor(ysq[q][:, :, :], y_[:, :, :], y_[:, :, :], ALU.mult), [y_], [ysq[q]])
            V_(lambda E: E.tensor_reduce(s_[:, 1, :], ysq[q][:, :, :], AX.X, ALU.add), [ysq[q]], [s_], keep=True)
            V_(lambda E: E.tensor_scalar(s_[:, 1, :], s_[:, 1, :], 1.0 / 64, 64e-5, ALU.mult, ALU.add), [s_], [s_])
            A_(lambda E: E.activation(out=s_[:, 1, :], in_=s_[:, 1, :], func=AF.Sqrt), [s_], [s_])
            V_(lambda E: E.reciprocal(s_[:, 1, :], s_[:, 1, :]), [s_], [s_])
            V_(lambda E: E.tensor_tensor(y_[:, :, :], y_[:, :, :], s_[:, 1, :].unsqueeze(2).to_broadcast([C, H, NV]), ALU.mult), [y_, s_], [y_])
            V_(lambda E: E.tensor_tensor(fl(y_), fl(y_), lng[:, :], ALU.mult), [y_, lng], [y_])
            G_(lambda E: E.tensor_tensor(fl(y_), fl(y_), lnb[:, :], ALU.add), [y_, lnb], [y_])
            V_(lambda E: E.tensor_tensor(ysq[q][:, :, :], vtok[p][:, k, :].rearrange("p (h c) -> p h c", c=NV),
                                         bon[q][:, :].unsqueeze(2).to_broadcast([C, H, NV]), ALU.mult), [vtok[p], bon[q]], [ysq[q]])
            G_(lambda E: E.tensor_tensor(y_[:, :, :], y_[:, :, :], ysq[q][:, :, :], ALU.add), [y_, ysq[q]], [y_])
            V_(lambda E: E.tensor_tensor(yab[q][:, :], fl(y_), pg[0:C, 0:GW], ALU.mult), [y_, pg], [yab[q]])
            for j in range(NJ):
                T_(lambda E, j=j: E.transpose(psA[0][:, j * C:(j + 1) * C], yab[q][:, j * 128:(j + 1) * 128], cx.identf[0:C, 0:C]),
                   [yab[q], cx.identf], [psA[0]], inc=(j == NJ - 1))
            A_(lambda E: E.copy(fl(yas[q]), psA[0][:, 0:NJ * C]), [psA[0]], [yas[q]])
            P.dma('sp', sc['yaT'][g0:g1, c0 + ci * C:c0 + (ci + 1) * C].rearrange("(h p) t -> p h t", p=128), yas[q][:, :, :],
                  reads=[yas[q]], writes=[sc['yaT']], keep=True)

        NCt = TW // C
        for cp in range(NCt // NB):
            part1(cp)
            for k in range(NB):
                part2(cp, k)


def merge_phase(P, cx, sc, x_dram, out_dram, wbr_ap, wout_ap, g_ap, b_ap, L):
    Wb = load_w_bf16(P, 'wbr', wbr_ap, 1536, D_MODEL)
    Wo = load_w_bf16(P, 'wo', wout_ap, D_MODEL, D_MODEL)
    gb = P.sb('lng2', [128, D_MODEL]); bb = P.sb('lnb2', [128, D_MODEL])
    P.dma('sp', gb[:, :], g_ap.partition_broadcast(128), writes=[gb])
    P.dma('sp', bb[:, :], b_ap.partition_broadcast(128), writes=[bb])
    n = L // 128
    NS = 4
    P.interleave([(lambda k=k: merge_tiles(P, cx, sc, x_dram, out_dram, Wb, Wo, gb, bb, range(k, n, NS))) for k in range(NS)])


def merge_tiles(P, cx, sc, x_dram, out_dram, Wb, Wo, gb, bb, tiles):
    yT = [[P.sb('myT', [128, 4, 128], BF16) for _ in range(3)] for _ in range(1)]
    gt = [P.sb('mgt', [128, 3072]) for _ in range(1)]
    xi = [P.sb('mxi', [128, D_MODEL]) for _ in range(1)]
    mm = [P.sb('mmm', [128, D_MODEL]) for _ in range(1)]
    tmp = [P.sb('mtmp', [128, 512]) for _ in range(2)]
    mb = [P.sb('mmb', [128, D_MODEL], BF16) for _ in range(1)]
    mT = [P.sb('mmT', [128, 8, 128], BF16) for _ in range(1)]
    y = [P.sb('my', [128, D_MODEL]) for _ in range(1)]
    st = [P.sb('mst', [128, 2, 6]) for _ in range(1)]
    mv = [P.sb('mmv', [128, 2]) for _ in range(1)]
    rs = [P.sb('mrs', [128, 2]) for _ in range(1)]
    psA = [P.ps('mpA', [128, 512]) for _ in range(1)]
    psT = [P.ps('mpT', [128, 1024], BF16) for _ in range(1)]
    psO = [psA[0], psA[0]]
    eps = 1e-5 / (ALPHA * ALPHA)
    names = ('yaT', 'ybT', 'ycT')
    na = 0
    for it_, i in enumerate(tiles):
        p = 0
        r0 = i * 128
        for n in range(3):
            P.dma('sp', yT[p][n][:, :, :], sc[names[n]][:, r0:r0 + 128].rearrange("(c p) t -> p c t", p=128),
                  reads=[sc[names[n]]], writes=[yT[p][n]])
        P.dma('sp', gt[p][:, :], sc['gates'][r0:r0 + 128, :], reads=[sc['gates']], writes=[gt[p]])
        P.dma('sp', xi[p][:, :], x_dram[r0:r0 + 128, :], reads=[x_dram], writes=[xi[p]])
        for half in range(2):
            hs = slice(half * 512, (half + 1) * 512)
            for n in range(3):
                ps = psA[0]
                na += 1
                for c in range(4):
                    P.op('pe', lambda E, c=c: E.matmul(ps[:, :], yT[p][n][:, c, :], Wb[:, n * 4 + c, hs], start=(c == 0), stop=(c == 3)),
                         reads=[yT[p][n], Wb], writes=[ps], inc=(c == 3))
                gs = gt[p][:, n * 1024 + half * 512:n * 1024 + (half + 1) * 512]
                if n == 0:
                    P.op('dve', lambda E: E.tensor_tensor(mm[p][:, hs], ps[:, :], gs, ALU.mult), reads=[ps, gt[p]], writes=[mm[p]], keep=(half == 1))
                else:
                    tp = tmp[n % 2]
                    P.op('dve', lambda E: E.tensor_tensor(tp[:, :], ps[:, :], gs, ALU.mult), reads=[ps, gt[p]], writes=[tp])
                    P.op('pool', lambda E: E.tensor_tensor(mm[p][:, hs], mm[p][:, hs], tp[:, :], ALU.add), reads=[mm[p], tp], writes=[mm[p]], keep=True)
        P.op('act', lambda E: E.copy(mb[p][:, :], mm[p][:, :]), reads=[mm[p]], writes=[mb[p]])
        pt = psT[0]
        for kc in range(8):
            P.op('pe', lambda E, kc=kc: E.transpose(pt[:, kc * 128:(kc + 1) * 128], mb[p][:, kc * 128:(kc + 1) * 128], cx.ident[:, :]),
                 reads=[mb[p], cx.ident], writes=[pt], inc=(kc == 7))
        P.op('dve', lambda E: E.tensor_copy(mT[p][:, :, :].rearrange("p k t -> p (k t)"), pt[:, :]), reads=[pt], writes=[mT[p]])
        for half in range(2):
            po = psO[half]
            for kc in range(8):
                P.op('pe', lambda E, kc=kc: E.matmul(po[:, :], mT[p][:, kc, :], Wo[:, kc, half * 512:(half + 1) * 512], start=(kc == 0), stop=(kc == 7)),
                     reads=[mT[p], Wo], writes=[po], inc=(kc == 7))
            P.op('dve', lambda E: E.scalar_tensor_tensor(y[p][:, half * 512:(half + 1) * 512], po[:, :], 1.0 / ALPHA,
                                                         xi[p][:, half * 512:(half + 1) * 512], ALU.mult, ALU.add),
                 reads=[po, xi[p]], writes=[y[p]], keep=(half == 1))
        layernorm_out(P, y[p], gb, bb, out_dram, r0, eps, st[p], mv[p], rs[p])


STOP_AFTER = 1000
RW_C = 128
GD_C = 128
NDT = BF16
PARAM_NAMES = ["ffn1_w_in", "ffn1_w_out", "ln1_g", "ln1_b", "mix_w_in", "rw_shift_mu", "rw_w0", "rw_w_up", "rw_a0", "rw_a_up",
               "rw_g_up", "rw_k_k", "rw_k_a", "rw_r_k", "rw_ln_g", "rw_ln_b", "da_lam_q1", "da_lam_k1", "da_lam_q2", "da_lam_k2",
               "da_norm_g", "gd_conv_w", "gd_a_log", "gd_dt_bias", "gd_norm_g", "mix_w_branch", "mix_w_out", "ln2_g", "ln2_b",
               "ffn2_w_in", "ffn2_w_out", "ln3_g", "ln3_b"]


def host_consts():
    c = {'ident': np.eye(128, dtype=np.float32)}
    ab, cm = da_consts()
    c['abias'] = ab
    c['cmask'] = cm
    tl, nm, st = tri_consts(GD_C)
    c['tri_le'] = tl
    c['negmask'] = nm
    c['strict'] = st
    bd = np.zeros((128, 128), np.float32)
    bd[:64, :64] = 1
    bd[64:, 64:] = 1
    c['bd'] = bd
    for n, v in rw_consts(RW_C).items():
        c['rw_' + n] = v
    return c


def build_model(L, shapes, layers=(0, 1), phases=None):
    nc = bass.Bass("TRN2", target_bir_lowering=False)
    P = Prog(nc)
    x = P.dram('x', [L, D_MODEL], kind='ExternalInput')
    out = P.dram('out', [L, D_MODEL], kind='ExternalOutput')
    prm = {n: nc.dram_tensor(n, list(shapes[n]), F32, kind='ExternalInput').ap() for n in PARAM_NAMES}
    hc = host_consts()
    cst = {n: nc.dram_tensor('c_' + n, list(v.shape), F32, kind='ExternalInput').ap() for n, v in hc.items()}
    cx = Ctx(P, cst['ident'])
    xs = [P.dram('xs%d' % i, [L, D_MODEL]) for i in range(2)]
    sc = mix_scratch(P, L)
    cur = x
    nl = len(layers)
    nph = [0]

    def _go():
        nph[0] += 1
        return nph[0] <= STOP_AFTER
    for li, l in enumerate(layers):
        a, b = xs[0], xs[1]
        if _go():
            P.phase_begin()
            ffn_phase(P, cx, cur, a, prm['ffn1_w_in'][l], prm['ffn1_w_out'][l], prm['ln1_g'][l], prm['ln1_b'][l], L)
            P.phase_end()
        if _go():
            P.phase_begin()
            mixproj_phase(P, cx, a, prm['mix_w_in'][l], sc, L)
            P.phase_end()
        if _go():
            P.phase_begin()
            rw_w = {'mu': prm['rw_shift_mu'][l], 'w0': prm['rw_w0'][l], 'w_up': prm['rw_w_up'][l], 'a0': prm['rw_a0'][l],
                    'a_up': prm['rw_a_up'][l], 'g_up': prm['rw_g_up'][l], 'k_k': prm['rw_k_k'][l], 'k_a': prm['rw_k_a'][l],
                    'r_k': prm['rw_r_k'][l].rearrange("h n -> (h n)"), 'ln_g': prm['rw_ln_g'][l], 'ln_b': prm['rw_ln_b'][l]}
            rw_c = {n: cst['rw_' + n] for n in ('nsu', 'su', 'iu', 'niu', 'nsl', 'bo', 'hs', 'rm')}
            rw_c['bd'] = cst['bd']
            P.interleave([lambda: rwkv_phase(P, cx, sc, rw_w, rw_c, L, js=(0, 1), half=True, C=RW_C),
                          lambda: rwkv_phase(P, cx, sc, rw_w, rw_c, L, js=(2, 3), half=True, C=RW_C)])
            P.phase_end()
        if _go():
            P.phase_begin()
            lam_init = 0.8 - 0.6 * math.exp(-0.3 * l)
            diffattn_phase(P, cx, sc, [prm[n][l] for n in ('da_lam_q1', 'da_lam_k1', 'da_lam_q2', 'da_lam_k2')], prm['da_norm_g'][l],
                           lam_init, cst['abias'], cst['cmask'], L)
            P.phase_end()
        if _go():
            P.phase_begin()
            gd_a = (prm['gd_conv_w'][l], prm['gd_a_log'][l], prm['gd_dt_bias'][l], prm['gd_norm_g'][l],
                    {n: cst[n] for n in ('tri_le', 'negmask', 'strict', 'bd')}, L)
            P.interleave([lambda: gdn_phase(P, cx, sc, *gd_a, hs=(0, 1), half=True, C=GD_C),
                          lambda: gdn_phase(P, cx, sc, *gd_a, hs=(2, 3), half=True, C=GD_C)])
            P.phase_end()
        if _go():
            P.phase_begin()
            merge_phase(P, cx, sc, a, b, prm['mix_w_branch'][l].rearrange("n c d -> (n c) d"), prm['mix_w_out'][l], prm['ln2_g'][l], prm['ln2_b'][l], L)
            P.phase_end()
        dst = out if li == nl - 1 else a
        if _go():
            P.phase_begin()
            ffn_phase(P, cx, b, dst, prm['ffn2_w_in'][l], prm['ffn2_w_out'][l], prm['ln3_g'][l], prm['ln3_b'][l], L)
            P.phase_end()
        cur = a
    P.finish([out])
    return P.build(), hc, P


_CACHE = {}


def kernel(**inputs):
    x = np.asarray(inputs['x'], dtype=np.float32)
    B, L, D = x.shape
    shapes = {n: np.asarray(inputs[n]).shape for n in PARAM_NAMES}
    nc, hc, _ = build_model(L, shapes)
    base = {n: np.ascontiguousarray(np.asarray(inputs[n], dtype=np.float32)) for n in PARAM_NAMES}
    for n, v in hc.items():
        base['c_' + n] = v
    in_maps = []
    for b in range(B):
        m = dict(base)
        m['x'] = np.ascontiguousarray(x[b])
        in_maps.append(m)
    res = run_bass_kernel_spmd(nc, in_maps, core_ids=list(range(B)))
    return np.stack([np.asarray(r['out'], dtype=np.float32) for r in res.results], axis=0)
```
